# Optimizing a Trainium2 kernel written in Bass

```python
import math, functools
import jax, jax.numpy as jnp
from jax import lax
import numpy as np


D_MODEL = 1024
BATCH = 8
SEQ = 2048
DEPTH = 1
DEC_BATCH = 128
DEC_SEQ = 1
PAST_LEN = 16384
PAGE_SIZE = 128

N_META = 16
D_MIX = D_MODEL
HG_WIDTH = D_MIX // 2
HG_HEAD_DIM = 128
HG_HEADS = HG_WIDTH // HG_HEAD_DIM
HG_CHUNK = 64
SSM_WIDTH = D_MIX - HG_WIDTH
SSM_HEAD_DIM = 64
SSM_HEADS = SSM_WIDTH // SSM_HEAD_DIM
SSM_GROUPS = 2
SSM_STATE = 128
SSM_CHUNK = 128
CONV_WIDTH = 4
CONV_DIM = SSM_WIDTH + 2 * SSM_GROUPS * SSM_STATE
D_FF = 2816
D_IN_PROJ = 4 * HG_WIDTH + SSM_WIDTH + CONV_DIM + SSM_HEADS
SPLITS = (HG_WIDTH, 2 * HG_WIDTH, 3 * HG_WIDTH, 4 * HG_WIDTH,
          4 * HG_WIDTH + SSM_WIDTH, 4 * HG_WIDTH + SSM_WIDTH + CONV_DIM)
EPS = 1e-6

kernel_name = 'hymba_hgrn2_mamba2_macaron_step'


def rmsnorm(x, w):
    xf = x.astype(jnp.float32)
    y = xf * lax.rsqrt(jnp.mean(xf * xf, axis=-1, keepdims=True) + EPS)
    return (y * w.astype(jnp.float32)).astype(x.dtype)


def swiglu(x, wg, wu, wd):
    return (jax.nn.silu(x @ wg) * (x @ wu)) @ wd


def forget_lower_bound(lb_logits, layer):
    return jnp.cumsum(jax.nn.softmax(lb_logits.astype(jnp.float32), axis=0), axis=0)[layer]


def hgrn2_chunked(q, k, v, logf, s0, chunk):
    bsz, seqlen, nh, _ = q.shape
    dv = v.shape[-1]
    n = seqlen // chunk

    def to_chunks(t):
        return jnp.moveaxis(t.reshape((bsz, n, chunk) + t.shape[2:]), 1, 0)

    causal = jnp.tril(jnp.ones((chunk, chunk), dtype=bool))[None, :, :, None, None]

    def step(s, inp):
        qc, kc, vc, lc = inp
        b = jnp.cumsum(lc, axis=1)
        decay = jnp.exp(jnp.where(causal, b[:, :, None] - b[:, None, :], -jnp.inf))
        scores = jnp.einsum('bthk,bshk,btshk->bhts', qc, kc, decay)
        o = (jnp.einsum('bhts,bshv->bthv', scores, vc)
             + jnp.einsum('bthk,bhkv->bthv', qc * jnp.exp(b), s))
        b_last = b[:, -1]
        s = (s * jnp.exp(b_last)[..., None]
             + jnp.einsum('bshk,bshv->bhkv', kc * jnp.exp(b_last[:, None] - b), vc))
        return s, o

    s, o = lax.scan(step, s0, (to_chunks(q), to_chunks(k), to_chunks(v), to_chunks(logf)))
    return jnp.moveaxis(o, 0, 1).reshape(bsz, seqlen, nh, dv), s


def ssd_chunked(x, dt, bm, cm, h0, chunk, a_neg):
    bsz, seqlen, nh, hp = x.shape
    ng, ns = bm.shape[2], bm.shape[3]
    rep = nh // ng
    n = seqlen // chunk

    def to_chunks(t):
        return jnp.moveaxis(t.reshape((bsz, n, chunk) + t.shape[2:]), 1, 0)

    causal = jnp.tril(jnp.ones((chunk, chunk), dtype=bool))
    a_g = a_neg.reshape(ng, rep)

    def step(h, inp):
        xc, dtc, bc, cc = inp
        cum = jnp.cumsum(dtc * a_g, axis=1)
        cum_t = jnp.moveaxis(cum, 1, -1)
        seg = jnp.exp(jnp.where(causal, cum_t[..., :, None] - cum_t[..., None, :], -jnp.inf))
        cb = jnp.einsum('btgn,bsgn->bgts', cc, bc)
        y = jnp.einsum('bgts,bgrts,bsgr,bsgrp->btgrp', cb, seg, dtc, xc)
        y = y + jnp.einsum('btgn,bgrpn,btgr->btgrp', cc, h, jnp.exp(cum))
        last = cum[:, -1]
        h = (h * jnp.exp(last)[..., None, None]
             + jnp.einsum('bsgn,bsgr,bsgrp->bgrpn', bc, dtc * jnp.exp(last[:, None] - cum), xc))
        return h, y

    xg = x.reshape(bsz, seqlen, ng, rep, hp)
    dtg = dt.reshape(bsz, seqlen, ng, rep)
    h, y = lax.scan(step, h0.reshape(bsz, ng, rep, hp, ns),
                    (to_chunks(xg), to_chunks(dtg), to_chunks(bm), to_chunks(cm)))
    return jnp.moveaxis(y, 0, 1).reshape(bsz, seqlen, nh, hp), h.reshape(bsz, nh, hp, ns)


def run_segments(scan_fn, chunk, segs, state, *seqs):
    outs = []
    off = 0
    for n in segs:
        o, state = scan_fn(*[s[:, off:off + n] for s in seqs], state, math.gcd(n, chunk))
        outs.append(o)
        off += n
    return jnp.concatenate(outs, axis=1), state


def causal_conv(xbc, buf, w, b):
    xp = jnp.concatenate([buf, xbc], axis=1)
    seqlen = xbc.shape[1]
    out = b + xp[:, 0:seqlen] * w[0]
    for j in range(1, CONV_WIDTH):
        out = out + xp[:, j:j + seqlen] * w[j]
    return jax.nn.silu(out), xp[:, xp.shape[1] - (CONV_WIDTH - 1):]


def decoder_layer(h, segs, s_hg, s_ssm, s_conv, lb, n1, w1g, w1u, w1d, nm, w_in, hg_norm,
                  conv_w, conv_b, dt_bias, a_log, d_skip, ssm_norm, w_out, n2, w2g, w2u, w2d):
    f32 = jnp.float32
    bsz, seqlen, _ = h.shape
    h = h + 0.5 * swiglu(rmsnorm(h, n1), w1g, w1u, w1d)
    u = rmsnorm(h, nm) @ w_in
    q, fz, iv, g, z, xbc, dt_raw = jnp.split(u, SPLITS, axis=-1)

    fz = fz.astype(f32)
    logf = jnp.log(lb + (1.0 - lb) * jax.nn.sigmoid(fz))
    kk = (1.0 - lb) * jax.nn.sigmoid(-fz)
    hg_shape = (bsz, seqlen, HG_HEADS, HG_HEAD_DIM)
    o_hg, s_hg = run_segments(hgrn2_chunked, HG_CHUNK, segs, s_hg.astype(f32),
                              jax.nn.silu(q.astype(f32)).reshape(hg_shape), kk.reshape(hg_shape),
                              iv.astype(f32).reshape(hg_shape), logf.reshape(hg_shape))
    o_hg = o_hg * lax.rsqrt(jnp.mean(o_hg * o_hg, axis=-1, keepdims=True) + EPS)
    o_hg = o_hg.reshape(bsz, seqlen, HG_WIDTH) * hg_norm.astype(f32) * jax.nn.silu(g.astype(f32))

    xbc_act, conv_buf = causal_conv(xbc.astype(f32), s_conv.astype(f32), conv_w.astype(f32), conv_b.astype(f32))
    xs, bm, cm = jnp.split(xbc_act, (SSM_WIDTH, SSM_WIDTH + SSM_GROUPS * SSM_STATE), axis=-1)
    xs = xs.reshape(bsz, seqlen, SSM_HEADS, SSM_HEAD_DIM)
    bm = bm.reshape(bsz, seqlen, SSM_GROUPS, SSM_STATE)
    cm = cm.reshape(bsz, seqlen, SSM_GROUPS, SSM_STATE)
    dt = jax.nn.softplus(dt_raw.astype(f32) + dt_bias.astype(f32))
    a_neg = -jnp.exp(a_log.astype(f32))
    y, s_ssm = run_segments(functools.partial(ssd_chunked, a_neg=a_neg), SSM_CHUNK, segs,
                            s_ssm.astype(f32), xs, dt, bm, cm)
    y = y + d_skip.astype(f32)[:, None] * xs
    yz = (y.reshape(bsz, seqlen, SSM_WIDTH) * jax.nn.silu(z.astype(f32))).reshape(bsz, seqlen, SSM_GROUPS, -1)
    yz = yz * lax.rsqrt(jnp.mean(yz * yz, axis=-1, keepdims=True) + EPS)
    yz = yz.reshape(bsz, seqlen, SSM_WIDTH) * ssm_norm.astype(f32)

    mix = jnp.concatenate([o_hg, yz], axis=-1).astype(h.dtype) @ w_out
    h = h + mix
    h = h + 0.5 * swiglu(rmsnorm(h, n2), w2g, w2u, w2d)
    return h, s_hg, s_ssm, conv_buf


def setup_inputs(seed: int = 0) -> dict:
    key = jax.random.key(seed)
    ks = jax.random.split(key, 32)
    f32 = jnp.float32

    def nrm(k, shape, scale):
        return scale * jax.random.normal(k, shape, f32)

    L = DEPTH
    dt0 = jnp.exp(jax.random.uniform(ks[20], (L, SSM_HEADS), f32, math.log(1e-3), math.log(1e-1)))
    return {
        'x_prompt': nrm(ks[0], (BATCH, SEQ, D_MODEL), 1.0),
        'x_sample': nrm(ks[1], (DEC_BATCH, DEC_SEQ, D_MODEL), 1.0),
        'state_hgrn': nrm(ks[2], (L, DEC_BATCH, HG_HEADS, HG_HEAD_DIM, HG_HEAD_DIM), 0.3),
        'state_ssm': nrm(ks[3], (L, DEC_BATCH, SSM_HEADS, SSM_HEAD_DIM, SSM_STATE), 0.1),
        'state_conv': nrm(ks[4], (L, DEC_BATCH, CONV_WIDTH - 1, CONV_DIM), 1.0),
        'meta_tokens': nrm(ks[5], (N_META, D_MODEL), 1.0),
        'lb_logits': nrm(ks[6], (L + 1, HG_WIDTH), 0.5),
        'norm_ffn1': 1.0 + nrm(ks[7], (L, D_MODEL), 0.02),
        'w_ffn1_gate': nrm(ks[8], (L, D_MODEL, D_FF), D_MODEL ** -0.5),
        'w_ffn1_up': nrm(ks[9], (L, D_MODEL, D_FF), D_MODEL ** -0.5),
        'w_ffn1_down': nrm(ks[10], (L, D_FF, D_MODEL), D_FF ** -0.5),
        'norm_mix': 1.0 + nrm(ks[11], (L, D_MODEL), 0.02),
        'w_in': nrm(ks[12], (L, D_MODEL, D_IN_PROJ), D_MODEL ** -0.5),
        'hg_norm': 1.0 + nrm(ks[13], (L, HG_WIDTH), 0.02),
        'conv_w': nrm(ks[14], (L, CONV_WIDTH, CONV_DIM), CONV_WIDTH ** -0.5),
        'conv_b': nrm(ks[15], (L, CONV_DIM), 0.01),
        'dt_bias': dt0 + jnp.log(-jnp.expm1(-dt0)),
        'a_log': jnp.log(jax.random.uniform(ks[21], (L, SSM_HEADS), f32, 1.0, 16.0)),
        'd_skip': 1.0 + nrm(ks[16], (L, SSM_HEADS), 0.1),
        'ssm_norm': 1.0 + nrm(ks[17], (L, SSM_WIDTH), 0.02),
        'w_out': nrm(ks[18], (L, D_MIX, D_MODEL), D_MIX ** -0.5),
        'norm_ffn2': 1.0 + nrm(ks[19], (L, D_MODEL), 0.02),
        'w_ffn2_gate': nrm(ks[22], (L, D_MODEL, D_FF), D_MODEL ** -0.5),
        'w_ffn2_up': nrm(ks[23], (L, D_MODEL, D_FF), D_MODEL ** -0.5),
        'w_ffn2_down': nrm(ks[24], (L, D_FF, D_MODEL), D_FF ** -0.5),
        'norm_final': 1.0 + nrm(ks[25], (D_MODEL,), 0.02),
    }


def reference(x_prompt, x_sample, state_hgrn, state_ssm, state_conv, meta_tokens, lb_logits,
              norm_ffn1, w_ffn1_gate, w_ffn1_up, w_ffn1_down, norm_mix, w_in, hg_norm, conv_w, conv_b,
              dt_bias, a_log, d_skip, ssm_norm, w_out, norm_ffn2, w_ffn2_gate, w_ffn2_up, w_ffn2_down,
              norm_final):
    f32 = jnp.float32
    bp, seq_p, _ = x_prompt.shape
    seq_s = x_sample.shape[1]
    meta = jnp.broadcast_to(meta_tokens.astype(x_prompt.dtype)[None], (bp, N_META, D_MODEL))
    hp = jnp.concatenate([meta, x_prompt], axis=1)
    hs = x_sample
    hg_p, ssm_p, conv_p, hg_s, ssm_s, conv_s = [], [], [], [], [], []
    for l in range(DEPTH):
        lb = forget_lower_bound(lb_logits, l)
        w = (norm_ffn1[l], w_ffn1_gate[l], w_ffn1_up[l], w_ffn1_down[l], norm_mix[l], w_in[l], hg_norm[l],
             conv_w[l], conv_b[l], dt_bias[l], a_log[l], d_skip[l], ssm_norm[l], w_out[l],
             norm_ffn2[l], w_ffn2_gate[l], w_ffn2_up[l], w_ffn2_down[l])
        hp, a, b, c = decoder_layer(
            hp, (N_META, seq_p),
            jnp.zeros((bp, HG_HEADS, HG_HEAD_DIM, HG_HEAD_DIM), f32),
            jnp.zeros((bp, SSM_HEADS, SSM_HEAD_DIM, SSM_STATE), f32),
            jnp.zeros((bp, CONV_WIDTH - 1, CONV_DIM), f32),
            lb, *w)
        hg_p.append(a)
        ssm_p.append(b)
        conv_p.append(c)
        hs, a, b, c = decoder_layer(hs, (seq_s,), state_hgrn[l], state_ssm[l], state_conv[l], lb, *w)
        hg_s.append(a)
        ssm_s.append(b)
        conv_s.append(c)
    y_prompt = rmsnorm(hp[:, N_META:], norm_final)
    y_sample = rmsnorm(hs, norm_final)
    dt_out = x_prompt.dtype
    hgrn_prompt = jnp.stack(hg_p).astype(dt_out)
    ssm_prompt = jnp.stack(ssm_p).astype(dt_out)
    conv_prompt = jnp.stack(conv_p).astype(dt_out)
    hgrn_sample = jnp.stack(hg_s).astype(dt_out)
    ssm_sample = jnp.stack(ssm_s).astype(dt_out)
    conv_sample = jnp.stack(conv_s).astype(dt_out)
    return (y_prompt, y_sample, hgrn_prompt, ssm_prompt, conv_prompt, hgrn_sample, ssm_sample, conv_sample)
```

```python
import contextlib
import numpy as np
import concourse.bass as bass
import concourse.mybir as mybir
from concourse.bass_utils import run_bass_kernel_spmd

F32 = mybir.dt.float32
BF16 = mybir.dt.bfloat16
ALU = mybir.AluOpType
AF = mybir.ActivationFunctionType

NCOL = 2080
D = 1024
DFF = 2816
DIN = 3592
EPS = 1e-6


class T:
    __slots__ = ('name', 'lw', 'rd', 'rd_dma', 'semcnt', 'excl', 'parents')

    def __init__(self, name, excl=False):
        self.name = name
        self.excl = excl
        self.parents = None
        self.lw = None
        self.rd = {}
        self.rd_dma = []
        self.semcnt = 0


class Op:
    __slots__ = ('eng', 'fn', 'deps', 'sig', 'ticket', 'dma', 'dsem', 'dval', 'seq')


class Proxy:
    __slots__ = ('op',)


def _resolve(t):
    if t.parents:
        ps_ = t.parents
        t.parents = None
        for p in ps_:
            _resolve(p)
            cand = list(p.rd.values()) + ([p.lw] if p.lw is not None else [])
            for op in cand:
                if op.dma:
                    t.rd_dma.append(op)
                else:
                    cur = t.rd.get(op.eng)
                    if cur is None or cur.seq < op.seq:
                        t.rd[op.eng] = op
            t.rd_dma.extend(p.rd_dma)


class Prog:
    ENGS = ('pe', 'act', 'dve', 'pool', 'sp')

    def __init__(self, nc):
        self.nc = nc
        self.ops = {e: [] for e in self.ENGS}
        self.pending_dma = []
        self.stream = None
        self.nseq = 0

    def _mk(self, eng, fn, r, w, dma, extra):
        for t in list(r) + list(w):
            _resolve(t)
        op = Op()
        self.nseq += 1
        op.seq = self.nseq
        op.eng = eng
        op.fn = fn
        op.dma = dma
        op.sig = False
        op.ticket = 0
        op.dsem = None
        op.dval = 0
        deps = set()
        xr = [t for t in r if t.excl]
        if xr:
            r = [t for t in r if not t.excl]
            w = list(w) + [t for t in xr if t not in w]
        for t in r:
            if t.lw is not None:
                deps.add(t.lw)
        for t in w:
            if t.lw is not None:
                deps.add(t.lw)
            deps.update(t.rd.values())
            deps.update(t.rd_dma)
        for d in extra:
            deps.add(d)
        op.deps = [d for d in deps if not (d.eng == 'pe' and eng == 'pe' and not d.dma and not dma)]
        for d in op.deps:
            d.sig = True
        for t in r:
            if dma:
                t.rd_dma.append(op)
            else:
                t.rd[eng] = op
        for t in w:
            t.lw = op
            t.rd = {}
            t.rd_dma = []
        self.ops[eng].append(op)
        return op

    def add(self, eng, fn, r=(), w=(), extra=()):
        if self.stream is not None:
            self.stream.append(('add', eng, fn, list(r), list(w), None, None))
            return None
        extra = [x.op if isinstance(x, Proxy) else x for x in extra]
        return self._mk(eng, fn, r, w, False, extra)

    def flush(self, streams):
        assert self.stream is None
        idx = [0] * len(streams)
        cum = []
        for s in streams:
            c, acc = [], 0.0
            for rec in s:
                c.append(acc)
                acc += 0.02 if rec[1] in ('pe', 'sp') or rec[0] == 'dma' else (1.0 if rec[1] == 'dve' else 0.6)
            cum.append((c, max(acc, 1e-9)))
        while True:
            best = None
            for i, s in enumerate(streams):
                if idx[i] < len(s):
                    frac = cum[i][0][idx[i]] / cum[i][1]
                    if best is None or frac < best[0]:
                        best = (frac, i)
            if best is None:
                break
            i = best[1]
            kind, eng, fn, r, w, semtile, proxy = streams[i][idx[i]]
            idx[i] += 1
            if kind == 'add':
                self._mk(eng, fn, r, w, False, ())
            else:
                proxy.op = self.dma(eng, fn, semtile, r, w)

    def dma(self, eng, fn, semtile, r=(), w=(), extra=()):
        if self.stream is not None:
            px = Proxy()
            self.stream.append(('dma', eng, fn, list(r), list(w), semtile, px))
            return px
        op = self._mk(eng, fn, r, w, True, extra)
        semtile.semcnt += 16
        op.dsem = semtile
        op.dval = semtile.semcnt
        self.pending_dma.append(op)
        return op

    def barrier(self):
        last = [self.ops[e][-1] for e in self.ENGS if self.ops[e]]
        extra = last + self.pending_dma
        self.pending_dma = []
        for e in self.ENGS:
            self._mk(e, lambda eng: eng.nop(), (), (), False, extra)

    def emit(self):
        nc = self.nc
        with contextlib.ExitStack() as es:
            engsem = {e: es.enter_context(nc.semaphore('s_' + e)) for e in self.ENGS}
            tiles = []
            seen_t = set()
            for e in self.ENGS:
                for op in self.ops[e]:
                    if op.dma and id(op.dsem) not in seen_t:
                        seen_t.add(id(op.dsem))
                        tiles.append(op.dsem)
            assert len(tiles) <= 90, len(tiles)
            tsem = {id(t): es.enter_context(nc.semaphore('d_%d' % i)) for i, t in enumerate(tiles)}
            for e in self.ENGS:
                cnt = 0
                for op in self.ops[e]:
                    if not op.dma and op.sig:
                        cnt += 1
                        op.ticket = cnt
            block = es.enter_context(nc.Block())

            def body(ename):
                def run(engine):
                    seen = {}
                    for op in self.ops[ename]:
                        need = {}
                        for d in op.deps:
                            if d.dma:
                                key = tsem[id(d.dsem)]
                                val = d.dval
                            else:
                                key = engsem[d.eng]
                                val = d.ticket
                            kk = key.num
                            if kk not in need or need[kk][1] < val:
                                need[kk] = (key, val)
                        for kk, (key, val) in need.items():
                            if seen.get(kk, 0) < val:
                                engine.wait_ge(key, val)
                                seen[kk] = val
                        ins = op.fn(engine)
                        if op.dma:
                            ins.then_inc(tsem[id(op.dsem)], 16)
                        elif op.sig:
                            ins.then_inc(engsem[ename], 1)
                return run
            block.tensor(body('pe'))
            block.scalar(body('act'))
            block.vector(body('dve'))
            block.gpsimd(body('pool'))
            block.sync(body('sp'))


class _Stop(Exception):
    pass


def build_nc(stop=None):
    nc = bass.Bass("TRN2", target_bir_lowering=False)
    P = Prog(nc)
    stores = []

    def chk(tag):
        if stop == tag:
            raise _Stop()
    try:
        _build_body(nc, P, stores, chk)
    except _Stop:
        pass
    P.add('sp', lambda e: e.nop(), extra=stores)
    P.emit()
    return nc


def _build_body(nc, P, stores, chk):

    def din(name, shape):
        return nc.dram_tensor(name, list(shape), F32, kind="ExternalInput").ap()

    def dout(name, shape):
        return nc.dram_tensor(name, list(shape), F32, kind="ExternalOutput").ap()

    xin = din("xin", [NCOL, D])
    s_hg = din("s_hg", [16, 4, 128, 128])
    s_ssm = din("s_ssm", [16, 8, 64, 128])
    s_conv = din("s_conv", [16, 3, 1024])
    w1g = din("w1g", [D, DFF]); w1u = din("w1u", [D, DFF]); w1d = din("w1d", [DFF, D])
    w2g = din("w2g", [D, DFF]); w2u = din("w2u", [D, DFF]); w2d = din("w2d", [DFF, D])
    w_in = din("w_in", [D, DIN]); w_out = din("w_out", [D, D])
    vecs_d = din("vecs", [128, 104])
    consts_d = din("consts", [128, 1664])
    y_d = dout("y", [NCOL, D])
    hgp_d = dout("hg_p", [4, 128, 128])
    ssmp_d = dout("ssm_p", [8, 64, 128])
    convp_d = dout("conv_p", [3, 1024])
    hgs_d = dout("hg_s", [16, 4, 128, 128])
    ssms_d = dout("ssm_s", [16, 8, 64, 128])
    convs_d = dout("conv_s", [16, 3, 1024])

    BASE = 16512
    TOTAL = 212832
    cnt = [0]

    def sbat(off, shape, dt):
        cnt[0] += 1
        return nc.alloc_sbuf_tensor_at("t%d" % cnt[0], list(shape), dt, offset=BASE + off).ap()

    hT = sbat(0, [128, 8, NCOL], F32)
    Th = [T("h%d" % i) for i in range(17)]

    def hts(c0, n):
        out = []
        for i in range(17):
            a, b = (0, 32) if i == 0 else (32 + 128 * (i - 1), 32 + 128 * i)
            if a < c0 + n and b > c0:
                out.append(Th[i])
        return out
    o = 66560
    consts = sbat(o, [128, 1664], F32); o += 6656
    ident_f = consts[:, 0:128]; tri_f = consts[:, 128:256]; ones_f = consts[:, 256:384]
    reset = consts[:, 384:896]; resetA = consts[:, 896:1024]; E8 = consts[0:8, 1024:1536]; mhalf = consts[:, 1536:1664]
    vecs = sbat(o, [128, 104], F32); o += 416
    cb16 = sbat(o, [128, 256], BF16); o += 512
    ident_b = cb16[:, 0:128]; ones_b = cb16[:, 128:256]
    dv = sbat(o, [128, 64], F32); o += 256
    c1 = dv[:, 0:4]; c0_ = dv[:, 4:8]; a_bc = dv[:, 8:16]; a_col = dv[0:8, 16:17]; dvt = dv[:, 20:32]
    Tconst = T("const")
    stage = []
    Tstage = []
    for i in range(2):
        stage.append(sbat(o, [128, 2048], F32)); o += 8192
        Tstage.append(T("stage%d" % i))
    PH = o
    stg_i = [0]
    stg_only0 = [False]

    def next_stage():
        i = 0 if stg_only0[0] else stg_i[0] % 2
        stg_i[0] += 1
        return stage[i], Tstage[i]

    ps = [nc.alloc_psum_tensor("ps%d" % i, [128, 512], F32).ap() for i in range(8)]
    Tps = [T("ps%d" % i, excl=True) for i in range(8)]

    def act(out, in_, func, r, w, scale=1.0, bias=0.0):
        P.add('act', lambda e: e.activation(out=out, in_=in_, func=func, bias=bias, scale=scale), r, w)

    def tt(eng, out, a, b, op, r, w):
        P.add(eng, lambda e: e.tensor_tensor(out=out, in0=a, in1=b, op=op), r, w)

    def stt(eng, out, a, sc, b, op0, op1, r, w):
        P.add(eng, lambda e: e.scalar_tensor_tensor(out=out, in0=a, scalar=sc, in1=b, op0=op0, op1=op1), r, w)

    def ts(eng, out, a, s1, s2, op0, op1, r, w):
        P.add(eng, lambda e: e.tensor_scalar(out=out, in0=a, scalar1=s1, scalar2=s2, op0=op0, op1=op1), r, w)

    def cp(eng, out, in_, r, w):
        if eng == 'act':
            P.add('act', lambda e: e.copy(out=out, in_=in_), r, w)
        else:
            P.add(eng, lambda e: e.tensor_copy(out=out, in_=in_), r, w)

    def mm(out, lhsT, rhs, start, stop, r, w):
        P.add('pe', lambda e: e.matmul(out, lhsT, rhs, start=start, stop=stop), r, w)

    def tr(out, in_, idn, r, w):
        P.add('pe', lambda e: e.transpose(out, in_, idn), r, w)

    def ld(out, in_, semT, r=(), w=(), q='sp'):
        return P.dma(q, lambda e: e.dma_start(out=out, in_=in_), semT, r, w)

    def v3(ap2d, a):
        return ap2d.rearrange("p (a b) -> p a b", a=a)

    ld(consts, consts_d, Tconst, w=[Tconst])
    ld(vecs, vecs_d, Tconst, w=[Tconst])
    cp('dve', ident_b, ident_f, [Tconst], [Tconst])
    cp('dve', ones_b, ones_f, [Tconst], [Tconst])
    tt('dve', dvt[:, 0:4], vecs[:, 84:88], vecs[:, 80:84], ALU.subtract, [Tconst], [Tconst])
    act(dvt[:, 4:8], dvt[:, 0:4], AF.Exp, [Tconst], [Tconst])
    ts('dve', dvt[:, 4:8], dvt[:, 4:8], 1.0, None, ALU.add, ALU.bypass, [Tconst], [Tconst])
    P.add('dve', lambda e: e.reciprocal(out=dvt[:, 8:12], in_=dvt[:, 4:8]), [Tconst], [Tconst])
    ts('dve', c1, dvt[:, 8:12], -0.5, 0.5, ALU.mult, ALU.add, [Tconst], [Tconst])
    tt('dve', c0_, dvt[:, 8:12], c1, ALU.add, [Tconst], [Tconst])
    act(a_bc, vecs[:, 92:100], AF.Exp, [Tconst], [Tconst])
    ts('dve', a_bc, a_bc, -1.0, None, ALU.mult, ALU.bypass, [Tconst], [Tconst])
    act(a_col, vecs[0:8, 101:102], AF.Exp, [Tconst], [Tconst])
    ts('dve', a_col, a_col, -1.0, None, ALU.mult, ALU.bypass, [Tconst], [Tconst])
    nw = lambda k: vecs[:, 8 * k:8 * k + 8]
    hgn = vecs[:, 32:36]; snrm = vecs[:, 36:40]
    cw = lambda j: vecs[:, 40 + 8 * j:48 + 8 * j]
    cbias = vecs[:, 72:80]; Dexp = vecs[:, 88:92]; dtb = vecs[0:8, 100:101]

    xrows = [(0, 32)] + [(32 + 128 * i, 128) for i in range(16)]

    xst = [sbat(TOTAL - 8192, [128, 1024], F32), sbat(TOTAL - 4096, [128, 1024], F32)]
    Txst = [T("xst0"), T("xst1")]

    def load_chunk(ci):
        r0, n = xrows[ci]
        st, Tst = xst[ci % 2], Txst[ci % 2]
        ld(st[0:n, 0:1024], xin[r0:r0 + n, :], Tst, w=[Tst])
        for half in range(2):
            bk = 4 + (2 * ci + half) % 4
            for k in range(4):
                c = half * 4 + k
                tr(ps[bk][:, k * 128:k * 128 + n], st[0:n, c * 128:(c + 1) * 128], ident_f[0:n, 0:n], [Tst, Tconst], [Tps[bk]])
            eng = 'act' if half == 0 else 'dve'
            cp(eng, hT[:, half * 4:half * 4 + 4, r0:r0 + n], v3(ps[bk], 4)[:, :, 0:n], [Tps[bk]], [Th[ci]])

    CT = [(0, 416), (416, 416), (832, 416), (1248, 416), (1664, 416)]
    TILE_CHUNKS = [[0, 1, 2, 3], [4, 5, 6], [7, 8, 9], [10, 11, 12], [13, 14, 15, 16]]

    def prep_x(ti):
        for ci in TILE_CHUNKS[ti]:
            load_chunk(ci)

    def ffn(nk, wg, wu, wd, prep=None, post=None):
        oo = PH
        xn = sbat(oo, [128, 8, NCOL], BF16); oo += 33280
        hid = []
        for i in range(2):
            hid.append(sbat(oo, [128, 2, NCOL], BF16)); oo += 8320
        wb = []
        for i in range(7):
            wb.append(sbat(oo, [128, 2048], BF16)); oo += 4096
        sqb = sbat(oo, [128, 8, 512], BF16); oo += 8192
        rstd = sbat(oo, [128, 512], F32); oo += 2048
        tln = sbat(oo, [128, 512], F32); oo += 2048
        sil = []
        for i in range(2):
            sil.append(sbat(oo, [128, 512], F32)); oo += 2048
        assert BASE + oo <= BASE + TOTAL, oo
        Txn = [T("xn%d" % i) for i in range(5)]
        Thid = [T("hid%d" % i) for i in range(2)]
        Twb = [T("wb%d" % i) for i in range(7)]
        Tsqb = T("sqb"); Trstd = T("rstd"); Ttln = T("tln"); Tsil = [T("sil0"), T("sil1")]
        def norm_tile(ti):
            c0, n = CT[ti]
            hh = hts(c0, n)
            act(sqb[:, :, 0:n], hT[:, :, c0:c0 + n], AF.Square, hh, [Tsqb])
            for c in range(8):
                mm(ps[4][:, 0:n], ones_b, sqb[:, c, 0:n], c == 0, c == 7, [Tsqb, Tconst], [Tps[4]])
            act(tln[:, 0:n], ps[4][:, 0:n], AF.Ln, [Tps[4]], [Ttln], scale=1.0 / D, bias=EPS)
            act(rstd[:, 0:n], tln[:, 0:n], AF.Exp, [Ttln], [Trstd], scale=-0.5)
            for c in range(8):
                stt('dve', xn[:, c, c0:c0 + n], hT[:, c, c0:c0 + n], nw(nk)[:, c:c + 1], rstd[:, 0:n],
                    ALU.mult, ALU.mult, hh + [Trstd, Tconst], [Txn[ti]])
        wgv = wg.rearrange("(c p) n -> p c n", p=128)
        wuv = wu.rearrange("(c p) n -> p c n", p=128)
        wdv = wd.rearrange("(j p) n -> p j n", p=128)
        wslot = [0, 0]
        nld = [0]

        def load_w(src_ap, shape3):
            st, Tst = next_stage()
            if shape3[0] == 8:
                i = wslot[0] % 4
                wslot[0] += 1
            else:
                i = 4 + wslot[1] % 3
                wslot[1] += 1
            ld(st.rearrange("p (a b) -> p a b", a=shape3[0]), src_ap, Tst, w=[Tst])
            nld[0] += 1
            cp('dve' if nld[0] <= 6 else 'pool', wb[i], st, [Tst], [Twb[i]])
            return i

        def load_jp(jp):
            a = load_w(wgv[:, :, jp * 256:(jp + 1) * 256], (8, 256))
            b = load_w(wuv[:, :, jp * 256:(jp + 1) * 256], (8, 256))
            c = load_w(wdv[:, 2 * jp:2 * jp + 2, :], (2, 1024))
            return (a, b, c)
        wq = [load_jp(0)]
        for ti in range(len(CT)):
            if prep is not None:
                prep(ti)
        norm_tile(0)
        wq.append(load_jp(1))
        dbk = [0]

        def up_phase(jp, ti):
            ia, ib, ic = wq[jp]
            wga = v3(wb[ia], 8); wua = v3(wb[ib], 8)
            hb = hid[jp % 2]; Thb = Thid[jp % 2]
            c0, n = CT[ti]
            for jj in range(2):
                pg = ps[2 * jj]; pu = ps[2 * jj + 1]
                for c in range(8):
                    mm(pg[:, 0:n], wga[:, c, jj * 128:(jj + 1) * 128], xn[:, c, c0:c0 + n], c == 0, c == 7,
                       [Twb[ia], Txn[ti]], [Tps[2 * jj]])
                for c in range(8):
                    mm(pu[:, 0:n], wua[:, c, jj * 128:(jj + 1) * 128], xn[:, c, c0:c0 + n], c == 0, c == 7,
                       [Twb[ib], Txn[ti]], [Tps[2 * jj + 1]])
                act(sil[jj][:, 0:n], pg[:, 0:n], AF.Silu, [Tps[2 * jj]], [Tsil[jj]])
                tt('dve', hb[:, jj, c0:c0 + n], sil[jj][:, 0:n], pu[:, 0:n], ALU.mult, [Tsil[jj], Tps[2 * jj + 1]], [Thb])

        def down_phase(jp, ti):
            ia, ib, ic = wq[jp]
            wda = v3(wb[ic], 2)
            hb = hid[jp % 2]; Thb = Thid[jp % 2]
            c0, n = CT[ti]
            hh = hts(c0, n)
            for m in range(8):
                bk = 4 + dbk[0] % (2 if (jp == 10 and post is not None) else 4)
                dbk[0] += 1
                for jj in range(2):
                    mm(ps[bk][:, 0:n], wda[:, jj, m * 128:(m + 1) * 128], hb[:, jj, c0:c0 + n], jj == 0, jj == 1,
                       [Twb[ic], Thb], [Tps[bk]])
                stt('dve', hT[:, m, c0:c0 + n], ps[bk][:, 0:n], 0.5, hT[:, m, c0:c0 + n], ALU.mult, ALU.add,
                    [Tps[bk]] + hh, hh)

        for jp in range(11):
            for ti in range(len(CT)):
                if jp == 0 and ti + 1 < len(CT):
                    norm_tile(ti + 1)
                up_phase(jp, ti)
                if jp > 0:
                    down_phase(jp - 1, ti)
            if jp + 2 < 11:
                wq.append(load_jp(jp + 2))
        for ti in range(len(CT)):
            down_phase(10, ti)
            if post is not None:
                post(ti, oo)

    ffn(0, w1g, w1u, w1d, prep=prep_x)
    P.barrier()
    chk('ffn1')

    FIXED_MIX = 57472 + 16384 + 4192 + 2048 + 1024 + 2048 + 1024 + 384
    ARENA_OFF = PH - 8192
    ARENA_BYTES = (TOTAL - PH - FIXED_MIX) // 512 * 512 + 8192
    oo = ARENA_OFF + ARENA_BYTES
    win_b = sbat(oo, [128, 8, DIN], BF16); oo += 57472
    wout_b = sbat(oo, [128, 8, D], BF16); oo += 16384
    Twin = [T("win%d" % b) for b in range(15)]; Twout = T("wout")
    winv = w_in.rearrange("(c p) n -> p c n", p=128)
    woutv = w_out.rearrange("(c p) n -> p c n", p=128)
    for b in [2, 3, 10, 11, 12, 13, 0, 1, 6, 7, 8, 9, 4, 5, 14]:
        c0 = b * 256
        n = min(256, DIN - c0)
        st, Tst = next_stage()
        sv = st[:, 0:8 * n].rearrange("p (a b) -> p a b", a=8)
        q_ = 'sp' if Tst is Tstage[0] else 'act'
        ld(sv, winv[:, :, c0:c0 + n], Tst, w=[Tst], q=q_)
        cp('dve' if q_ == 'sp' else 'act', win_b[:, :, c0:c0 + n], sv, [Tst], [Twin[b]])
    def load_wout():
        for b in range(4):
            st, Tst = next_stage()
            sv = v3(st, 8)
            ld(sv, woutv[:, :, b * 256:(b + 1) * 256], Tst, w=[Tst])
            cp('act' if b % 2 == 0 else 'dve', wout_b[:, :, b * 256:(b + 1) * 256], sv, [Tst], [Twout])
    chk('mw')
    stg_only0[0] = True
    raw = sbat(oo, [128, 8, 131], F32); oo += 4192
    Sst = sbat(oo, [128, 4, 128], F32); oo += 2048
    Sbf = sbat(oo, [128, 4, 128], BF16); oo += 1024
    Hst = sbat(oo, [128, 512], F32); oo += 2048
    Hbf = sbat(oo, [128, 512], BF16); oo += 1024
    sm = sbat(oo, [128, 96], F32); oo += 384
    assert oo <= TOTAL, oo
    Traw = T("raw"); TS = T("S"); TSb = T("Sbf"); TH = T("H"); THb = T("Hbf"); Tsm = T("sm")
    UNIT = 512
    A_OFF = ARENA_OFF
    A_N = ARENA_BYTES // UNIT
    a_free = [True] * A_N
    a_last = [None] * A_N
    for u in range(8192 // UNIT):
        a_last[u] = Tstage[1]
    a_pend = {}
    peak = [0]

    first_flag = [False]

    class Buf:
        def __init__(self, nbytes=2048, fixed=False):
            if first_flag[0] and not fixed:
                nbytes = max(UNIT, nbytes // 4)
            k = (nbytes + UNIT - 1) // UNIT
            sid = id(P.stream) if P.stream is not None else None
            own = a_pend.get(sid, set()) if sid is not None else set()
            run = None
            u = 0
            while u + k <= A_N:
                ok = True
                for j in range(k):
                    if not (a_free[u + j] or (u + j) in own):
                        ok = False
                        u = u + j + 1
                        break
                if ok:
                    run = list(range(u, u + k))
                    break
            assert run is not None, "arena full"
            self.t = T("buf")
            par = []
            for x in run:
                if a_last[x] is not None and a_last[x] not in par:
                    par.append(a_last[x])
                a_last[x] = self.t
                a_free[x] = False
                own.discard(x)
            self.t.parents = par
            self.run = run
            peak[0] = max(peak[0], run[-1] + 1)
            off = A_OFF + run[0] * UNIT
            self.f = sbat(off, [128, k * UNIT // 4], F32)
            self.b = sbat(off, [128, k * UNIT // 2], BF16)

        def free(self):
            sid = id(P.stream) if P.stream is not None else None
            if sid is None:
                for x in self.run:
                    a_free[x] = True
            else:
                a_pend.setdefault(sid, set()).update(self.run)

    def end_section():
        for s in a_pend.values():
            for x in s:
                a_free[x] = True
        a_pend.clear()

    P.add('pool', lambda e: e.memset(raw, 0.0), (), [Traw])
    P.add('pool', lambda e: e.memset(Sst, 0.0), (), [TS])
    P.add('pool', lambda e: e.memset(Hst, 0.0), (), [TH])
    P.add('pool', lambda e: e.memset(Sbf, 0.0), (), [TSb])
    P.add('pool', lambda e: e.memset(Hbf, 0.0), (), [THb])

    chk('m0a')
    pbk = [0]
    bank_list = [list(range(8))]

    def bank():
        bl = bank_list[0]
        b = bl[pbk[0] % len(bl)]
        pbk[0] += 1
        return ps[b], Tps[b]

    def rms_pow(src_ps, Tsrc, n_feat, width, dst_f, Tdst):
        ts('dve', dst_f, src_ps, 1.0 / n_feat, EPS, ALU.mult, ALU.add, [Tsrc], [Tdst])
        tt('pool', dst_f, dst_f, mhalf[:, 0:1].broadcast_to([128, width]) if width != 128 else mhalf, ALU.pow, [Tdst, Tconst], [Tdst])

    prenorm = {}

    def pre_norm(ci):
        c0, NT = (0, 32) if ci == 0 else (32 + 128 * (ci - 1), 128)
        hh = [Th[ci]]
        Bxn = Buf(); Bsq = Buf(); Brs = Buf()
        xn_m = v3(Bxn.b[:, 0:8 * NT], 8)
        sqh = v3(Bsq.b[:, 0:8 * NT], 8)
        act(sqh, hT[:, :, c0:c0 + NT], AF.Square, hh, [Bsq.t])
        pb, Tpb = bank()
        for c in range(8):
            mm(pb[:, 0:NT], ones_b, sqh[:, c, :], c == 0, c == 7, [Bsq.t, Tconst], [Tpb])
        rstd = Brs.f[:, 0:NT]
        act(rstd, pb[:, 0:NT], AF.Ln, [Tpb], [Brs.t], scale=1.0 / D, bias=EPS)
        act(rstd, rstd, AF.Exp, [Brs.t], [Brs.t], scale=-0.5)
        for c in range(8):
            stt('dve', xn_m[:, c, :], hT[:, c, c0:c0 + NT], nw(1)[:, c:c + 1], rstd, ALU.mult, ALU.mult,
                hh + [Brs.t, Tconst], [Bxn.t])
        Bsq.free(); Brs.free()
        return Bxn, xn_m

    def mixer_chunk(ci):
        first = ci == 0
        first_flag[0] = first
        c0, NT = (0, 32) if first else (32 + 128 * (ci - 1), 128)
        TS_ = 16 if first else 128
        hh = [Th[ci]]
        Bxn, xn_m = prenorm.pop(ci) if ci in prenorm else pre_norm(ci)

        if first:
            chk('m0b')
        def proj(col0, nfc, M=128):
            pb, Tpb = bank()
            wts = [Twin[b] for b in range(col0 // 256, (col0 + (nfc - 1) * 128 + M - 1) // 256 + 1)]
            for k in range(nfc):
                for c in range(8):
                    mm(pb[0:M, k * NT:(k + 1) * NT], win_b[:, c, col0 + k * 128:col0 + k * 128 + M], xn_m[:, c, :],
                       c == 0, c == 7, wts + [Bxn.t], [Tpb])
            return pb, Tpb
        BF_ = Buf(); Bq = Buf(); Bg = Buf(); Bz = Buf()
        tmpF = v3(BF_.f[:, 0:4 * NT], 4); sq = v3(Bq.f[:, 0:4 * NT], 4)
        sg = v3(Bg.f[:, 0:4 * NT], 4); sz = v3(Bz.f[:, 0:4 * NT], 4)
        pb, Tpb = proj(512, 4)
        act(tmpF, v3(pb[:, 0:4 * NT], 4), AF.Tanh, [Tpb], [BF_.t], scale=0.5)
        for half in range(2):
            pb, Tpb = proj(2560 + 512 * half, 4)
            cp('act', raw[:, 4 * half:4 * half + 4, 3:3 + NT], v3(pb[:, 0:4 * NT], 4), [Tpb], [Traw])
        pb, Tpb = proj(0, 4)
        act(sq, v3(pb[:, 0:4 * NT], 4), AF.Silu, [Tpb], [Bq.t])
        pb, Tpb = proj(1536, 4)
        act(sg, v3(pb[:, 0:4 * NT], 4), AF.Silu, [Tpb], [Bg.t])
        pb, Tpb = proj(2048, 4)
        act(sz, v3(pb[:, 0:4 * NT], 4), AF.Silu, [Tpb], [Bz.t])
        if first:
            chk('m0c')
        if first:
            chk('m0d')
        if first:
            chk('m0e')
        Bv = Buf(1024, True)
        pb, Tpb = bank()
        for c in range(8):
            mm(pb[0:NT, :], xn_m[:, c, :], win_b[:, c, 1024:1536], c == 0, c == 7, [Twin[4], Twin[5], Bxn.t], [Tpb])
        v_tok = Bv.b[0:NT, 0:512]
        cp('act', v_tok, pb[0:NT, :], [Tpb], [Bv.t])
        if first:
            Bvf = Buf(2048, True)
            cp('dve', Bvf.f[0:32, :], pb[0:32, :], [Tpb], [Bvf.t])

        Bdt = Buf(1536, True)
        dtT = Bdt.f[0:8, 0:NT]
        pb, Tpb = proj(3584, 1, M=8)
        act(dtT, pb[0:8, 0:NT], AF.Exp, [Tpb, Tconst], [Bdt.t], bias=dtb)
        act(dtT, dtT, AF.Ln, [Bdt.t], [Bdt.t], bias=1.0)
        Bxn.free()
        BmA = Buf(1024); BmB = Buf(1024)
        mixA = v3(BmA.b[:, 0:4 * NT], 4); mixB = v3(BmB.b[:, 0:4 * NT], 4)
        S1 = []
        P.stream = S1
        bank_list[0] = [0, 1, 2, 3]
        BK = Buf(); BB = Buf(); Be = Buf(); Bqk = Buf()
        tmpK = v3(BK.f[:, 0:4 * NT], 4); tmpB = v3(BB.f[:, 0:4 * NT], 4); eb = v3(Be.f[:, 0:4 * NT], 4)
        qt = v3(Bqk.b[:, 0:4 * NT], 4); kt = v3(Bqk.b[:, 4 * NT:8 * NT], 4)
        bc4 = lambda col: col.unsqueeze(2).broadcast_to([128, 4, NT])
        tt('dve', tmpF, tmpF, bc4(c1), ALU.mult, [BF_.t, Tconst], [BF_.t])
        tt('dve', tmpF, tmpF, bc4(c0_), ALU.add, [BF_.t, Tconst], [BF_.t])
        ts('dve', tmpK, tmpF, -1.0, 1.0, ALU.mult, ALU.add, [BF_.t], [BK.t])
        tt('dve', sg, sg, bc4(hgn), ALU.mult, [Bg.t, Tconst], [Bg.t])
        if first:
            Bsg = Buf(512, True)
            fs = v3(Bsg.f[:, 0:64], 4); ks = v3(Bsg.f[:, 64:128], 4)
            cp('pool', fs, tmpF[:, :, 16:32], [BF_.t], [Bsg.t])
            cp('pool', ks, tmpK[:, :, 16:32], [BK.t], [Bsg.t])
        act(tmpF, tmpF, AF.Ln, [BF_.t], [BF_.t])
        rmask = resetA[:, 0:4 * NT] if first else reset
        P.add('dve', lambda e: e.tensor_tensor_scan(out=BB.f[:, 0:4 * NT], data0=rmask, data1=BF_.f[:, 0:4 * NT],
                                                    initial=0.0, op0=ALU.mult, op1=ALU.add), [BF_.t, Tconst], [BB.t])
        act(eb, tmpB, AF.Exp, [BB.t], [Be.t])
        act(tmpF, tmpB, AF.Exp, [BB.t], [BF_.t], scale=-1.0)
        tt('dve', kt, tmpK, tmpF, ALU.mult, [BK.t, BF_.t], [Bqk.t])
        tt('dve', qt, sq, eb, ALU.mult, [Bq.t, Be.t], [Bqk.t])
        BB.free()

        if first:
            po_s, Tpo_s = ps[7], Tps[7]
            BstA = Buf(2048, True)
            for b in range(16):
                if b % 2 == 0:
                    st, Tst = next_stage()
                    st = st[:, 0:512]
                else:
                    st, Tst = BstA.f[:, 0:512], BstA.t
                sv = v3(st, 4)
                ld(sv, s_hg[b].rearrange("h k v -> k h v"), Tst, w=[Tst], q='act')
                pV, TpV = bank()
                mm(pV[:, 0:512], ident_f[0:32, 16 + b:17 + b].broadcast_to([32, 128]), Bvf.f[0:32, :], True, True,
                   [Tconst, Bvf.t], [TpV])
                Bt = Buf(2048, True); Bt2 = Buf(2048, True)
                tt('dve', v3(Bt.f, 4), sv, fs[:, :, b:b + 1].broadcast_to([128, 4, 128]), ALU.mult, [Tst, Bsg.t], [Bt.t])
                tt('dve', v3(Bt2.f, 4), v3(pV, 4), ks[:, :, b:b + 1].broadcast_to([128, 4, 128]), ALU.mult, [TpV, Bsg.t], [Bt2.t])
                tt('dve', st, Bt2.f, Bt.f, ALU.add, [Bt.t, Bt2.t], [Tst])
                for h in range(4):
                    mm(po_s[:, h * 16 + b:h * 16 + b + 1], sv[:, h, :], sq[:, h, 16 + b:17 + b], True, True, [Tst, Bq.t], [Tpo_s])
                stores.append(P.dma('act', (lambda sv=sv, b=b: lambda e: e.dma_start(out=hgs_d[b].rearrange("h k v -> k h v"), in_=sv))(),
                                    Tst, r=[Tst]))
                Bt.free(); Bt2.free()
            Bsg.free(); Bvf.free(); BstA.free()

        subs = [(0, 16, 0)] if first else [(0, 64, 0), (64, 64, 64)]
        for (t0, n, pb0) in subs:
            PR = slice(pb0, pb0 + n)
            pS, TpS = bank()
            for h in range(4):
                mm(pS[PR, h * n:(h + 1) * n], kt[:, h, t0:t0 + n], qt[:, h, t0:t0 + n], True, True, [Bqk.t], [TpS])
            Bp = Buf(512)
            PT = v3(Bp.b[PR, 0:4 * n], 4)
            tt('dve', PT, v3(pS[PR, 0:4 * n], 4), tri_f[PR, pb0:pb0 + n].unsqueeze(1).broadcast_to([n, 4, n]), ALU.mult,
               [TpS, Tconst], [Bp.t])
            pK, TpK = bank()
            pKb = pK.bitcast(BF16)
            for h in range(4):
                tr(pKb[PR, h * 128:(h + 1) * 128], kt[:, h, t0:t0 + n], ident_b, [Bqk.t, Tconst], [TpK])
            Bkt = Buf(1024, True)
            kt_tok = Bkt.b[PR, 0:512]
            cp('act', kt_tok, pKb[PR, 0:512], [TpK], [Bkt.t])
            pO, TpO = bank()
            for h in range(4):
                mm(pO[:, h * n:(h + 1) * n], v_tok[PR, h * 128:(h + 1) * 128], PT[:, h, :], True, False, [Bv.t, Bp.t], [TpO])
                mm(pO[:, h * n:(h + 1) * n], Sbf[:, h, :], qt[:, h, t0:t0 + n], False, True, [TSb, Bqk.t], [TpO])
            pKV, TpKV = bank()
            for h in range(4):
                mm(pKV[:, h * 128:(h + 1) * 128], kt_tok[:, h * 128:(h + 1) * 128], v_tok[PR, h * 128:(h + 1) * 128], True, True,
                   [Bkt.t, Bv.t], [TpKV])
            Bts = Buf(2048, True)
            tmpS = v3(Bts.f, 4)
            tt('dve', tmpS, Sst, v3(pKV, 4), ALU.add, [TS, TpKV], [Bts.t])
            tt('dve', Sst, tmpS, eb[:, :, t0 + n - 1:t0 + n].broadcast_to([128, 4, 128]), ALU.mult, [Bts.t, Be.t], [TS])
            cp('act', Sbf, Sst, [TS], [TSb])
            Bts.free(); Bp.free(); Bkt.free()
            hg_post(pO, TpO, n, lambda h0, h1, _t0=t0, _n=n: sg[:, h0:h1, _t0:_t0 + _n],
                    Bg.t, lambda _t0=t0, _n=n: mixA[:, :, _t0:_t0 + _n], BmA.t)
        if first:
            hg_post(po_s, Tpo_s, 16, lambda h0, h1: sg[:, h0:h1, 16:32], Bg.t, lambda: mixA[:, :, 16:32], BmA.t)
        Bq.free(); Bg.free(); Be.free(); Bqk.free(); Bv.free(); BK.free(); BF_.free()
        S2 = []
        P.stream = S2
        bank_list[0] = [4, 5, 6] if first else [4, 5, 6, 7]
        Bacc = [Buf(), Buf()]
        Bxs = Buf(); Bxb = Buf()
        xs = v3(Bxs.f[:, 0:4 * NT], 4)
        xsb = v3(Bxb.b[:, 0:4 * NT], 4); BC = v3(Bxb.b[:, 4 * NT:8 * NT], 4)
        for half in range(2):
            eng = 'dve'
            acc = v3(Bacc[half].f[:, 0:4 * NT], 4)
            for k in range(4):
                fc = 4 * half + k
                ts(eng, acc[:, k, :], raw[:, fc, 0:NT], cw(0)[:, fc:fc + 1], cbias[:, fc:fc + 1], ALU.mult, ALU.add,
                   [Traw, Tconst], [Bacc[half].t])
                for j in range(1, 4):
                    stt(eng, acc[:, k, :], raw[:, fc, j:j + NT], cw(j)[:, fc:fc + 1], acc[:, k, :], ALU.mult, ALU.add,
                        [Traw, Tconst, Bacc[half].t], [Bacc[half].t])
        if first:
            Bst2 = Buf(4096, True)
            st, Tst = Bst2.f, Bst2.t
            ld(st[0:48, 0:1024], s_conv.rearrange("b j c -> (b j) c"), Tst, w=[Tst])
            stores.append(P.dma('sp', lambda e: e.dma_start(out=convs_d[:, 0:2, :], in_=s_conv[:, 1:3, :]), Tst, r=[Tst]))
            pSC, TpSC = bank()
            for fc in range(8):
                tr(pSC[:, fc * 48:(fc + 1) * 48], st[0:48, fc * 128:(fc + 1) * 128], ident_f[0:48, 0:48], [Tst, Tconst], [TpSC])
            Bsc = Buf(1536, True)
            cp('dve', Bsc.f[:, 0:384], pSC[:, 0:384], [TpSC], [Bsc.t])
            scv = Bsc.f[:, 0:384].rearrange("p (f b j) -> p f b j", f=8, b=16)
            for fc in range(8):
                half, k = fc // 4, fc % 4
                accs = v3(Bacc[half].f[:, 0:4 * NT], 4)[:, k, 16:32]
                ts('dve', accs, raw[:, fc, 3 + 16:3 + 32], cw(3)[:, fc:fc + 1], cbias[:, fc:fc + 1], ALU.mult, ALU.add,
                   [Traw, Tconst, Bacc[half].t], [Bacc[half].t])
                for j in range(3):
                    stt('dve', accs, scv[:, fc, :, j], cw(j)[:, fc:fc + 1], accs, ALU.mult, ALU.add,
                        [Bsc.t, Tconst, Bacc[half].t], [Bacc[half].t])
            Bsc.free()
        act(xs, v3(Bacc[0].f[:, 0:4 * NT], 4), AF.Silu, [Bacc[0].t], [Bxs.t])
        act(BC, v3(Bacc[1].f[:, 0:4 * NT], 4), AF.Silu, [Bacc[1].t], [Bxb.t])
        act(xsb, v3(Bacc[0].f[:, 0:4 * NT], 4), AF.Silu, [Bacc[0].t], [Bxb.t])
        Bacc[0].free(); Bacc[1].free()
        cp('pool', raw[:, :, 0:3], raw[:, :, TS_:TS_ + 3], [Traw], [Traw])

        if first:
            Bd = Buf(512, True); Bys = Buf(512, True); Bxss = Buf(512, True); BstB = Buf(2048, True)
            dtaT = Bdt.f[0:8, 256:272]
            ts('dve', dtaT, dtT[:, 16:32], a_col, None, ALU.mult, ALU.bypass, [Bdt.t, Tconst], [Bdt.t])
            pE, TpE = bank()
            for fc in range(4):
                mm(pE[:, fc * 16:(fc + 1) * 16], E8[:, fc * 128:(fc + 1) * 128], dtT[:, 16:32], True, True, [Tconst, Bdt.t], [TpE])
            for fc in range(4):
                mm(pE[:, 64 + fc * 16:64 + (fc + 1) * 16], E8[:, fc * 128:(fc + 1) * 128], dtaT, True, True, [Tconst, Bdt.t], [TpE])
            dec = v3(Bd.f[:, 0:64], 4); xdts = v3(Bd.f[:, 64:128], 4)
            act(dec, v3(pE[:, 64:128], 4), AF.Exp, [TpE], [Bd.t])
            tt('dve', xdts, xs[:, :, 16:32], v3(pE[:, 0:64], 4), ALU.mult, [Bxs.t, TpE], [Bd.t])
            ysb = v3(Bys.f[:, 0:64], 4)
            xs_s = v3(Bxss.f[:, 0:64], 4)
            cp('pool', xs_s, xs[:, :, 16:32], [Bxs.t], [Bxss.t])
            for b in range(16):
                if b % 2 == 0:
                    st, Tst = Bst2.f[:, 0:512], Bst2.t
                else:
                    st, Tst = BstB.f[:, 0:512], BstB.t
                sv = v3(st, 4)
                ld(sv, s_ssm[b].rearrange("(f hh) q n -> (hh q) f n", hh=2), Tst, w=[Tst], q='act')
                pBC, TpBC = bank()
                for i in range(4):
                    mm(pBC[:, i * 128:(i + 1) * 128], BC[:, i, 16 + b:17 + b].broadcast_to([128, 128]), ident_b, True, True,
                       [Bxb.t, Tconst], [TpBC])
                Bt = Buf(2048, True); Bt2 = Buf(2048, True)
                tt('dve', v3(Bt.f, 4), sv, dec[:, :, b:b + 1].broadcast_to([128, 4, 128]), ALU.mult, [Tst, Bd.t], [Bt.t])
                b4 = lambda ap2: ap2.rearrange("p (g n) -> p g n", g=2).unsqueeze(2).broadcast_to([128, 2, 2, 128])
                f4 = lambda ap: ap.rearrange("p (g i n) -> p g i n", g=2, i=2)
                xcol = xdts[:, :, b:b + 1].rearrange("p (g i) o -> p g i o", g=2).broadcast_to([128, 2, 2, 128])
                tt('dve', f4(Bt2.f), b4(pBC[:, 0:256]), xcol, ALU.mult, [TpBC, Bd.t], [Bt2.t])
                tt('dve', st, Bt2.f, Bt.f, ALU.add, [Bt.t, Bt2.t], [Tst])
                tt('dve', f4(Bt2.f), f4(st), b4(pBC[:, 256:512]), ALU.mult, [Tst, TpBC], [Bt2.t])
                P.add('dve', (lambda b=b, src=v3(Bt2.f, 4): lambda e: e.tensor_reduce(out=ysb[:, :, b], in_=src, axis=mybir.AxisListType.X, op=ALU.add))(),
                      [Bt2.t], [Bys.t])
                stores.append(P.dma('act', (lambda sv=sv, b=b: lambda e: e.dma_start(
                    out=ssms_d[b].rearrange("(f hh) q n -> (hh q) f n", hh=2), in_=sv))(), Tst, r=[Tst]))
                Bt.free(); Bt2.free()
            Bd.free()
            pa, Tpa = bank(); pb2, Tpb2 = bank()
            for fc in range(8):
                pp, Tpp = (pa, Tpa) if fc < 4 else (pb2, Tpb2)
                tr(pp[0:16, (fc % 4) * 128:(fc % 4 + 1) * 128], raw[:, fc, 3 + 16:3 + 32], ident_f, [Traw, Tconst], [Tpp])
            st, Tst = Bst2.f, Bst2.t
            cp('dve', st[0:16, 0:512], pa[0:16, :], [Tpa], [Tst])
            cp('dve', st[0:16, 512:1024], pb2[0:16, :], [Tpb2], [Tst])
            stores.append(P.dma('sp', lambda e: e.dma_start(out=convs_d[:, 2, :], in_=st[0:16, 0:1024]), Tst, r=[Tst]))

        n = TS_
        pD, TpD = bank()
        tr(pD[0:n, 0:8], dtT[:, 0:n], ident_f[0:8, 0:8], [Bdt.t, Tconst], [TpD])
        dt_tok = sm[0:n, 0:8]; dta = sm[0:n, 8:16]; cums = sm[0:n, 16:32]; wdec = sm[0:n, 32:40]; wgt = sm[0:n, 40:48]
        ecl = sm[:, 48:56]
        cp('dve', dt_tok, pD[0:n, 0:8], [TpD], [Tsm])
        tt('dve', dta, dt_tok, a_bc[0:n, :], ALU.mult, [Tsm, Tconst], [Tsm])
        pC, TpC = bank()
        mm(pC[0:n, 0:8], tri_f[0:n, 0:n], dta, True, True, [Tsm, Tconst], [TpC])
        mm(pC[:, 8:16], ones_f[0:n, :], dta, True, True, [Tsm, Tconst], [TpC])
        cp('dve', sm[:, 56:64], pC[:, 8:16], [TpC], [Tsm])
        cp('dve', cums[:, 0:8], pC[0:n, 0:8], [TpC], [Tsm])
        tt('dve', wdec, sm[0:n, 56:64], cums[:, 0:8], ALU.subtract, [Tsm], [Tsm])
        act(wdec, wdec, AF.Exp, [Tsm], [Tsm])
        tt('dve', wgt, wdec, dt_tok, ALU.mult, [Tsm], [Tsm])
        act(ecl, sm[:, 56:64], AF.Exp, [Tsm], [Tsm])
        pR = [bank(), bank()]
        for h in range(8):
            pp, Tpp = pR[h // 4]
            mm(pp[:, (h % 4) * n:(h % 4 + 1) * n], dta[:, h:h + 1].broadcast_to([n, 128]), tri_f[0:n, 0:n], True, True,
               [Tsm, Tconst], [Tpp])
        BE = Buf(); Bec = Buf(); BD = [Buf(), Buf()]
        E = v3(BE.b[0:n, 0:8 * n], 8); ecum = v3(Bec.b[:, 0:8 * n], 8)
        for g in range(2):
            pp, Tpp = pR[g]
            Dm = v3(BD[g].f[0:n, 0:4 * n], 4)
            tt('dve', Dm, v3(pp[0:n, 0:4 * n], 4), cums[:, 4 * g:4 * g + 4].unsqueeze(2).broadcast_to([n, 4, n]), ALU.subtract,
               [Tpp, Tsm], [BD[g].t])
            ts('dve', Dm, Dm, 0.0, None, ALU.min, ALU.bypass, [BD[g].t], [BD[g].t])
            act(E[:, 4 * g:4 * g + 4, :], Dm, AF.Exp, [BD[g].t], [BE.t])
            act(ecum[:, 4 * g:4 * g + 4, :], v3(pp[:, 0:4 * n], 4), AF.Exp, [Tpp], [Bec.t])
        BD[0].free(); BD[1].free()
        pCB, TpCB = bank()
        for g in range(2):
            mm(pCB[0:n, g * n:(g + 1) * n], BC[:, g, 0:n], BC[:, 2 + g, 0:n], True, True, [Bxb.t], [TpCB])
        Bcb = Buf(1024)
        CBm = v3(Bcb.f[0:n, 0:2 * n], 2)
        tt('dve', CBm, v3(pCB[0:n, 0:2 * n], 2), tri_f[0:n, 0:n].unsqueeze(1).broadcast_to([n, 2, n]), ALU.mult, [TpCB, Tconst], [Bcb.t])
        BW = Buf(); BCs = Buf()
        Wm = v3(BW.b[0:n, 0:8 * n], 8); Cs = v3(BCs.b[:, 0:8 * n], 8)
        for g in range(2):
            tt('dve', Wm[:, 4 * g:4 * g + 4, :], E[:, 4 * g:4 * g + 4, :], CBm[:, g:g + 1, :].broadcast_to([n, 4, n]),
               ALU.mult, [BE.t, Bcb.t], [BW.t])
            tt('dve', Cs[:, 4 * g:4 * g + 4, :], ecum[:, 4 * g:4 * g + 4, :], BC[:, 2 + g:3 + g, 0:n].broadcast_to([128, 4, n]),
               ALU.mult, [Bec.t, Bxb.t], [BCs.t])
        BE.free(); Bec.free(); Bcb.free()
        pX, TpX = bank()
        pXb = pX.bitcast(BF16)
        for k in range(4):
            tr(pXb[0:n, k * 128:(k + 1) * 128], xsb[:, k, 0:n], ident_b, [Bxb.t, Tconst], [TpX])
        for g in range(2):
            tr(pXb[0:n, 512 + g * 128:512 + (g + 1) * 128], BC[:, g, 0:n], ident_b, [Bxb.t, Tconst], [TpX])
        Bx = Buf(2048, True)
        xdt = Bx.b[0:n, 0:512]; xw = Bx.b[0:n, 512:1024]
        Bbt = Buf(512, True)
        Btok = Bbt.b[0:n, 0:256]
        tt('dve', xdt.rearrange("p (h q) -> p h q", h=8), pXb[0:n, 0:512].rearrange("p (h q) -> p h q", h=8),
           dt_tok.unsqueeze(2).broadcast_to([n, 8, 64]), ALU.mult, [TpX, Tsm], [Bx.t])
        tt('dve', xw.rearrange("p (h q) -> p h q", h=8), pXb[0:n, 0:512].rearrange("p (h q) -> p h q", h=8),
           wgt.unsqueeze(2).broadcast_to([n, 8, 64]), ALU.mult, [TpX, Tsm], [Bx.t])
        cp('act', Btok, pXb[0:n, 512:768], [TpX], [Bbt.t])
        pY, TpY = bank()
        for h in range(8):
            outp = pY[(h % 2) * 64:(h % 2) * 64 + 64, (h // 2) * n:(h // 2 + 1) * n]
            mm(outp, xdt[:, h * 64:(h + 1) * 64], Wm[:, h, :], True, False, [Bx.t, BW.t], [TpY])
            mm(outp, Hbf[:, h * 64:(h + 1) * 64], Cs[:, h, :], False, True, [THb, BCs.t], [TpY])
        pH, TpH = bank()
        for g in range(2):
            mm(pH[:, g * 256:(g + 1) * 256], Btok[:, g * 128:(g + 1) * 128], xw[:, g * 256:(g + 1) * 256], True, True,
               [Bbt.t, Bx.t], [TpH])
        Bth = Buf(2048, True)
        tt('dve', Bth.f.rearrange("p (h q) -> p h q", h=8), Hst.rearrange("p (h q) -> p h q", h=8),
           ecl.unsqueeze(2).broadcast_to([128, 8, 64]), ALU.mult, [TH, Tsm], [Bth.t])
        tt('dve', Hst, Bth.f, pH, ALU.add, [Bth.t, TpH], [TH])
        cp('act', Hbf, Hst, [TH], [THb])
        Bth.free(); BW.free(); BCs.free(); Bx.free(); Bbt.free()
        ssd_post(pY, TpY, n, lambda: xs[:, :, 0:n], Bxs.t, lambda: sz[:, :, 0:n], Bz.t, lambda: mixB[:, :, 0:n], BmB.t)
        if first:
            ssd_post(ysb.rearrange("p a b -> p (a b)"), Bys.t, 16, lambda: xs_s, Bxss.t, lambda: sz[:, :, 16:32], Bz.t,
                     lambda: mixB[:, :, 16:32], BmB.t, in_sbuf=True)
            Bys.free(); Bxss.free(); Bst2.free(); BstB.free()
        Bxs.free(); Bxb.free(); Bdt.free(); Bz.free()

        P.stream = None
        P.flush([S1, S2])
        end_section()
        bank_list[0] = list(range(8))
        if first:
            load_wout()
        if ci + 1 < 17:
            first_flag[0] = False
            prenorm[ci + 1] = pre_norm(ci + 1)
        for half in range(2):
            pb, Tpb = bank()
            for k in range(4):
                m = 4 * half + k
                for fc in range(8):
                    mm(pb[:, k * NT:(k + 1) * NT], wout_b[:, fc, m * 128:(m + 1) * 128], (mixA if fc < 4 else mixB)[:, fc % 4, :], fc == 0, fc == 7,
                       [Twout, BmA.t, BmB.t], [Tpb])
            tt('dve', hT[:, 4 * half:4 * half + 4, c0:c0 + NT], hT[:, 4 * half:4 * half + 4, c0:c0 + NT],
               v3(pb[:, 0:4 * NT], 4), ALU.add, hh + [Tpb], hh)
        BmA.free(); BmB.free()

    def hg_post(pO, TpO, n, sg_f, Tsg, out_f, Tout):
        B1 = Buf(1024); B2 = Buf(1024)
        sqo = B2.b[:, 0:4 * n]
        act(sqo, pO[:, 0:4 * n], AF.Square, [TpO], [B2.t])
        pss, Tpss = bank()
        mm(pss[:, 0:4 * n], ones_b, sqo, True, True, [B2.t, Tconst], [Tpss])
        rs = B2.f[:, 0:4 * n]
        act(rs, pss[:, 0:4 * n], AF.Ln, [Tpss], [B2.t], scale=1.0 / 128, bias=EPS)
        act(rs, rs, AF.Exp, [B2.t], [B2.t], scale=-0.5)
        t1 = v3(B1.f[:, 0:4 * n], 4)
        tt('dve', t1, v3(pO[:, 0:4 * n], 4), sg_f(0, 4), ALU.mult, [TpO, Tsg, B1.t], [B1.t])
        tt('dve', out_f(), t1, v3(rs, 4), ALU.mult, [B1.t, B2.t], [Tout])
        B1.free(); B2.free()

    def ssd_post(pY, TpY, n, xs_f, Txs, sz_f, Tsz, out_f, Tout, in_sbuf=False):
        B1 = Buf(); B2 = Buf(); B3 = Buf(1024)
        yv = v3(B1.f[:, 0:4 * n], 4)
        tt('dve', yv, xs_f(), Dexp.unsqueeze(2).broadcast_to([128, 4, n]), ALU.mult, [Txs, Tconst], [B1.t])
        tt('dve', yv, yv, v3(pY[:, 0:4 * n], 4), ALU.add, [B1.t, TpY], [B1.t])
        tt('dve', yv, yv, sz_f(), ALU.mult, [B1.t, Tsz], [B1.t])
        sqy = v3(B2.b[:, 0:4 * n], 4)
        act(sqy, yv, AF.Square, [B1.t], [B2.t])
        pss, Tpss = bank()
        for g in range(2):
            for i in range(2):
                mm(pss[:, g * n:(g + 1) * n], ones_b, sqy[:, 2 * g + i, :], i == 0, i == 1, [B2.t, Tconst], [Tpss])
        rs = B3.f[:, 0:2 * n]
        act(rs, pss[:, 0:2 * n], AF.Ln, [Tpss], [B3.t], scale=1.0 / 256, bias=EPS)
        act(rs, rs, AF.Exp, [B3.t], [B3.t], scale=-0.5)
        tt('dve', yv, yv, snrm.unsqueeze(2).broadcast_to([128, 4, n]), ALU.mult, [B1.t, Tconst], [B1.t])
        for g in range(2):
            tt('dve', out_f()[:, 2 * g:2 * g + 2, :], yv[:, 2 * g:2 * g + 2, :],
               rs[:, g * n:(g + 1) * n].unsqueeze(1).broadcast_to([128, 2, n]), ALU.mult, [B1.t, B3.t], [Tout])
        B1.free(); B2.free(); B3.free()

    S_CTX = {}

    for ci in range(17):
        mixer_chunk(ci)
        chk('mix%d' % ci)
    stores.append(P.dma('sp', lambda e: e.dma_start(out=hgp_d.rearrange("h k v -> k h v"), in_=Sst), TS, r=[TS]))
    pb, Tpb = bank()
    for fc in range(4):
        tr(pb[:, fc * 128:(fc + 1) * 128], Hst[:, fc * 128:(fc + 1) * 128], ident_f, [TH, Tconst], [Tpb])
    st, Tst = next_stage()
    cp('dve', st[:, 0:512], pb, [Tpb], [Tst])
    stores.append(P.dma('sp', (lambda st=st: lambda e: e.dma_start(out=ssmp_d.rearrange("(f hh) q n -> (hh q) f n", hh=2),
                                                               in_=v3(st[:, 0:512], 4)))(), Tst, r=[Tst]))
    pa, Tpa = bank(); pb2, Tpb2 = bank()
    for fc in range(8):
        pp, Tpp = (pa, Tpa) if fc < 4 else (pb2, Tpb2)
        tr(pp[0:3, (fc % 4) * 128:(fc % 4 + 1) * 128], raw[:, fc, 0:3], ident_f, [Traw, Tconst], [Tpp])
    st2, Tst2 = next_stage()
    cp('dve', st2[0:3, 0:512], pa[0:3, :], [Tpa], [Tst2])
    cp('dve', st2[0:3, 512:1024], pb2[0:3, :], [Tpb2], [Tst2])
    stores.append(P.dma('sp', (lambda st2=st2: lambda e: e.dma_start(out=convp_d, in_=st2[0:3, 0:1024]))(), Tst2, r=[Tst2]))
    P.barrier()
    stg_only0[0] = False
    chk('mixer')
    fstate = {}

    def final_post(ti, off):
        if 'b' not in fstate:
            fb = []
            for i in range(3):
                a = sbat(off, [128, 8, 128], BF16); off += 2048
                b_ = sbat(off, [128, 128], F32); off += 512
                c_ = sbat(off, [128, 8, 128], F32); off += 4096
                fb.append((a, b_, c_, T("fsq%d" % i), T("frs%d" % i), T("fy%d" % i)))
            assert off <= TOTAL, off
            fstate['b'] = fb
            fstate['k'] = 0
        bank_list[0] = [0, 1, 2, 3, 6, 7]

        def stage_a(ci):
            r0, n = xrows[ci]
            hh = [Th[ci]]
            fsq, frs, fy, Tfsq, Tfrs, Tfy = fstate['b'][ci % 3]
            act(fsq[:, :, 0:n], hT[:, :, r0:r0 + n], AF.Square, hh, [Tfsq])
            pb, Tpb = bank()
            for c in range(8):
                mm(pb[:, 0:n], ones_b, fsq[:, c, 0:n], c == 0, c == 7, [Tfsq, Tconst], [Tpb])
            act(frs[:, 0:n], pb[:, 0:n], AF.Ln, [Tpb], [Tfrs], scale=1.0 / D, bias=EPS)
            act(frs[:, 0:n], frs[:, 0:n], AF.Exp, [Tfrs], [Tfrs], scale=-0.5)
            for c in range(8):
                stt('dve', fy[:, c, 0:n], hT[:, c, r0:r0 + n], nw(3)[:, c:c + 1], frs[:, 0:n], ALU.mult, ALU.mult,
                    hh + [Tfrs, Tconst], [Tfy])

        def stage_b(ci):
            r0, n = xrows[ci]
            fsq, frs, fy, Tfsq, Tfrs, Tfy = fstate['b'][ci % 3]
            st, Tst = next_stage()
            for half in range(2):
                pb, Tpb = bank()
                for k in range(4):
                    tr(pb[0:n, k * 128:(k + 1) * 128], fy[:, 4 * half + k, 0:n], ident_f, [Tfy, Tconst], [Tpb])
                cp('act' if half == 0 else 'dve', st[0:n, 512 * half:512 * half + 512], pb[0:n, :], [Tpb], [Tst])
            stores.append(P.dma('sp', (lambda st=st, r0=r0, n=n: lambda e: e.dma_start(out=y_d[r0:r0 + n, :], in_=st[0:n, 0:1024]))(),
                                Tst, r=[Tst]))

        L = TILE_CHUNKS[ti]
        stage_a(L[0])
        for k in range(1, len(L)):
            stage_a(L[k])
            stage_b(L[k - 1])
        stage_b(L[-1])
        bank_list[0] = list(range(8))

    ffn(2, w2g, w2u, w2d, post=final_post)


_NC_CACHE = {}


def kernel(x_prompt, x_sample, state_hgrn, state_ssm, state_conv, meta_tokens, lb_logits,
           norm_ffn1, w_ffn1_gate, w_ffn1_up, w_ffn1_down, norm_mix, w_in, hg_norm, conv_w, conv_b,
           dt_bias, a_log, d_skip, ssm_norm, w_out, norm_ffn2, w_ffn2_gate, w_ffn2_up, w_ffn2_down,
           norm_final):
    f = lambda a: np.ascontiguousarray(np.asarray(a, dtype=np.float32))
    x_prompt = f(x_prompt); x_sample = f(x_sample)
    state_hgrn = f(state_hgrn)[0]; state_ssm = f(state_ssm)[0]; state_conv = f(state_conv)[0]
    meta = f(meta_tokens)
    vec = np.zeros((128, 104), np.float32)
    pc = lambda v: f(v).reshape(-1, 128).T
    vec[:, 0:8] = pc(norm_ffn1[0]); vec[:, 8:16] = pc(norm_mix[0]); vec[:, 16:24] = pc(norm_ffn2[0]); vec[:, 24:32] = pc(norm_final)
    vec[:, 32:36] = pc(hg_norm[0]); vec[:, 36:40] = pc(ssm_norm[0])
    cwv = f(conv_w)[0]
    for j in range(4):
        vec[:, 40 + 8 * j:48 + 8 * j] = pc(cwv[j])
    vec[:, 72:80] = pc(conv_b[0])
    lbl = f(lb_logits)
    vec[:, 80:84] = pc(lbl[0]); vec[:, 84:88] = pc(lbl[1])
    vec[:, 88:92] = pc(np.repeat(f(d_skip)[0], 64))
    vec[:, 92:100] = np.broadcast_to(f(a_log)[0][None, :], (128, 8))
    vec[0:8, 100] = f(dt_bias)[0]
    vec[0:8, 101] = f(a_log)[0]
    cst = np.zeros((128, 1664), np.float32)
    cst[:, 0:128] = np.eye(128, dtype=np.float32)
    cst[:, 128:256] = np.triu(np.ones((128, 128), np.float32))
    cst[:, 256:384] = 1.0
    r = np.ones(512, np.float32); r[::64] = 0.0
    cst[:, 384:896] = r[None, :]
    ra = np.ones(128, np.float32); ra[::32] = 0.0
    cst[:, 896:1024] = ra[None, :]
    for h in range(8):
        cst[h, 1024 + 64 * h:1024 + 64 * (h + 1)] = 1.0
    cst[:, 1536:1664] = -0.5
    if 'nc' not in _NC_CACHE:
        _NC_CACHE['nc'] = build_nc()
    nc = _NC_CACHE['nc']
    shared = dict(w1g=f(w_ffn1_gate)[0], w1u=f(w_ffn1_up)[0], w1d=f(w_ffn1_down)[0],
                  w2g=f(w_ffn2_gate)[0], w2u=f(w_ffn2_up)[0], w2d=f(w_ffn2_down)[0],
                  w_in=f(w_in)[0], w_out=f(w_out)[0], vecs=vec, consts=cst)
    in_maps = []
    for c in range(8):
        xs_ = x_sample[16 * c:16 * c + 16, 0, :]
        m = dict(shared)
        m['xin'] = np.ascontiguousarray(np.concatenate([meta, xs_, x_prompt[c]], axis=0))
        m['s_hg'] = np.ascontiguousarray(state_hgrn[16 * c:16 * c + 16])
        m['s_ssm'] = np.ascontiguousarray(state_ssm[16 * c:16 * c + 16])
        m['s_conv'] = np.ascontiguousarray(state_conv[16 * c:16 * c + 16])
        in_maps.append(m)
    res = run_bass_kernel_spmd(nc, in_maps, core_ids=list(range(8)))
    R = res.results
    y_prompt = np.stack([R[c]['y'][32:] for c in range(8)], 0)
    y_sample = np.concatenate([R[c]['y'][16:32] for c in range(8)], 0)[:, None, :]
    hgrn_prompt = np.stack([R[c]['hg_p'] for c in range(8)], 0)[None]
    ssm_prompt = np.stack([R[c]['ssm_p'] for c in range(8)], 0)[None]
    conv_prompt = np.stack([R[c]['conv_p'] for c in range(8)], 0)[None]
    hgrn_sample = np.concatenate([R[c]['hg_s'] for c in range(8)], 0)[None]
    ssm_sample = np.concatenate([R[c]['ssm_s'] for c in range(8)], 0)[None]
    conv_sample = np.concatenate([R[c]['conv_s'] for c in range(8)], 0)[None]
    return tuple(np.ascontiguousarray(a, dtype=np.float32) for a in
                 (y_prompt, y_sample, hgrn_prompt, ssm_prompt, conv_prompt, hgrn_sample, ssm_sample, conv_sample))
```

```python
import contextlib
import numpy as np
import concourse.bass as bass
import concourse.mybir as mybir
from concourse.bass_utils import run_bass_kernel_spmd

F32 = mybir.dt.float32
BF16 = mybir.dt.bfloat16
ALU = mybir.AluOpType
AF = mybir.ActivationFunctionType

NCOL = 2080
D = 1024
DFF = 2816
DIN = 3592
EPS = 1e-6


class T:
    __slots__ = ('name', 'lw', 'rd', 'rd_dma', 'semcnt', 'excl', 'parents')

    def __init__(self, name, excl=False):
        self.name = name
        self.excl = excl
        self.parents = None
        self.lw = None
        self.rd = {}
        self.rd_dma = []
        self.semcnt = 0


class Op:
    __slots__ = ('eng', 'fn', 'deps', 'sig', 'ticket', 'dma', 'dsem', 'dval', 'seq')


class Proxy:
    __slots__ = ('op',)


def _resolve(t):
    if t.parents:
        ps_ = t.parents
        t.parents = None
        for p in ps_:
            _resolve(p)
            cand = list(p.rd.values()) + ([p.lw] if p.lw is not None else [])
            for op in cand:
                if op.dma:
                    t.rd_dma.append(op)
                else:
                    cur = t.rd.get(op.eng)
                    if cur is None or cur.seq < op.seq:
                        t.rd[op.eng] = op
            t.rd_dma.extend(p.rd_dma)


class Prog:
    ENGS = ('pe', 'act', 'dve', 'pool', 'sp')

    def __init__(self, nc):
        self.nc = nc
        self.ops = {e: [] for e in self.ENGS}
        self.pending_dma = []
        self.stream = None
        self.nseq = 0

    def _mk(self, eng, fn, r, w, dma, extra):
        for t in list(r) + list(w):
            _resolve(t)
        op = Op()
        self.nseq += 1
        op.seq = self.nseq
        op.eng = eng
        op.fn = fn
        op.dma = dma
        op.sig = False
        op.ticket = 0
        op.dsem = None
        op.dval = 0
        deps = set()
        xr = [t for t in r if t.excl]
        if xr:
            r = [t for t in r if not t.excl]
            w = list(w) + [t for t in xr if t not in w]
        for t in r:
            if t.lw is not None:
                deps.add(t.lw)
        for t in w:
            if t.lw is not None:
                deps.add(t.lw)
            deps.update(t.rd.values())
            deps.update(t.rd_dma)
        for d in extra:
            deps.add(d)
        op.deps = [d for d in deps if not (d.eng == 'pe' and eng == 'pe' and not d.dma and not dma)]
        for d in op.deps:
            d.sig = True
        for t in r:
            if dma:
                t.rd_dma.append(op)
            else:
                t.rd[eng] = op
        for t in w:
            t.lw = op
            t.rd = {}
            t.rd_dma = []
        self.ops[eng].append(op)
        return op

    def add(self, eng, fn, r=(), w=(), extra=()):
        if self.stream is not None:
            self.stream.append(('add', eng, fn, list(r), list(w), None, None))
            return None
        extra = [x.op if isinstance(x, Proxy) else x for x in extra]
        return self._mk(eng, fn, r, w, False, extra)

    def flush(self, streams):
        assert self.stream is None
        idx = [0] * len(streams)
        cum = []
        for s in streams:
            c, acc = [], 0.0
            for rec in s:
                c.append(acc)
                acc += 0.3 if rec[1] in ('pe', 'sp') or rec[0] == 'dma' else 1.0
            cum.append((c, max(acc, 1e-9)))
        while True:
            best = None
            for i, s in enumerate(streams):
                if idx[i] < len(s):
                    frac = cum[i][0][idx[i]] / cum[i][1]
                    if best is None or frac < best[0]:
                        best = (frac, i)
            if best is None:
                break
            i = best[1]
            kind, eng, fn, r, w, semtile, proxy = streams[i][idx[i]]
            idx[i] += 1
            if kind == 'add':
                self._mk(eng, fn, r, w, False, ())
            else:
                proxy.op = self.dma(eng, fn, semtile, r, w)

    def dma(self, eng, fn, semtile, r=(), w=(), extra=()):
        if self.stream is not None:
            px = Proxy()
            self.stream.append(('dma', eng, fn, list(r), list(w), semtile, px))
            return px
        op = self._mk(eng, fn, r, w, True, extra)
        semtile.semcnt += 16
        op.dsem = semtile
        op.dval = semtile.semcnt
        self.pending_dma.append(op)
        return op

    def barrier(self):
        last = [self.ops[e][-1] for e in self.ENGS if self.ops[e]]
        extra = last + self.pending_dma
        self.pending_dma = []
        for e in self.ENGS:
            self._mk(e, lambda eng: eng.nop(), (), (), False, extra)

    def emit(self):
        nc = self.nc
        with contextlib.ExitStack() as es:
            engsem = {e: es.enter_context(nc.semaphore('s_' + e)) for e in self.ENGS}
            tiles = []
            seen_t = set()
            for e in self.ENGS:
                for op in self.ops[e]:
                    if op.dma and id(op.dsem) not in seen_t:
                        seen_t.add(id(op.dsem))
                        tiles.append(op.dsem)
            assert len(tiles) <= 90, len(tiles)
            tsem = {id(t): es.enter_context(nc.semaphore('d_%d' % i)) for i, t in enumerate(tiles)}
            for e in self.ENGS:
                cnt = 0
                for op in self.ops[e]:
                    if not op.dma and op.sig:
                        cnt += 1
                        op.ticket = cnt
            block = es.enter_context(nc.Block())

            def body(ename):
                def run(engine):
                    seen = {}
                    for op in self.ops[ename]:
                        need = {}
                        for d in op.deps:
                            if d.dma:
                                key = tsem[id(d.dsem)]
                                val = d.dval
                            else:
                                key = engsem[d.eng]
                                val = d.ticket
                            kk = key.num
                            if kk not in need or need[kk][1] < val:
                                need[kk] = (key, val)
                        for kk, (key, val) in need.items():
                            if seen.get(kk, 0) < val:
                                engine.wait_ge(key, val)
                                seen[kk] = val
                        ins = op.fn(engine)
                        if op.dma:
                            ins.then_inc(tsem[id(op.dsem)], 16)
                        elif op.sig:
                            ins.then_inc(engsem[ename], 1)
                return run
            block.tensor(body('pe'))
            block.scalar(body('act'))
            block.vector(body('dve'))
            block.gpsimd(body('pool'))
            block.sync(body('sp'))


class _Stop(Exception):
    pass


def build_nc(stop=None):
    nc = bass.Bass("TRN2", target_bir_lowering=False)
    P = Prog(nc)
    stores = []

    def chk(tag):
        if stop == tag:
            raise _Stop()
    try:
        _build_body(nc, P, stores, chk)
    except _Stop:
        pass
    P.add('sp', lambda e: e.nop(), extra=stores)
    P.emit()
    return nc


def _build_body(nc, P, stores, chk):

    def din(name, shape):
        return nc.dram_tensor(name, list(shape), F32, kind="ExternalInput").ap()

    def dout(name, shape):
        return nc.dram_tensor(name, list(shape), F32, kind="ExternalOutput").ap()

    xin = din("xin", [NCOL, D])
    s_hg = din("s_hg", [16, 4, 128, 128])
    s_ssm = din("s_ssm", [16, 8, 64, 128])
    s_conv = din("s_conv", [16, 3, 1024])
    w1g = din("w1g", [D, DFF]); w1u = din("w1u", [D, DFF]); w1d = din("w1d", [DFF, D])
    w2g = din("w2g", [D, DFF]); w2u = din("w2u", [D, DFF]); w2d = din("w2d", [DFF, D])
    w_in = din("w_in", [D, DIN]); w_out = din("w_out", [D, D])
    vecs_d = din("vecs", [128, 104])
    consts_d = din("consts", [128, 1664])
    y_d = dout("y", [NCOL, D])
    hgp_d = dout("hg_p", [4, 128, 128])
    ssmp_d = dout("ssm_p", [8, 64, 128])
    convp_d = dout("conv_p", [3, 1024])
    hgs_d = dout("hg_s", [16, 4, 128, 128])
    ssms_d = dout("ssm_s", [16, 8, 64, 128])
    convs_d = dout("conv_s", [16, 3, 1024])

    BASE = 16512
    TOTAL = 212832
    cnt = [0]

    def sbat(off, shape, dt):
        cnt[0] += 1
        return nc.alloc_sbuf_tensor_at("t%d" % cnt[0], list(shape), dt, offset=BASE + off).ap()

    hT = sbat(0, [128, 8, NCOL], F32)
    Th = [T("h%d" % i) for i in range(17)]

    def hts(c0, n):
        out = []
        for i in range(17):
            a, b = (0, 32) if i == 0 else (32 + 128 * (i - 1), 32 + 128 * i)
            if a < c0 + n and b > c0:
                out.append(Th[i])
        return out
    o = 66560
    consts = sbat(o, [128, 1664], F32); o += 6656
    ident_f = consts[:, 0:128]; tri_f = consts[:, 128:256]; ones_f = consts[:, 256:384]
    reset = consts[:, 384:896]; resetA = consts[:, 896:1024]; E8 = consts[0:8, 1024:1536]; mhalf = consts[:, 1536:1664]
    vecs = sbat(o, [128, 104], F32); o += 416
    cb16 = sbat(o, [128, 256], BF16); o += 512
    ident_b = cb16[:, 0:128]; ones_b = cb16[:, 128:256]
    dv = sbat(o, [128, 64], F32); o += 256
    c1 = dv[:, 0:4]; c0_ = dv[:, 4:8]; a_bc = dv[:, 8:16]; a_col = dv[0:8, 16:17]; dvt = dv[:, 20:32]
    Tconst = T("const")
    stage = []
    Tstage = []
    for i in range(2):
        stage.append(sbat(o, [128, 2048], F32)); o += 8192
        Tstage.append(T("stage%d" % i))
    PH = o
    stg_i = [0]
    stg_only0 = [False]

    def next_stage():
        i = 0 if stg_only0[0] else stg_i[0] % 2
        stg_i[0] += 1
        return stage[i], Tstage[i]

    ps = [nc.alloc_psum_tensor("ps%d" % i, [128, 512], F32).ap() for i in range(8)]
    Tps = [T("ps%d" % i, excl=True) for i in range(8)]

    def act(out, in_, func, r, w, scale=1.0, bias=0.0):
        P.add('act', lambda e: e.activation(out=out, in_=in_, func=func, bias=bias, scale=scale), r, w)

    def tt(eng, out, a, b, op, r, w):
        P.add(eng, lambda e: e.tensor_tensor(out=out, in0=a, in1=b, op=op), r, w)

    def stt(eng, out, a, sc, b, op0, op1, r, w):
        P.add(eng, lambda e: e.scalar_tensor_tensor(out=out, in0=a, scalar=sc, in1=b, op0=op0, op1=op1), r, w)

    def ts(eng, out, a, s1, s2, op0, op1, r, w):
        P.add(eng, lambda e: e.tensor_scalar(out=out, in0=a, scalar1=s1, scalar2=s2, op0=op0, op1=op1), r, w)

    def cp(eng, out, in_, r, w):
        if eng == 'act':
            P.add('act', lambda e: e.copy(out=out, in_=in_), r, w)
        else:
            P.add(eng, lambda e: e.tensor_copy(out=out, in_=in_), r, w)

    def mm(out, lhsT, rhs, start, stop, r, w):
        P.add('pe', lambda e: e.matmul(out, lhsT, rhs, start=start, stop=stop), r, w)

    def tr(out, in_, idn, r, w):
        P.add('pe', lambda e: e.transpose(out, in_, idn), r, w)

    def ld(out, in_, semT, r=(), w=(), q='sp'):
        return P.dma(q, lambda e: e.dma_start(out=out, in_=in_), semT, r, w)

    def v3(ap2d, a):
        return ap2d.rearrange("p (a b) -> p a b", a=a)

    ld(consts, consts_d, Tconst, w=[Tconst])
    ld(vecs, vecs_d, Tconst, w=[Tconst])
    cp('dve', ident_b, ident_f, [Tconst], [Tconst])
    cp('dve', ones_b, ones_f, [Tconst], [Tconst])
    tt('dve', dvt[:, 0:4], vecs[:, 84:88], vecs[:, 80:84], ALU.subtract, [Tconst], [Tconst])
    act(dvt[:, 4:8], dvt[:, 0:4], AF.Exp, [Tconst], [Tconst])
    ts('dve', dvt[:, 4:8], dvt[:, 4:8], 1.0, None, ALU.add, ALU.bypass, [Tconst], [Tconst])
    P.add('dve', lambda e: e.reciprocal(out=dvt[:, 8:12], in_=dvt[:, 4:8]), [Tconst], [Tconst])
    ts('dve', c1, dvt[:, 8:12], -0.5, 0.5, ALU.mult, ALU.add, [Tconst], [Tconst])
    tt('dve', c0_, dvt[:, 8:12], c1, ALU.add, [Tconst], [Tconst])
    act(a_bc, vecs[:, 92:100], AF.Exp, [Tconst], [Tconst])
    ts('dve', a_bc, a_bc, -1.0, None, ALU.mult, ALU.bypass, [Tconst], [Tconst])
    act(a_col, vecs[0:8, 101:102], AF.Exp, [Tconst], [Tconst])
    ts('dve', a_col, a_col, -1.0, None, ALU.mult, ALU.bypass, [Tconst], [Tconst])
    nw = lambda k: vecs[:, 8 * k:8 * k + 8]
    hgn = vecs[:, 32:36]; snrm = vecs[:, 36:40]
    cw = lambda j: vecs[:, 40 + 8 * j:48 + 8 * j]
    cbias = vecs[:, 72:80]; Dexp = vecs[:, 88:92]; dtb = vecs[0:8, 100:101]

    xrows = [(0, 32)] + [(32 + 128 * i, 128) for i in range(16)]

    xst = [sbat(TOTAL - 8192, [128, 1024], F32), sbat(TOTAL - 4096, [128, 1024], F32)]
    Txst = [T("xst0"), T("xst1")]

    def load_chunk(ci):
        r0, n = xrows[ci]
        st, Tst = xst[ci % 2], Txst[ci % 2]
        ld(st[0:n, 0:1024], xin[r0:r0 + n, :], Tst, w=[Tst])
        for half in range(2):
            bk = 4 + (2 * ci + half) % 4
            for k in range(4):
                c = half * 4 + k
                tr(ps[bk][:, k * 128:k * 128 + n], st[0:n, c * 128:(c + 1) * 128], ident_f[0:n, 0:n], [Tst, Tconst], [Tps[bk]])
            eng = 'act' if half == 0 else 'dve'
            cp(eng, hT[:, half * 4:half * 4 + 4, r0:r0 + n], v3(ps[bk], 4)[:, :, 0:n], [Tps[bk]], [Th[ci]])

    CT = [(0, 416), (416, 416), (832, 416), (1248, 416), (1664, 416)]
    TILE_CHUNKS = [[0, 1, 2, 3], [4, 5, 6], [7, 8, 9], [10, 11, 12], [13, 14, 15, 16]]

    def prep_x(ti):
        for ci in TILE_CHUNKS[ti]:
            load_chunk(ci)

    def ffn(nk, wg, wu, wd, prep=None, post=None):
        oo = PH
        xn = sbat(oo, [128, 8, NCOL], BF16); oo += 33280
        hid = []
        for i in range(2):
            hid.append(sbat(oo, [128, 2, NCOL], BF16)); oo += 8320
        wb = []
        for i in range(7):
            wb.append(sbat(oo, [128, 2048], BF16)); oo += 4096
        sqb = sbat(oo, [128, 8, 512], BF16); oo += 8192
        rstd = sbat(oo, [128, 512], F32); oo += 2048
        tln = sbat(oo, [128, 512], F32); oo += 2048
        sil = []
        for i in range(2):
            sil.append(sbat(oo, [128, 512], F32)); oo += 2048
        assert BASE + oo <= BASE + TOTAL, oo
        Txn = [T("xn%d" % i) for i in range(5)]
        Thid = [T("hid%d" % i) for i in range(2)]
        Twb = [T("wb%d" % i) for i in range(7)]
        Tsqb = T("sqb"); Trstd = T("rstd"); Ttln = T("tln"); Tsil = [T("sil0"), T("sil1")]
        def norm_tile(ti):
            c0, n = CT[ti]
            hh = hts(c0, n)
            act(sqb[:, :, 0:n], hT[:, :, c0:c0 + n], AF.Square, hh, [Tsqb])
            for c in range(8):
                mm(ps[4][:, 0:n], ones_b, sqb[:, c, 0:n], c == 0, c == 7, [Tsqb, Tconst], [Tps[4]])
            act(tln[:, 0:n], ps[4][:, 0:n], AF.Ln, [Tps[4]], [Ttln], scale=1.0 / D, bias=EPS)
            act(rstd[:, 0:n], tln[:, 0:n], AF.Exp, [Ttln], [Trstd], scale=-0.5)
            for c in range(8):
                stt('dve', xn[:, c, c0:c0 + n], hT[:, c, c0:c0 + n], nw(nk)[:, c:c + 1], rstd[:, 0:n],
                    ALU.mult, ALU.mult, hh + [Trstd, Tconst], [Txn[ti]])
        wgv = wg.rearrange("(c p) n -> p c n", p=128)
        wuv = wu.rearrange("(c p) n -> p c n", p=128)
        wdv = wd.rearrange("(j p) n -> p j n", p=128)
        wslot = [0, 0]
        nld = [0]

        def load_w(src_ap, shape3):
            st, Tst = next_stage()
            if shape3[0] == 8:
                i = wslot[0] % 4
                wslot[0] += 1
            else:
                i = 4 + wslot[1] % 3
                wslot[1] += 1
            ld(st.rearrange("p (a b) -> p a b", a=shape3[0]), src_ap, Tst, w=[Tst])
            nld[0] += 1
            cp('dve' if nld[0] <= 6 else 'pool', wb[i], st, [Tst], [Twb[i]])
            return i

        def load_jp(jp):
            a = load_w(wgv[:, :, jp * 256:(jp + 1) * 256], (8, 256))
            b = load_w(wuv[:, :, jp * 256:(jp + 1) * 256], (8, 256))
            c = load_w(wdv[:, 2 * jp:2 * jp + 2, :], (2, 1024))
            return (a, b, c)
        wq = [load_jp(0)]
        for ti in range(len(CT)):
            if prep is not None:
                prep(ti)
        norm_tile(0)
        wq.append(load_jp(1))
        dbk = [0]

        def up_phase(jp, ti):
            ia, ib, ic = wq[jp]
            wga = v3(wb[ia], 8); wua = v3(wb[ib], 8)
            hb = hid[jp % 2]; Thb = Thid[jp % 2]
            c0, n = CT[ti]
            for jj in range(2):
                pg = ps[2 * jj]; pu = ps[2 * jj + 1]
                for c in range(8):
                    mm(pg[:, 0:n], wga[:, c, jj * 128:(jj + 1) * 128], xn[:, c, c0:c0 + n], c == 0, c == 7,
                       [Twb[ia], Txn[ti]], [Tps[2 * jj]])
                for c in range(8):
                    mm(pu[:, 0:n], wua[:, c, jj * 128:(jj + 1) * 128], xn[:, c, c0:c0 + n], c == 0, c == 7,
                       [Twb[ib], Txn[ti]], [Tps[2 * jj + 1]])
                act(sil[jj][:, 0:n], pg[:, 0:n], AF.Silu, [Tps[2 * jj]], [Tsil[jj]])
                tt('dve', hb[:, jj, c0:c0 + n], sil[jj][:, 0:n], pu[:, 0:n], ALU.mult, [Tsil[jj], Tps[2 * jj + 1]], [Thb])

        def down_phase(jp, ti):
            ia, ib, ic = wq[jp]
            wda = v3(wb[ic], 2)
            hb = hid[jp % 2]; Thb = Thid[jp % 2]
            c0, n = CT[ti]
            hh = hts(c0, n)
            for m in range(8):
                bk = 4 + dbk[0] % (2 if (jp == 10 and post is not None) else 4)
                dbk[0] += 1
                for jj in range(2):
                    mm(ps[bk][:, 0:n], wda[:, jj, m * 128:(m + 1) * 128], hb[:, jj, c0:c0 + n], jj == 0, jj == 1,
                       [Twb[ic], Thb], [Tps[bk]])
                stt('dve', hT[:, m, c0:c0 + n], ps[bk][:, 0:n], 0.5, hT[:, m, c0:c0 + n], ALU.mult, ALU.add,
                    [Tps[bk]] + hh, hh)

        for jp in range(11):
            for ti in range(len(CT)):
                if jp == 0 and ti + 1 < len(CT):
                    norm_tile(ti + 1)
                up_phase(jp, ti)
                if jp > 0:
                    down_phase(jp - 1, ti)
            if jp + 2 < 11:
                wq.append(load_jp(jp + 2))
        for ti in range(len(CT)):
            down_phase(10, ti)
            if post is not None:
                post(ti, oo)

    ffn(0, w1g, w1u, w1d, prep=prep_x)
    P.barrier()
    chk('ffn1')

    FIXED_MIX = 57472 + 16384 + 4192 + 2048 + 1024 + 2048 + 1024 + 384
    ARENA_OFF = PH - 8192
    ARENA_BYTES = (TOTAL - PH - FIXED_MIX) // 512 * 512 + 8192
    oo = ARENA_OFF + ARENA_BYTES
    win_b = sbat(oo, [128, 8, DIN], BF16); oo += 57472
    wout_b = sbat(oo, [128, 8, D], BF16); oo += 16384
    Twin = [T("win%d" % b) for b in range(15)]; Twout = T("wout")
    winv = w_in.rearrange("(c p) n -> p c n", p=128)
    woutv = w_out.rearrange("(c p) n -> p c n", p=128)
    for b in [2, 3, 10, 11, 12, 13, 0, 1, 6, 7, 8, 9, 4, 5, 14]:
        c0 = b * 256
        n = min(256, DIN - c0)
        st, Tst = next_stage()
        sv = st[:, 0:8 * n].rearrange("p (a b) -> p a b", a=8)
        q_ = 'sp' if Tst is Tstage[0] else 'act'
        ld(sv, winv[:, :, c0:c0 + n], Tst, w=[Tst], q=q_)
        cp('dve' if q_ == 'sp' else 'act', win_b[:, :, c0:c0 + n], sv, [Tst], [Twin[b]])
    def load_wout():
        for b in range(4):
            st, Tst = next_stage()
            sv = v3(st, 8)
            ld(sv, woutv[:, :, b * 256:(b + 1) * 256], Tst, w=[Tst])
            cp('act' if b % 2 == 0 else 'dve', wout_b[:, :, b * 256:(b + 1) * 256], sv, [Tst], [Twout])
    chk('mw')
    stg_only0[0] = True
    raw = sbat(oo, [128, 8, 131], F32); oo += 4192
    Sst = sbat(oo, [128, 4, 128], F32); oo += 2048
    Sbf = sbat(oo, [128, 4, 128], BF16); oo += 1024
    Hst = sbat(oo, [128, 512], F32); oo += 2048
    Hbf = sbat(oo, [128, 512], BF16); oo += 1024
    sm = sbat(oo, [128, 96], F32); oo += 384
    assert oo <= TOTAL, oo
    Traw = T("raw"); TS = T("S"); TSb = T("Sbf"); TH = T("H"); THb = T("Hbf"); Tsm = T("sm")
    UNIT = 512
    A_OFF = ARENA_OFF
    A_N = ARENA_BYTES // UNIT
    a_free = [True] * A_N
    a_last = [None] * A_N
    for u in range(8192 // UNIT):
        a_last[u] = Tstage[1]
    a_pend = {}
    peak = [0]

    first_flag = [False]

    class Buf:
        def __init__(self, nbytes=2048, fixed=False):
            if first_flag[0] and not fixed:
                nbytes = max(UNIT, nbytes // 4)
            k = (nbytes + UNIT - 1) // UNIT
            sid = id(P.stream) if P.stream is not None else None
            own = a_pend.get(sid, set()) if sid is not None else set()
            run = None
            u = 0
            while u + k <= A_N:
                ok = True
                for j in range(k):
                    if not (a_free[u + j] or (u + j) in own):
                        ok = False
                        u = u + j + 1
                        break
                if ok:
                    run = list(range(u, u + k))
                    break
            assert run is not None, "arena full"
            self.t = T("buf")
            par = []
            for x in run:
                if a_last[x] is not None and a_last[x] not in par:
                    par.append(a_last[x])
                a_last[x] = self.t
                a_free[x] = False
                own.discard(x)
            self.t.parents = par
            self.run = run
            peak[0] = max(peak[0], run[-1] + 1)
            off = A_OFF + run[0] * UNIT
            self.f = sbat(off, [128, k * UNIT // 4], F32)
            self.b = sbat(off, [128, k * UNIT // 2], BF16)

        def free(self):
            sid = id(P.stream) if P.stream is not None else None
            if sid is None:
                for x in self.run:
                    a_free[x] = True
            else:
                a_pend.setdefault(sid, set()).update(self.run)

    def end_section():
        for s in a_pend.values():
            for x in s:
                a_free[x] = True
        a_pend.clear()

    P.add('pool', lambda e: e.memset(raw, 0.0), (), [Traw])
    P.add('pool', lambda e: e.memset(Sst, 0.0), (), [TS])
    P.add('pool', lambda e: e.memset(Hst, 0.0), (), [TH])
    P.add('pool', lambda e: e.memset(Sbf, 0.0), (), [TSb])
    P.add('pool', lambda e: e.memset(Hbf, 0.0), (), [THb])

    chk('m0a')
    pbk = [0]
    bank_list = [list(range(8))]

    def bank():
        bl = bank_list[0]
        b = bl[pbk[0] % len(bl)]
        pbk[0] += 1
        return ps[b], Tps[b]

    def rms_pow(src_ps, Tsrc, n_feat, width, dst_f, Tdst):
        ts('dve', dst_f, src_ps, 1.0 / n_feat, EPS, ALU.mult, ALU.add, [Tsrc], [Tdst])
        tt('pool', dst_f, dst_f, mhalf[:, 0:1].broadcast_to([128, width]) if width != 128 else mhalf, ALU.pow, [Tdst, Tconst], [Tdst])

    prenorm = {}

    def pre_norm(ci):
        c0, NT = (0, 32) if ci == 0 else (32 + 128 * (ci - 1), 128)
        hh = [Th[ci]]
        Bxn = Buf(); Bsq = Buf(); Brs = Buf()
        xn_m = v3(Bxn.b[:, 0:8 * NT], 8)
        sqh = v3(Bsq.b[:, 0:8 * NT], 8)
        act(sqh, hT[:, :, c0:c0 + NT], AF.Square, hh, [Bsq.t])
        pb, Tpb = bank()
        for c in range(8):
            mm(pb[:, 0:NT], ones_b, sqh[:, c, :], c == 0, c == 7, [Bsq.t, Tconst], [Tpb])
        rstd = Brs.f[:, 0:NT]
        act(rstd, pb[:, 0:NT], AF.Ln, [Tpb], [Brs.t], scale=1.0 / D, bias=EPS)
        act(rstd, rstd, AF.Exp, [Brs.t], [Brs.t], scale=-0.5)
        for c in range(8):
            stt('dve', xn_m[:, c, :], hT[:, c, c0:c0 + NT], nw(1)[:, c:c + 1], rstd, ALU.mult, ALU.mult,
                hh + [Brs.t, Tconst], [Bxn.t])
        Bsq.free(); Brs.free()
        return Bxn, xn_m

    def mixer_chunk(ci):
        first = ci == 0
        first_flag[0] = first
        c0, NT = (0, 32) if first else (32 + 128 * (ci - 1), 128)
        TS_ = 16 if first else 128
        hh = [Th[ci]]
        Bxn, xn_m = prenorm.pop(ci) if ci in prenorm else pre_norm(ci)

        if first:
            chk('m0b')
        def proj(col0, nfc, M=128):
            pb, Tpb = bank()
            wts = [Twin[b] for b in range(col0 // 256, (col0 + (nfc - 1) * 128 + M - 1) // 256 + 1)]
            for k in range(nfc):
                for c in range(8):
                    mm(pb[0:M, k * NT:(k + 1) * NT], win_b[:, c, col0 + k * 128:col0 + k * 128 + M], xn_m[:, c, :],
                       c == 0, c == 7, wts + [Bxn.t], [Tpb])
            return pb, Tpb
        BF_ = Buf(); Bq = Buf(); Bg = Buf(); Bz = Buf()
        tmpF = v3(BF_.f[:, 0:4 * NT], 4); sq = v3(Bq.f[:, 0:4 * NT], 4)
        sg = v3(Bg.f[:, 0:4 * NT], 4); sz = v3(Bz.f[:, 0:4 * NT], 4)
        pb, Tpb = proj(512, 4)
        act(tmpF, v3(pb[:, 0:4 * NT], 4), AF.Tanh, [Tpb], [BF_.t], scale=0.5)
        for half in range(2):
            pb, Tpb = proj(2560 + 512 * half, 4)
            cp('act', raw[:, 4 * half:4 * half + 4, 3:3 + NT], v3(pb[:, 0:4 * NT], 4), [Tpb], [Traw])
        pb, Tpb = proj(0, 4)
        act(sq, v3(pb[:, 0:4 * NT], 4), AF.Silu, [Tpb], [Bq.t])
        pb, Tpb = proj(1536, 4)
        act(sg, v3(pb[:, 0:4 * NT], 4), AF.Silu, [Tpb], [Bg.t])
        pb, Tpb = proj(2048, 4)
        act(sz, v3(pb[:, 0:4 * NT], 4), AF.Silu, [Tpb], [Bz.t])
        if first:
            chk('m0c')
        if first:
            chk('m0d')
        if first:
            chk('m0e')
        Bv = Buf(1024, True)
        pb, Tpb = bank()
        for c in range(8):
            mm(pb[0:NT, :], xn_m[:, c, :], win_b[:, c, 1024:1536], c == 0, c == 7, [Twin[4], Twin[5], Bxn.t], [Tpb])
        v_tok = Bv.b[0:NT, 0:512]
        cp('act', v_tok, pb[0:NT, :], [Tpb], [Bv.t])
        if first:
            Bvf = Buf(2048, True)
            cp('dve', Bvf.f[0:32, :], pb[0:32, :], [Tpb], [Bvf.t])

        Bdt = Buf(1536, True)
        dtT = Bdt.f[0:8, 0:NT]
        pb, Tpb = proj(3584, 1, M=8)
        act(dtT, pb[0:8, 0:NT], AF.Exp, [Tpb, Tconst], [Bdt.t], bias=dtb)
        act(dtT, dtT, AF.Ln, [Bdt.t], [Bdt.t], bias=1.0)
        Bxn.free()
        BmA = Buf(1024); BmB = Buf(1024)
        mixA = v3(BmA.b[:, 0:4 * NT], 4); mixB = v3(BmB.b[:, 0:4 * NT], 4)
        S1 = []
        P.stream = S1
        bank_list[0] = [0, 1, 2, 3]
        BK = Buf(); BB = Buf(); Be = Buf(); Bqk = Buf()
        tmpK = v3(BK.f[:, 0:4 * NT], 4); tmpB = v3(BB.f[:, 0:4 * NT], 4); eb = v3(Be.f[:, 0:4 * NT], 4)
        qt = v3(Bqk.b[:, 0:4 * NT], 4); kt = v3(Bqk.b[:, 4 * NT:8 * NT], 4)
        bc4 = lambda col: col.unsqueeze(2).broadcast_to([128, 4, NT])
        tt('dve', tmpF, tmpF, bc4(c1), ALU.mult, [BF_.t, Tconst], [BF_.t])
        tt('dve', tmpF, tmpF, bc4(c0_), ALU.add, [BF_.t, Tconst], [BF_.t])
        ts('dve', tmpK, tmpF, -1.0, 1.0, ALU.mult, ALU.add, [BF_.t], [BK.t])
        tt('dve', sg, sg, bc4(hgn), ALU.mult, [Bg.t, Tconst], [Bg.t])
        if first:
            Bsg = Buf(512, True)
            fs = v3(Bsg.f[:, 0:64], 4); ks = v3(Bsg.f[:, 64:128], 4)
            cp('pool', fs, tmpF[:, :, 16:32], [BF_.t], [Bsg.t])
            cp('pool', ks, tmpK[:, :, 16:32], [BK.t], [Bsg.t])
        act(tmpF, tmpF, AF.Ln, [BF_.t], [BF_.t])
        rmask = resetA[:, 0:4 * NT] if first else reset
        P.add('dve', lambda e: e.tensor_tensor_scan(out=BB.f[:, 0:4 * NT], data0=rmask, data1=BF_.f[:, 0:4 * NT],
                                                    initial=0.0, op0=ALU.mult, op1=ALU.add), [BF_.t, Tconst], [BB.t])
        act(eb, tmpB, AF.Exp, [BB.t], [Be.t])
        act(tmpF, tmpB, AF.Exp, [BB.t], [BF_.t], scale=-1.0)
        tt('dve', kt, tmpK, tmpF, ALU.mult, [BK.t, BF_.t], [Bqk.t])
        tt('dve', qt, sq, eb, ALU.mult, [Bq.t, Be.t], [Bqk.t])
        BB.free()

        if first:
            po_s, Tpo_s = ps[7], Tps[7]
            BstA = Buf(2048, True)
            for b in range(16):
                if b % 2 == 0:
                    st, Tst = next_stage()
                    st = st[:, 0:512]
                else:
                    st, Tst = BstA.f[:, 0:512], BstA.t
                sv = v3(st, 4)
                ld(sv, s_hg[b].rearrange("h k v -> k h v"), Tst, w=[Tst], q='act')
                pV, TpV = bank()
                mm(pV[:, 0:512], ident_f[0:32, 16 + b:17 + b].broadcast_to([32, 128]), Bvf.f[0:32, :], True, True,
                   [Tconst, Bvf.t], [TpV])
                Bt = Buf(2048, True); Bt2 = Buf(2048, True)
                tt('dve', v3(Bt.f, 4), sv, fs[:, :, b:b + 1].broadcast_to([128, 4, 128]), ALU.mult, [Tst, Bsg.t], [Bt.t])
                tt('dve', v3(Bt2.f, 4), v3(pV, 4), ks[:, :, b:b + 1].broadcast_to([128, 4, 128]), ALU.mult, [TpV, Bsg.t], [Bt2.t])
                tt('dve', st, Bt2.f, Bt.f, ALU.add, [Bt.t, Bt2.t], [Tst])
                for h in range(4):
                    mm(po_s[:, h * 16 + b:h * 16 + b + 1], sv[:, h, :], sq[:, h, 16 + b:17 + b], True, True, [Tst, Bq.t], [Tpo_s])
                stores.append(P.dma('act', (lambda sv=sv, b=b: lambda e: e.dma_start(out=hgs_d[b].rearrange("h k v -> k h v"), in_=sv))(),
                                    Tst, r=[Tst]))
                Bt.free(); Bt2.free()
            Bsg.free(); Bvf.free(); BstA.free()

        subs = [(0, 16, 0)] if first else [(0, 64, 0), (64, 64, 64)]
        for (t0, n, pb0) in subs:
            PR = slice(pb0, pb0 + n)
            pS, TpS = bank()
            for h in range(4):
                mm(pS[PR, h * n:(h + 1) * n], kt[:, h, t0:t0 + n], qt[:, h, t0:t0 + n], True, True, [Bqk.t], [TpS])
            Bp = Buf(512)
            PT = v3(Bp.b[PR, 0:4 * n], 4)
            tt('dve', PT, v3(pS[PR, 0:4 * n], 4), tri_f[PR, pb0:pb0 + n].unsqueeze(1).broadcast_to([n, 4, n]), ALU.mult,
               [TpS, Tconst], [Bp.t])
            pK, TpK = bank()
            pKb = pK.bitcast(BF16)
            for h in range(4):
                tr(pKb[PR, h * 128:(h + 1) * 128], kt[:, h, t0:t0 + n], ident_b, [Bqk.t, Tconst], [TpK])
            Bkt = Buf(1024, True)
            kt_tok = Bkt.b[PR, 0:512]
            cp('act', kt_tok, pKb[PR, 0:512], [TpK], [Bkt.t])
            pO, TpO = bank()
            for h in range(4):
                mm(pO[:, h * n:(h + 1) * n], v_tok[PR, h * 128:(h + 1) * 128], PT[:, h, :], True, False, [Bv.t, Bp.t], [TpO])
                mm(pO[:, h * n:(h + 1) * n], Sbf[:, h, :], qt[:, h, t0:t0 + n], False, True, [TSb, Bqk.t], [TpO])
            pKV, TpKV = bank()
            for h in range(4):
                mm(pKV[:, h * 128:(h + 1) * 128], kt_tok[:, h * 128:(h + 1) * 128], v_tok[PR, h * 128:(h + 1) * 128], True, True,
                   [Bkt.t, Bv.t], [TpKV])
            Bts = Buf(2048, True)
            tmpS = v3(Bts.f, 4)
            tt('dve', tmpS, Sst, v3(pKV, 4), ALU.add, [TS, TpKV], [Bts.t])
            tt('dve', Sst, tmpS, eb[:, :, t0 + n - 1:t0 + n].broadcast_to([128, 4, 128]), ALU.mult, [Bts.t, Be.t], [TS])
            cp('act', Sbf, Sst, [TS], [TSb])
            Bts.free(); Bp.free(); Bkt.free()
            hg_post(pO, TpO, n, lambda h0, h1, _t0=t0, _n=n: sg[:, h0:h1, _t0:_t0 + _n],
                    Bg.t, lambda _t0=t0, _n=n: mixA[:, :, _t0:_t0 + _n], BmA.t)
        if first:
            hg_post(po_s, Tpo_s, 16, lambda h0, h1: sg[:, h0:h1, 16:32], Bg.t, lambda: mixA[:, :, 16:32], BmA.t)
        Bq.free(); Bg.free(); Be.free(); Bqk.free(); Bv.free(); BK.free(); BF_.free()
        S2 = []
        P.stream = S2
        bank_list[0] = [4, 5, 6] if first else [4, 5, 6, 7]
        Bacc = [Buf(), Buf()]
        Bxs = Buf(); Bxb = Buf()
        xs = v3(Bxs.f[:, 0:4 * NT], 4)
        xsb = v3(Bxb.b[:, 0:4 * NT], 4); BC = v3(Bxb.b[:, 4 * NT:8 * NT], 4)
        for half in range(2):
            eng = 'dve'
            acc = v3(Bacc[half].f[:, 0:4 * NT], 4)
            for k in range(4):
                fc = 4 * half + k
                ts(eng, acc[:, k, :], raw[:, fc, 0:NT], cw(0)[:, fc:fc + 1], cbias[:, fc:fc + 1], ALU.mult, ALU.add,
                   [Traw, Tconst], [Bacc[half].t])
                for j in range(1, 4):
                    stt(eng, acc[:, k, :], raw[:, fc, j:j + NT], cw(j)[:, fc:fc + 1], acc[:, k, :], ALU.mult, ALU.add,
                        [Traw, Tconst, Bacc[half].t], [Bacc[half].t])
        if first:
            Bst2 = Buf(4096, True)
            st, Tst = Bst2.f, Bst2.t
            ld(st[0:48, 0:1024], s_conv.rearrange("b j c -> (b j) c"), Tst, w=[Tst])
            stores.append(P.dma('sp', lambda e: e.dma_start(out=convs_d[:, 0:2, :], in_=s_conv[:, 1:3, :]), Tst, r=[Tst]))
            pSC, TpSC = bank()
            for fc in range(8):
                tr(pSC[:, fc * 48:(fc + 1) * 48], st[0:48, fc * 128:(fc + 1) * 128], ident_f[0:48, 0:48], [Tst, Tconst], [TpSC])
            Bsc = Buf(1536, True)
            cp('dve', Bsc.f[:, 0:384], pSC[:, 0:384], [TpSC], [Bsc.t])
            scv = Bsc.f[:, 0:384].rearrange("p (f b j) -> p f b j", f=8, b=16)
            for fc in range(8):
                half, k = fc // 4, fc % 4
                accs = v3(Bacc[half].f[:, 0:4 * NT], 4)[:, k, 16:32]
                ts('dve', accs, raw[:, fc, 3 + 16:3 + 32], cw(3)[:, fc:fc + 1], cbias[:, fc:fc + 1], ALU.mult, ALU.add,
                   [Traw, Tconst, Bacc[half].t], [Bacc[half].t])
                for j in range(3):
                    stt('dve', accs, scv[:, fc, :, j], cw(j)[:, fc:fc + 1], accs, ALU.mult, ALU.add,
                        [Bsc.t, Tconst, Bacc[half].t], [Bacc[half].t])
            Bsc.free()
        act(xs, v3(Bacc[0].f[:, 0:4 * NT], 4), AF.Silu, [Bacc[0].t], [Bxs.t])
        act(BC, v3(Bacc[1].f[:, 0:4 * NT], 4), AF.Silu, [Bacc[1].t], [Bxb.t])
        act(xsb, v3(Bacc[0].f[:, 0:4 * NT], 4), AF.Silu, [Bacc[0].t], [Bxb.t])
        Bacc[0].free(); Bacc[1].free()
        cp('pool', raw[:, :, 0:3], raw[:, :, TS_:TS_ + 3], [Traw], [Traw])

        if first:
            Bd = Buf(512, True); Bys = Buf(512, True); Bxss = Buf(512, True); BstB = Buf(2048, True)
            dtaT = Bdt.f[0:8, 256:272]
            ts('dve', dtaT, dtT[:, 16:32], a_col, None, ALU.mult, ALU.bypass, [Bdt.t, Tconst], [Bdt.t])
            pE, TpE = bank()
            for fc in range(4):
                mm(pE[:, fc * 16:(fc + 1) * 16], E8[:, fc * 128:(fc + 1) * 128], dtT[:, 16:32], True, True, [Tconst, Bdt.t], [TpE])
            for fc in range(4):
                mm(pE[:, 64 + fc * 16:64 + (fc + 1) * 16], E8[:, fc * 128:(fc + 1) * 128], dtaT, True, True, [Tconst, Bdt.t], [TpE])
            dec = v3(Bd.f[:, 0:64], 4); xdts = v3(Bd.f[:, 64:128], 4)
            act(dec, v3(pE[:, 64:128], 4), AF.Exp, [TpE], [Bd.t])
            tt('dve', xdts, xs[:, :, 16:32], v3(pE[:, 0:64], 4), ALU.mult, [Bxs.t, TpE], [Bd.t])
            ysb = v3(Bys.f[:, 0:64], 4)
            xs_s = v3(Bxss.f[:, 0:64], 4)
            cp('pool', xs_s, xs[:, :, 16:32], [Bxs.t], [Bxss.t])
            for b in range(16):
                if b % 2 == 0:
                    st, Tst = Bst2.f[:, 0:512], Bst2.t
                else:
                    st, Tst = BstB.f[:, 0:512], BstB.t
                sv = v3(st, 4)
                ld(sv, s_ssm[b].rearrange("(f hh) q n -> (hh q) f n", hh=2), Tst, w=[Tst], q='act')
                pBC, TpBC = bank()
                for i in range(4):
                    mm(pBC[:, i * 128:(i + 1) * 128], BC[:, i, 16 + b:17 + b].broadcast_to([128, 128]), ident_b, True, True,
                       [Bxb.t, Tconst], [TpBC])
                Bt = Buf(2048, True); Bt2 = Buf(2048, True)
                tt('dve', v3(Bt.f, 4), sv, dec[:, :, b:b + 1].broadcast_to([128, 4, 128]), ALU.mult, [Tst, Bd.t], [Bt.t])
                b4 = lambda ap2: ap2.rearrange("p (g n) -> p g n", g=2).unsqueeze(2).broadcast_to([128, 2, 2, 128])
                f4 = lambda ap: ap.rearrange("p (g i n) -> p g i n", g=2, i=2)
                xcol = xdts[:, :, b:b + 1].rearrange("p (g i) o -> p g i o", g=2).broadcast_to([128, 2, 2, 128])
                tt('dve', f4(Bt2.f), b4(pBC[:, 0:256]), xcol, ALU.mult, [TpBC, Bd.t], [Bt2.t])
                tt('dve', st, Bt2.f, Bt.f, ALU.add, [Bt.t, Bt2.t], [Tst])
                tt('dve', f4(Bt2.f), f4(st), b4(pBC[:, 256:512]), ALU.mult, [Tst, TpBC], [Bt2.t])
                P.add('dve', (lambda b=b, src=v3(Bt2.f, 4): lambda e: e.tensor_reduce(out=ysb[:, :, b], in_=src, axis=mybir.AxisListType.X, op=ALU.add))(),
                      [Bt2.t], [Bys.t])
                stores.append(P.dma('act', (lambda sv=sv, b=b: lambda e: e.dma_start(
                    out=ssms_d[b].rearrange("(f hh) q n -> (hh q) f n", hh=2), in_=sv))(), Tst, r=[Tst]))
                Bt.free(); Bt2.free()
            Bd.free()
            pa, Tpa = bank(); pb2, Tpb2 = bank()
            for fc in range(8):
                pp, Tpp = (pa, Tpa) if fc < 4 else (pb2, Tpb2)
                tr(pp[0:16, (fc % 4) * 128:(fc % 4 + 1) * 128], raw[:, fc, 3 + 16:3 + 32], ident_f, [Traw, Tconst], [Tpp])
            st, Tst = Bst2.f, Bst2.t
            cp('dve', st[0:16, 0:512], pa[0:16, :], [Tpa], [Tst])
            cp('dve', st[0:16, 512:1024], pb2[0:16, :], [Tpb2], [Tst])
            stores.append(P.dma('sp', lambda e: e.dma_start(out=convs_d[:, 2, :], in_=st[0:16, 0:1024]), Tst, r=[Tst]))

        n = TS_
        pD, TpD = bank()
        tr(pD[0:n, 0:8], dtT[:, 0:n], ident_f[0:8, 0:8], [Bdt.t, Tconst], [TpD])
        dt_tok = sm[0:n, 0:8]; dta = sm[0:n, 8:16]; cums = sm[0:n, 16:32]; wdec = sm[0:n, 32:40]; wgt = sm[0:n, 40:48]
        ecl = sm[:, 48:56]
        cp('dve', dt_tok, pD[0:n, 0:8], [TpD], [Tsm])
        tt('dve', dta, dt_tok, a_bc[0:n, :], ALU.mult, [Tsm, Tconst], [Tsm])
        pC, TpC = bank()
        mm(pC[0:n, 0:8], tri_f[0:n, 0:n], dta, True, True, [Tsm, Tconst], [TpC])
        mm(pC[:, 8:16], ones_f[0:n, :], dta, True, True, [Tsm, Tconst], [TpC])
        cp('dve', sm[:, 56:64], pC[:, 8:16], [TpC], [Tsm])
        cp('dve', cums[:, 0:8], pC[0:n, 0:8], [TpC], [Tsm])
        tt('dve', wdec, sm[0:n, 56:64], cums[:, 0:8], ALU.subtract, [Tsm], [Tsm])
        act(wdec, wdec, AF.Exp, [Tsm], [Tsm])
        tt('dve', wgt, wdec, dt_tok, ALU.mult, [Tsm], [Tsm])
        act(ecl, sm[:, 56:64], AF.Exp, [Tsm], [Tsm])
        pR = [bank(), bank()]
        for h in range(8):
            pp, Tpp = pR[h // 4]
            mm(pp[:, (h % 4) * n:(h % 4 + 1) * n], dta[:, h:h + 1].broadcast_to([n, 128]), tri_f[0:n, 0:n], True, True,
               [Tsm, Tconst], [Tpp])
        BE = Buf(); Bec = Buf(); BD = [Buf(), Buf()]
        E = v3(BE.b[0:n, 0:8 * n], 8); ecum = v3(Bec.b[:, 0:8 * n], 8)
        for g in range(2):
            pp, Tpp = pR[g]
            Dm = v3(BD[g].f[0:n, 0:4 * n], 4)
            tt('dve', Dm, v3(pp[0:n, 0:4 * n], 4), cums[:, 4 * g:4 * g + 4].unsqueeze(2).broadcast_to([n, 4, n]), ALU.subtract,
               [Tpp, Tsm], [BD[g].t])
            ts('dve', Dm, Dm, 0.0, None, ALU.min, ALU.bypass, [BD[g].t], [BD[g].t])
            act(E[:, 4 * g:4 * g + 4, :], Dm, AF.Exp, [BD[g].t], [BE.t])
            act(ecum[:, 4 * g:4 * g + 4, :], v3(pp[:, 0:4 * n], 4), AF.Exp, [Tpp], [Bec.t])
        BD[0].free(); BD[1].free()
        pCB, TpCB = bank()
        for g in range(2):
            mm(pCB[0:n, g * n:(g + 1) * n], BC[:, g, 0:n], BC[:, 2 + g, 0:n], True, True, [Bxb.t], [TpCB])
        Bcb = Buf(1024)
        CBm = v3(Bcb.f[0:n, 0:2 * n], 2)
        tt('dve', CBm, v3(pCB[0:n, 0:2 * n], 2), tri_f[0:n, 0:n].unsqueeze(1).broadcast_to([n, 2, n]), ALU.mult, [TpCB, Tconst], [Bcb.t])
        BW = Buf(); BCs = Buf()
        Wm = v3(BW.b[0:n, 0:8 * n], 8); Cs = v3(BCs.b[:, 0:8 * n], 8)
        for g in range(2):
            tt('dve', Wm[:, 4 * g:4 * g + 4, :], E[:, 4 * g:4 * g + 4, :], CBm[:, g:g + 1, :].broadcast_to([n, 4, n]),
               ALU.mult, [BE.t, Bcb.t], [BW.t])
            tt('dve', Cs[:, 4 * g:4 * g + 4, :], ecum[:, 4 * g:4 * g + 4, :], BC[:, 2 + g:3 + g, 0:n].broadcast_to([128, 4, n]),
               ALU.mult, [Bec.t, Bxb.t], [BCs.t])
        BE.free(); Bec.free(); Bcb.free()
        pX, TpX = bank()
        pXb = pX.bitcast(BF16)
        for k in range(4):
            tr(pXb[0:n, k * 128:(k + 1) * 128], xsb[:, k, 0:n], ident_b, [Bxb.t, Tconst], [TpX])
        for g in range(2):
            tr(pXb[0:n, 512 + g * 128:512 + (g + 1) * 128], BC[:, g, 0:n], ident_b, [Bxb.t, Tconst], [TpX])
        Bx = Buf(2048, True)
        xdt = Bx.b[0:n, 0:512]; xw = Bx.b[0:n, 512:1024]
        Bbt = Buf(512, True)
        Btok = Bbt.b[0:n, 0:256]
        tt('dve', xdt.rearrange("p (h q) -> p h q", h=8), pXb[0:n, 0:512].rearrange("p (h q) -> p h q", h=8),
           dt_tok.unsqueeze(2).broadcast_to([n, 8, 64]), ALU.mult, [TpX, Tsm], [Bx.t])
        tt('dve', xw.rearrange("p (h q) -> p h q", h=8), pXb[0:n, 0:512].rearrange("p (h q) -> p h q", h=8),
           wgt.unsqueeze(2).broadcast_to([n, 8, 64]), ALU.mult, [TpX, Tsm], [Bx.t])
        cp('act', Btok, pXb[0:n, 512:768], [TpX], [Bbt.t])
        pY, TpY = bank()
        for h in range(8):
            outp = pY[(h % 2) * 64:(h % 2) * 64 + 64, (h // 2) * n:(h // 2 + 1) * n]
            mm(outp, xdt[:, h * 64:(h + 1) * 64], Wm[:, h, :], True, False, [Bx.t, BW.t], [TpY])
            mm(outp, Hbf[:, h * 64:(h + 1) * 64], Cs[:, h, :], False, True, [THb, BCs.t], [TpY])
        pH, TpH = bank()
        for g in range(2):
            mm(pH[:, g * 256:(g + 1) * 256], Btok[:, g * 128:(g + 1) * 128], xw[:, g * 256:(g + 1) * 256], True, True,
               [Bbt.t, Bx.t], [TpH])
        Bth = Buf(2048, True)
        tt('dve', Bth.f.rearrange("p (h q) -> p h q", h=8), Hst.rearrange("p (h q) -> p h q", h=8),
           ecl.unsqueeze(2).broadcast_to([128, 8, 64]), ALU.mult, [TH, Tsm], [Bth.t])
        tt('dve', Hst, Bth.f, pH, ALU.add, [Bth.t, TpH], [TH])
        cp('act', Hbf, Hst, [TH], [THb])
        Bth.free(); BW.free(); BCs.free(); Bx.free(); Bbt.free()
        ssd_post(pY, TpY, n, lambda: xs[:, :, 0:n], Bxs.t, lambda: sz[:, :, 0:n], Bz.t, lambda: mixB[:, :, 0:n], BmB.t)
        if first:
            ssd_post(ysb.rearrange("p a b -> p (a b)"), Bys.t, 16, lambda: xs_s, Bxss.t, lambda: sz[:, :, 16:32], Bz.t,
                     lambda: mixB[:, :, 16:32], BmB.t, in_sbuf=True)
            Bys.free(); Bxss.free(); Bst2.free(); BstB.free()
        Bxs.free(); Bxb.free(); Bdt.free(); Bz.free()

        P.stream = None
        P.flush([S1, S2])
        end_section()
        bank_list[0] = list(range(8))
        if first:
            load_wout()
        if ci + 1 < 17:
            first_flag[0] = False
            prenorm[ci + 1] = pre_norm(ci + 1)
        for half in range(2):
            pb, Tpb = bank()
            for k in range(4):
                m = 4 * half + k
                for fc in range(8):
                    mm(pb[:, k * NT:(k + 1) * NT], wout_b[:, fc, m * 128:(m + 1) * 128], (mixA if fc < 4 else mixB)[:, fc % 4, :], fc == 0, fc == 7,
                       [Twout, BmA.t, BmB.t], [Tpb])
            tt('dve', hT[:, 4 * half:4 * half + 4, c0:c0 + NT], hT[:, 4 * half:4 * half + 4, c0:c0 + NT],
               v3(pb[:, 0:4 * NT], 4), ALU.add, hh + [Tpb], hh)
        BmA.free(); BmB.free()

    def hg_post(pO, TpO, n, sg_f, Tsg, out_f, Tout):
        B1 = Buf(1024); B2 = Buf(1024)
        sqo = B2.b[:, 0:4 * n]
        act(sqo, pO[:, 0:4 * n], AF.Square, [TpO], [B2.t])
        pss, Tpss = bank()
        mm(pss[:, 0:4 * n], ones_b, sqo, True, True, [B2.t, Tconst], [Tpss])
        rs = B2.f[:, 0:4 * n]
        act(rs, pss[:, 0:4 * n], AF.Ln, [Tpss], [B2.t], scale=1.0 / 128, bias=EPS)
        act(rs, rs, AF.Exp, [B2.t], [B2.t], scale=-0.5)
        t1 = v3(B1.f[:, 0:4 * n], 4)
        tt('dve', t1, v3(pO[:, 0:4 * n], 4), sg_f(0, 4), ALU.mult, [TpO, Tsg, B1.t], [B1.t])
        tt('dve', out_f(), t1, v3(rs, 4), ALU.mult, [B1.t, B2.t], [Tout])
        B1.free(); B2.free()

    def ssd_post(pY, TpY, n, xs_f, Txs, sz_f, Tsz, out_f, Tout, in_sbuf=False):
        B1 = Buf(); B2 = Buf(); B3 = Buf(1024)
        yv = v3(B1.f[:, 0:4 * n], 4)
        tt('dve', yv, xs_f(), Dexp.unsqueeze(2).broadcast_to([128, 4, n]), ALU.mult, [Txs, Tconst], [B1.t])
        tt('dve', yv, yv, v3(pY[:, 0:4 * n], 4), ALU.add, [B1.t, TpY], [B1.t])
        tt('dve', yv, yv, sz_f(), ALU.mult, [B1.t, Tsz], [B1.t])
        sqy = v3(B2.b[:, 0:4 * n], 4)
        act(sqy, yv, AF.Square, [B1.t], [B2.t])
        pss, Tpss = bank()
        for g in range(2):
            for i in range(2):
                mm(pss[:, g * n:(g + 1) * n], ones_b, sqy[:, 2 * g + i, :], i == 0, i == 1, [B2.t, Tconst], [Tpss])
        rs = B3.f[:, 0:2 * n]
        act(rs, pss[:, 0:2 * n], AF.Ln, [Tpss], [B3.t], scale=1.0 / 256, bias=EPS)
        act(rs, rs, AF.Exp, [B3.t], [B3.t], scale=-0.5)
        tt('dve', yv, yv, snrm.unsqueeze(2).broadcast_to([128, 4, n]), ALU.mult, [B1.t, Tconst], [B1.t])
        for g in range(2):
            tt('dve', out_f()[:, 2 * g:2 * g + 2, :], yv[:, 2 * g:2 * g + 2, :],
               rs[:, g * n:(g + 1) * n].unsqueeze(1).broadcast_to([128, 2, n]), ALU.mult, [B1.t, B3.t], [Tout])
        B1.free(); B2.free(); B3.free()

    S_CTX = {}

    for ci in range(17):
        mixer_chunk(ci)
        chk('mix%d' % ci)
    stores.append(P.dma('sp', lambda e: e.dma_start(out=hgp_d.rearrange("h k v -> k h v"), in_=Sst), TS, r=[TS]))
    pb, Tpb = bank()
    for fc in range(4):
        tr(pb[:, fc * 128:(fc + 1) * 128], Hst[:, fc * 128:(fc + 1) * 128], ident_f, [TH, Tconst], [Tpb])
    st, Tst = next_stage()
    cp('dve', st[:, 0:512], pb, [Tpb], [Tst])
    stores.append(P.dma('sp', (lambda st=st: lambda e: e.dma_start(out=ssmp_d.rearrange("(f hh) q n -> (hh q) f n", hh=2),
                                                               in_=v3(st[:, 0:512], 4)))(), Tst, r=[Tst]))
    pa, Tpa = bank(); pb2, Tpb2 = bank()
    for fc in range(8):
        pp, Tpp = (pa, Tpa) if fc < 4 else (pb2, Tpb2)
        tr(pp[0:3, (fc % 4) * 128:(fc % 4 + 1) * 128], raw[:, fc, 0:3], ident_f, [Traw, Tconst], [Tpp])
    st2, Tst2 = next_stage()
    cp('dve', st2[0:3, 0:512], pa[0:3, :], [Tpa], [Tst2])
    cp('dve', st2[0:3, 512:1024], pb2[0:3, :], [Tpb2], [Tst2])
    stores.append(P.dma('sp', (lambda st2=st2: lambda e: e.dma_start(out=convp_d, in_=st2[0:3, 0:1024]))(), Tst2, r=[Tst2]))
    P.barrier()
    stg_only0[0] = False
    chk('mixer')
    fstate = {}

    def final_post(ti, off):
        if 'b' not in fstate:
            fb = []
            for i in range(3):
                a = sbat(off, [128, 8, 128], BF16); off += 2048
                b_ = sbat(off, [128, 128], F32); off += 512
                c_ = sbat(off, [128, 8, 128], F32); off += 4096
                fb.append((a, b_, c_, T("fsq%d" % i), T("frs%d" % i), T("fy%d" % i)))
            assert off <= TOTAL, off
            fstate['b'] = fb
            fstate['k'] = 0
        bank_list[0] = [0, 1, 2, 3, 6, 7]

        def stage_a(ci):
            r0, n = xrows[ci]
            hh = [Th[ci]]
            fsq, frs, fy, Tfsq, Tfrs, Tfy = fstate['b'][ci % 3]
            act(fsq[:, :, 0:n], hT[:, :, r0:r0 + n], AF.Square, hh, [Tfsq])
            pb, Tpb = bank()
            for c in range(8):
                mm(pb[:, 0:n], ones_b, fsq[:, c, 0:n], c == 0, c == 7, [Tfsq, Tconst], [Tpb])
            act(frs[:, 0:n], pb[:, 0:n], AF.Ln, [Tpb], [Tfrs], scale=1.0 / D, bias=EPS)
            act(frs[:, 0:n], frs[:, 0:n], AF.Exp, [Tfrs], [Tfrs], scale=-0.5)
            for c in range(8):
                stt('dve', fy[:, c, 0:n], hT[:, c, r0:r0 + n], nw(3)[:, c:c + 1], frs[:, 0:n], ALU.mult, ALU.mult,
                    hh + [Tfrs, Tconst], [Tfy])

        def stage_b(ci):
            r0, n = xrows[ci]
            fsq, frs, fy, Tfsq, Tfrs, Tfy = fstate['b'][ci % 3]
            st, Tst = next_stage()
            for half in range(2):
                pb, Tpb = bank()
                for k in range(4):
                    tr(pb[0:n, k * 128:(k + 1) * 128], fy[:, 4 * half + k, 0:n], ident_f, [Tfy, Tconst], [Tpb])
                cp('act' if half == 0 else 'dve', st[0:n, 512 * half:512 * half + 512], pb[0:n, :], [Tpb], [Tst])
            stores.append(P.dma('sp', (lambda st=st, r0=r0, n=n: lambda e: e.dma_start(out=y_d[r0:r0 + n, :], in_=st[0:n, 0:1024]))(),
                                Tst, r=[Tst]))

        L = TILE_CHUNKS[ti]
        stage_a(L[0])
        for k in range(1, len(L)):
            stage_a(L[k])
            stage_b(L[k - 1])
        stage_b(L[-1])
        bank_list[0] = list(range(8))

    ffn(2, w2g, w2u, w2d, post=final_post)


_NC_CACHE = {}


def kernel(x_prompt, x_sample, state_hgrn, state_ssm, state_conv, meta_tokens, lb_logits,
           norm_ffn1, w_ffn1_gate, w_ffn1_up, w_ffn1_down, norm_mix, w_in, hg_norm, conv_w, conv_b,
           dt_bias, a_log, d_skip, ssm_norm, w_out, norm_ffn2, w_ffn2_gate, w_ffn2_up, w_ffn2_down,
           norm_final):
    f = lambda a: np.ascontiguousarray(np.asarray(a, dtype=np.float32))
    x_prompt = f(x_prompt); x_sample = f(x_sample)
    state_hgrn = f(state_hgrn)[0]; state_ssm = f(state_ssm)[0]; state_conv = f(state_conv)[0]
    meta = f(meta_tokens)
    vec = np.zeros((128, 104), np.float32)
    pc = lambda v: f(v).reshape(-1, 128).T
    vec[:, 0:8] = pc(norm_ffn1[0]); vec[:, 8:16] = pc(norm_mix[0]); vec[:, 16:24] = pc(norm_ffn2[0]); vec[:, 24:32] = pc(norm_final)
    vec[:, 32:36] = pc(hg_norm[0]); vec[:, 36:40] = pc(ssm_norm[0])
    cwv = f(conv_w)[0]
    for j in range(4):
        vec[:, 40 + 8 * j:48 + 8 * j] = pc(cwv[j])
    vec[:, 72:80] = pc(conv_b[0])
    lbl = f(lb_logits)
    vec[:, 80:84] = pc(lbl[0]); vec[:, 84:88] = pc(lbl[1])
    vec[:, 88:92] = pc(np.repeat(f(d_skip)[0], 64))
    vec[:, 92:100] = np.broadcast_to(f(a_log)[0][None, :], (128, 8))
    vec[0:8, 100] = f(dt_bias)[0]
    vec[0:8, 101] = f(a_log)[0]
    cst = np.zeros((128, 1664), np.float32)
    cst[:, 0:128] = np.eye(128, dtype=np.float32)
    cst[:, 128:256] = np.triu(np.ones((128, 128), np.float32))
    cst[:, 256:384] = 1.0
    r = np.ones(512, np.float32); r[::64] = 0.0
    cst[:, 384:896] = r[None, :]
    ra = np.ones(128, np.float32); ra[::32] = 0.0
    cst[:, 896:1024] = ra[None, :]
    for h in range(8):
        cst[h, 1024 + 64 * h:1024 + 64 * (h + 1)] = 1.0
    cst[:, 1536:1664] = -0.5
    if 'nc' not in _NC_CACHE:
        _NC_CACHE['nc'] = build_nc()
    nc = _NC_CACHE['nc']
    shared = dict(w1g=f(w_ffn1_gate)[0], w1u=f(w_ffn1_up)[0], w1d=f(w_ffn1_down)[0],
                  w2g=f(w_ffn2_gate)[0], w2u=f(w_ffn2_up)[0], w2d=f(w_ffn2_down)[0],
                  w_in=f(w_in)[0], w_out=f(w_out)[0], vecs=vec, consts=cst)
    in_maps = []
    for c in range(8):
        xs_ = x_sample[16 * c:16 * c + 16, 0, :]
        m = dict(shared)
        m['xin'] = np.ascontiguousarray(np.concatenate([meta, xs_, x_prompt[c]], axis=0))
        m['s_hg'] = np.ascontiguousarray(state_hgrn[16 * c:16 * c + 16])
        m['s_ssm'] = np.ascontiguousarray(state_ssm[16 * c:16 * c + 16])
        m['s_conv'] = np.ascontiguousarray(state_conv[16 * c:16 * c + 16])
        in_maps.append(m)
    res = run_bass_kernel_spmd(nc, in_maps, core_ids=list(range(8)))
    R = res.results
    y_prompt = np.stack([R[c]['y'][32:] for c in range(8)], 0)
    y_sample = np.concatenate([R[c]['y'][16:32] for c in range(8)], 0)[:, None, :]
    hgrn_prompt = np.stack([R[c]['hg_p'] for c in range(8)], 0)[None]
    ssm_prompt = np.stack([R[c]['ssm_p'] for c in range(8)], 0)[None]
    conv_prompt = np.stack([R[c]['conv_p'] for c in range(8)], 0)[None]
    hgrn_sample = np.concatenate([R[c]['hg_s'] for c in range(8)], 0)[None]
    ssm_sample = np.concatenate([R[c]['ssm_s'] for c in range(8)], 0)[None]
    conv_sample = np.concatenate([R[c]['conv_s'] for c in range(8)], 0)[None]
    return tuple(np.ascontiguousarray(a, dtype=np.float32) for a in
                 (y_prompt, y_sample, hgrn_prompt, ssm_prompt, conv_prompt, hgrn_sample, ssm_sample, conv_sample))
```

```python
import contextlib
import numpy as np
import concourse.bass as bass
import concourse.mybir as mybir
from concourse.bass_utils import run_bass_kernel_spmd

F32 = mybir.dt.float32
BF16 = mybir.dt.bfloat16
ALU = mybir.AluOpType
AF = mybir.ActivationFunctionType

NCOL = 2080
D = 1024
DFF = 2816
DIN = 3592
EPS = 1e-6


class T:
    __slots__ = ('name', 'lw', 'rd', 'rd_dma', 'semcnt', 'excl', 'parents')

    def __init__(self, name, excl=False):
        self.name = name
        self.excl = excl
        self.parents = None
        self.lw = None
        self.rd = {}
        self.rd_dma = []
        self.semcnt = 0


class Op:
    __slots__ = ('eng', 'fn', 'deps', 'sig', 'ticket', 'dma', 'dsem', 'dval', 'seq')


class Proxy:
    __slots__ = ('op',)


def _resolve(t):
    if t.parents:
        ps_ = t.parents
        t.parents = None
        for p in ps_:
            _resolve(p)
            cand = list(p.rd.values()) + ([p.lw] if p.lw is not None else [])
            for op in cand:
                if op.dma:
                    t.rd_dma.append(op)
                else:
                    cur = t.rd.get(op.eng)
                    if cur is None or cur.seq < op.seq:
                        t.rd[op.eng] = op
            t.rd_dma.extend(p.rd_dma)


class Prog:
    ENGS = ('pe', 'act', 'dve', 'pool', 'sp')

    def __init__(self, nc):
        self.nc = nc
        self.ops = {e: [] for e in self.ENGS}
        self.pending_dma = []
        self.stream = None
        self.nseq = 0

    def _mk(self, eng, fn, r, w, dma, extra):
        for t in list(r) + list(w):
            _resolve(t)
        op = Op()
        self.nseq += 1
        op.seq = self.nseq
        op.eng = eng
        op.fn = fn
        op.dma = dma
        op.sig = False
        op.ticket = 0
        op.dsem = None
        op.dval = 0
        deps = set()
        xr = [t for t in r if t.excl]
        if xr:
            r = [t for t in r if not t.excl]
            w = list(w) + [t for t in xr if t not in w]
        for t in r:
            if t.lw is not None:
                deps.add(t.lw)
        for t in w:
            if t.lw is not None:
                deps.add(t.lw)
            deps.update(t.rd.values())
            deps.update(t.rd_dma)
        for d in extra:
            deps.add(d)
        op.deps = [d for d in deps if not (d.eng == 'pe' and eng == 'pe' and not d.dma and not dma)]
        for d in op.deps:
            d.sig = True
        for t in r:
            if dma:
                t.rd_dma.append(op)
            else:
                t.rd[eng] = op
        for t in w:
            t.lw = op
            t.rd = {}
            t.rd_dma = []
        self.ops[eng].append(op)
        return op

    def add(self, eng, fn, r=(), w=(), extra=()):
        if self.stream is not None:
            self.stream.append(('add', eng, fn, list(r), list(w), None, None))
            return None
        extra = [x.op if isinstance(x, Proxy) else x for x in extra]
        return self._mk(eng, fn, r, w, False, extra)

    def flush(self, streams):
        assert self.stream is None
        idx = [0] * len(streams)
        cum = []
        for s in streams:
            c, acc = [], 0.0
            for rec in s:
                c.append(acc)
                acc += 0.1 if rec[1] in ('pe', 'sp') or rec[0] == 'dma' else 1.0
            cum.append((c, max(acc, 1e-9)))
        while True:
            best = None
            for i, s in enumerate(streams):
                if idx[i] < len(s):
                    frac = cum[i][0][idx[i]] / cum[i][1]
                    if best is None or frac < best[0]:
                        best = (frac, i)
            if best is None:
                break
            i = best[1]
            kind, eng, fn, r, w, semtile, proxy = streams[i][idx[i]]
            idx[i] += 1
            if kind == 'add':
                self._mk(eng, fn, r, w, False, ())
            else:
                proxy.op = self.dma(eng, fn, semtile, r, w)

    def dma(self, eng, fn, semtile, r=(), w=(), extra=()):
        if self.stream is not None:
            px = Proxy()
            self.stream.append(('dma', eng, fn, list(r), list(w), semtile, px))
            return px
        op = self._mk(eng, fn, r, w, True, extra)
        semtile.semcnt += 16
        op.dsem = semtile
        op.dval = semtile.semcnt
        self.pending_dma.append(op)
        return op

    def barrier(self):
        last = [self.ops[e][-1] for e in self.ENGS if self.ops[e]]
        extra = last + self.pending_dma
        self.pending_dma = []
        for e in self.ENGS:
            self._mk(e, lambda eng: eng.nop(), (), (), False, extra)

    def emit(self):
        nc = self.nc
        with contextlib.ExitStack() as es:
            engsem = {e: es.enter_context(nc.semaphore('s_' + e)) for e in self.ENGS}
            tiles = []
            seen_t = set()
            for e in self.ENGS:
                for op in self.ops[e]:
                    if op.dma and id(op.dsem) not in seen_t:
                        seen_t.add(id(op.dsem))
                        tiles.append(op.dsem)
            assert len(tiles) <= 90, len(tiles)
            tsem = {id(t): es.enter_context(nc.semaphore('d_%d' % i)) for i, t in enumerate(tiles)}
            for e in self.ENGS:
                cnt = 0
                for op in self.ops[e]:
                    if not op.dma and op.sig:
                        cnt += 1
                        op.ticket = cnt
            block = es.enter_context(nc.Block())

            def body(ename):
                def run(engine):
                    seen = {}
                    for op in self.ops[ename]:
                        need = {}
                        for d in op.deps:
                            if d.dma:
                                key = tsem[id(d.dsem)]
                                val = d.dval
                            else:
                                key = engsem[d.eng]
                                val = d.ticket
                            kk = key.num
                            if kk not in need or need[kk][1] < val:
                                need[kk] = (key, val)
                        for kk, (key, val) in need.items():
                            if seen.get(kk, 0) < val:
                                engine.wait_ge(key, val)
                                seen[kk] = val
                        ins = op.fn(engine)
                        if op.dma:
                            ins.then_inc(tsem[id(op.dsem)], 16)
                        elif op.sig:
                            ins.then_inc(engsem[ename], 1)
                return run
            block.tensor(body('pe'))
            block.scalar(body('act'))
            block.vector(body('dve'))
            block.gpsimd(body('pool'))
            block.sync(body('sp'))


class _Stop(Exception):
    pass


def build_nc(stop=None):
    nc = bass.Bass("TRN2", target_bir_lowering=False)
    P = Prog(nc)
    stores = []

    def chk(tag):
        if stop == tag:
            raise _Stop()
    try:
        _build_body(nc, P, stores, chk)
    except _Stop:
        pass
    P.add('sp', lambda e: e.nop(), extra=stores)
    P.emit()
    return nc


def _build_body(nc, P, stores, chk):

    def din(name, shape):
        return nc.dram_tensor(name, list(shape), F32, kind="ExternalInput").ap()

    def dout(name, shape):
        return nc.dram_tensor(name, list(shape), F32, kind="ExternalOutput").ap()

    xin = din("xin", [NCOL, D])
    s_hg = din("s_hg", [16, 4, 128, 128])
    s_ssm = din("s_ssm", [16, 8, 64, 128])
    s_conv = din("s_conv", [16, 3, 1024])
    w1g = din("w1g", [D, DFF]); w1u = din("w1u", [D, DFF]); w1d = din("w1d", [DFF, D])
    w2g = din("w2g", [D, DFF]); w2u = din("w2u", [D, DFF]); w2d = din("w2d", [DFF, D])
    w_in = din("w_in", [D, DIN]); w_out = din("w_out", [D, D])
    vecs_d = din("vecs", [128, 104])
    consts_d = din("consts", [128, 1664])
    y_d = dout("y", [NCOL, D])
    hgp_d = dout("hg_p", [4, 128, 128])
    ssmp_d = dout("ssm_p", [8, 64, 128])
    convp_d = dout("conv_p", [3, 1024])
    hgs_d = dout("hg_s", [16, 4, 128, 128])
    ssms_d = dout("ssm_s", [16, 8, 64, 128])
    convs_d = dout("conv_s", [16, 3, 1024])

    BASE = 16512
    TOTAL = 212832
    cnt = [0]

    def sbat(off, shape, dt):
        cnt[0] += 1
        return nc.alloc_sbuf_tensor_at("t%d" % cnt[0], list(shape), dt, offset=BASE + off).ap()

    hT = sbat(0, [128, 8, NCOL], F32)
    Th = [T("h%d" % i) for i in range(17)]

    def hts(c0, n):
        out = []
        for i in range(17):
            a, b = (0, 32) if i == 0 else (32 + 128 * (i - 1), 32 + 128 * i)
            if a < c0 + n and b > c0:
                out.append(Th[i])
        return out
    o = 66560
    consts = sbat(o, [128, 1664], F32); o += 6656
    ident_f = consts[:, 0:128]; tri_f = consts[:, 128:256]; ones_f = consts[:, 256:384]
    reset = consts[:, 384:896]; resetA = consts[:, 896:1024]; E8 = consts[0:8, 1024:1536]; mhalf = consts[:, 1536:1664]
    vecs = sbat(o, [128, 104], F32); o += 416
    cb16 = sbat(o, [128, 256], BF16); o += 512
    ident_b = cb16[:, 0:128]; ones_b = cb16[:, 128:256]
    dv = sbat(o, [128, 64], F32); o += 256
    c1 = dv[:, 0:4]; c0_ = dv[:, 4:8]; a_bc = dv[:, 8:16]; a_col = dv[0:8, 16:17]; dvt = dv[:, 20:32]
    Tconst = T("const")
    stage = []
    Tstage = []
    for i in range(2):
        stage.append(sbat(o, [128, 2048], F32)); o += 8192
        Tstage.append(T("stage%d" % i))
    PH = o
    stg_i = [0]
    stg_only0 = [False]

    def next_stage():
        i = 0 if stg_only0[0] else stg_i[0] % 2
        stg_i[0] += 1
        return stage[i], Tstage[i]

    ps = [nc.alloc_psum_tensor("ps%d" % i, [128, 512], F32).ap() for i in range(8)]
    Tps = [T("ps%d" % i, excl=True) for i in range(8)]

    def act(out, in_, func, r, w, scale=1.0, bias=0.0):
        P.add('act', lambda e: e.activation(out=out, in_=in_, func=func, bias=bias, scale=scale), r, w)

    def tt(eng, out, a, b, op, r, w):
        P.add(eng, lambda e: e.tensor_tensor(out=out, in0=a, in1=b, op=op), r, w)

    def stt(eng, out, a, sc, b, op0, op1, r, w):
        P.add(eng, lambda e: e.scalar_tensor_tensor(out=out, in0=a, scalar=sc, in1=b, op0=op0, op1=op1), r, w)

    def ts(eng, out, a, s1, s2, op0, op1, r, w):
        P.add(eng, lambda e: e.tensor_scalar(out=out, in0=a, scalar1=s1, scalar2=s2, op0=op0, op1=op1), r, w)

    def cp(eng, out, in_, r, w):
        if eng == 'act':
            P.add('act', lambda e: e.copy(out=out, in_=in_), r, w)
        else:
            P.add(eng, lambda e: e.tensor_copy(out=out, in_=in_), r, w)

    def mm(out, lhsT, rhs, start, stop, r, w):
        P.add('pe', lambda e: e.matmul(out, lhsT, rhs, start=start, stop=stop), r, w)

    def tr(out, in_, idn, r, w):
        P.add('pe', lambda e: e.transpose(out, in_, idn), r, w)

    def ld(out, in_, semT, r=(), w=(), q='sp'):
        return P.dma(q, lambda e: e.dma_start(out=out, in_=in_), semT, r, w)

    def v3(ap2d, a):
        return ap2d.rearrange("p (a b) -> p a b", a=a)

    ld(consts, consts_d, Tconst, w=[Tconst])
    ld(vecs, vecs_d, Tconst, w=[Tconst])
    cp('dve', ident_b, ident_f, [Tconst], [Tconst])
    cp('dve', ones_b, ones_f, [Tconst], [Tconst])
    tt('dve', dvt[:, 0:4], vecs[:, 84:88], vecs[:, 80:84], ALU.subtract, [Tconst], [Tconst])
    act(dvt[:, 4:8], dvt[:, 0:4], AF.Exp, [Tconst], [Tconst])
    ts('dve', dvt[:, 4:8], dvt[:, 4:8], 1.0, None, ALU.add, ALU.bypass, [Tconst], [Tconst])
    P.add('dve', lambda e: e.reciprocal(out=dvt[:, 8:12], in_=dvt[:, 4:8]), [Tconst], [Tconst])
    ts('dve', c1, dvt[:, 8:12], -0.5, 0.5, ALU.mult, ALU.add, [Tconst], [Tconst])
    tt('dve', c0_, dvt[:, 8:12], c1, ALU.add, [Tconst], [Tconst])
    act(a_bc, vecs[:, 92:100], AF.Exp, [Tconst], [Tconst])
    ts('dve', a_bc, a_bc, -1.0, None, ALU.mult, ALU.bypass, [Tconst], [Tconst])
    act(a_col, vecs[0:8, 101:102], AF.Exp, [Tconst], [Tconst])
    ts('dve', a_col, a_col, -1.0, None, ALU.mult, ALU.bypass, [Tconst], [Tconst])
    nw = lambda k: vecs[:, 8 * k:8 * k + 8]
    hgn = vecs[:, 32:36]; snrm = vecs[:, 36:40]
    cw = lambda j: vecs[:, 40 + 8 * j:48 + 8 * j]
    cbias = vecs[:, 72:80]; Dexp = vecs[:, 88:92]; dtb = vecs[0:8, 100:101]

    xrows = [(0, 32)] + [(32 + 128 * i, 128) for i in range(16)]

    xst = [sbat(TOTAL - 8192, [128, 1024], F32), sbat(TOTAL - 4096, [128, 1024], F32)]
    Txst = [T("xst0"), T("xst1")]

    def load_chunk(ci):
        r0, n = xrows[ci]
        st, Tst = xst[ci % 2], Txst[ci % 2]
        ld(st[0:n, 0:1024], xin[r0:r0 + n, :], Tst, w=[Tst])
        for half in range(2):
            bk = 4 + (2 * ci + half) % 4
            for k in range(4):
                c = half * 4 + k
                tr(ps[bk][:, k * 128:k * 128 + n], st[0:n, c * 128:(c + 1) * 128], ident_f[0:n, 0:n], [Tst, Tconst], [Tps[bk]])
            eng = 'act' if half == 0 else 'dve'
            cp(eng, hT[:, half * 4:half * 4 + 4, r0:r0 + n], v3(ps[bk], 4)[:, :, 0:n], [Tps[bk]], [Th[ci]])

    CT = [(0, 416), (416, 416), (832, 416), (1248, 416), (1664, 416)]
    TILE_CHUNKS = [[0, 1, 2, 3], [4, 5, 6], [7, 8, 9], [10, 11, 12], [13, 14, 15, 16]]

    def prep_x(ti):
        for ci in TILE_CHUNKS[ti]:
            load_chunk(ci)

    def ffn(nk, wg, wu, wd, prep=None, post=None):
        oo = PH
        xn = sbat(oo, [128, 8, NCOL], BF16); oo += 33280
        hid = []
        for i in range(2):
            hid.append(sbat(oo, [128, 2, NCOL], BF16)); oo += 8320
        wb = []
        for i in range(7):
            wb.append(sbat(oo, [128, 2048], BF16)); oo += 4096
        sqb = sbat(oo, [128, 8, 512], BF16); oo += 8192
        rstd = sbat(oo, [128, 512], F32); oo += 2048
        tln = sbat(oo, [128, 512], F32); oo += 2048
        sil = []
        for i in range(2):
            sil.append(sbat(oo, [128, 512], F32)); oo += 2048
        assert BASE + oo <= BASE + TOTAL, oo
        Txn = [T("xn%d" % i) for i in range(5)]
        Thid = [T("hid%d" % i) for i in range(2)]
        Twb = [T("wb%d" % i) for i in range(7)]
        Tsqb = T("sqb"); Trstd = T("rstd"); Ttln = T("tln"); Tsil = [T("sil0"), T("sil1")]
        def norm_tile(ti):
            c0, n = CT[ti]
            hh = hts(c0, n)
            act(sqb[:, :, 0:n], hT[:, :, c0:c0 + n], AF.Square, hh, [Tsqb])
            for c in range(8):
                mm(ps[4][:, 0:n], ones_b, sqb[:, c, 0:n], c == 0, c == 7, [Tsqb, Tconst], [Tps[4]])
            act(tln[:, 0:n], ps[4][:, 0:n], AF.Ln, [Tps[4]], [Ttln], scale=1.0 / D, bias=EPS)
            act(rstd[:, 0:n], tln[:, 0:n], AF.Exp, [Ttln], [Trstd], scale=-0.5)
            for c in range(8):
                stt('dve', xn[:, c, c0:c0 + n], hT[:, c, c0:c0 + n], nw(nk)[:, c:c + 1], rstd[:, 0:n],
                    ALU.mult, ALU.mult, hh + [Trstd, Tconst], [Txn[ti]])
        wgv = wg.rearrange("(c p) n -> p c n", p=128)
        wuv = wu.rearrange("(c p) n -> p c n", p=128)
        wdv = wd.rearrange("(j p) n -> p j n", p=128)
        wslot = [0, 0]
        nld = [0]

        def load_w(src_ap, shape3):
            st, Tst = next_stage()
            if shape3[0] == 8:
                i = wslot[0] % 4
                wslot[0] += 1
            else:
                i = 4 + wslot[1] % 3
                wslot[1] += 1
            ld(st.rearrange("p (a b) -> p a b", a=shape3[0]), src_ap, Tst, w=[Tst])
            nld[0] += 1
            cp('dve' if nld[0] <= 6 else 'pool', wb[i], st, [Tst], [Twb[i]])
            return i

        def load_jp(jp):
            a = load_w(wgv[:, :, jp * 256:(jp + 1) * 256], (8, 256))
            b = load_w(wuv[:, :, jp * 256:(jp + 1) * 256], (8, 256))
            c = load_w(wdv[:, 2 * jp:2 * jp + 2, :], (2, 1024))
            return (a, b, c)
        wq = [load_jp(0)]
        for ti in range(len(CT)):
            if prep is not None:
                prep(ti)
        norm_tile(0)
        wq.append(load_jp(1))
        dbk = [0]

        def up_phase(jp, ti):
            ia, ib, ic = wq[jp]
            wga = v3(wb[ia], 8); wua = v3(wb[ib], 8)
            hb = hid[jp % 2]; Thb = Thid[jp % 2]
            c0, n = CT[ti]
            for jj in range(2):
                pg = ps[2 * jj]; pu = ps[2 * jj + 1]
                for c in range(8):
                    mm(pg[:, 0:n], wga[:, c, jj * 128:(jj + 1) * 128], xn[:, c, c0:c0 + n], c == 0, c == 7,
                       [Twb[ia], Txn[ti]], [Tps[2 * jj]])
                for c in range(8):
                    mm(pu[:, 0:n], wua[:, c, jj * 128:(jj + 1) * 128], xn[:, c, c0:c0 + n], c == 0, c == 7,
                       [Twb[ib], Txn[ti]], [Tps[2 * jj + 1]])
                act(sil[jj][:, 0:n], pg[:, 0:n], AF.Silu, [Tps[2 * jj]], [Tsil[jj]])
                tt('dve', hb[:, jj, c0:c0 + n], sil[jj][:, 0:n], pu[:, 0:n], ALU.mult, [Tsil[jj], Tps[2 * jj + 1]], [Thb])

        def down_phase(jp, ti):
            ia, ib, ic = wq[jp]
            wda = v3(wb[ic], 2)
            hb = hid[jp % 2]; Thb = Thid[jp % 2]
            c0, n = CT[ti]
            hh = hts(c0, n)
            for m in range(8):
                bk = 4 + dbk[0] % (2 if (jp == 10 and post is not None) else 4)
                dbk[0] += 1
                for jj in range(2):
                    mm(ps[bk][:, 0:n], wda[:, jj, m * 128:(m + 1) * 128], hb[:, jj, c0:c0 + n], jj == 0, jj == 1,
                       [Twb[ic], Thb], [Tps[bk]])
                stt('dve', hT[:, m, c0:c0 + n], ps[bk][:, 0:n], 0.5, hT[:, m, c0:c0 + n], ALU.mult, ALU.add,
                    [Tps[bk]] + hh, hh)

        for jp in range(11):
            for ti in range(len(CT)):
                if jp == 0 and ti + 1 < len(CT):
                    norm_tile(ti + 1)
                up_phase(jp, ti)
                if jp > 0:
                    down_phase(jp - 1, ti)
            if jp + 2 < 11:
                wq.append(load_jp(jp + 2))
        for ti in range(len(CT)):
            down_phase(10, ti)
            if post is not None:
                post(ti, oo)

    ffn(0, w1g, w1u, w1d, prep=prep_x)
    P.barrier()
    chk('ffn1')

    FIXED_MIX = 57472 + 16384 + 4192 + 2048 + 1024 + 2048 + 1024 + 384
    ARENA_OFF = PH - 8192
    ARENA_BYTES = (TOTAL - PH - FIXED_MIX) // 512 * 512 + 8192
    oo = ARENA_OFF + ARENA_BYTES
    win_b = sbat(oo, [128, 8, DIN], BF16); oo += 57472
    wout_b = sbat(oo, [128, 8, D], BF16); oo += 16384
    Twin = [T("win%d" % b) for b in range(15)]; Twout = T("wout")
    winv = w_in.rearrange("(c p) n -> p c n", p=128)
    woutv = w_out.rearrange("(c p) n -> p c n", p=128)
    for b in [2, 3, 10, 11, 12, 13, 0, 1, 6, 7, 8, 9, 4, 5, 14]:
        c0 = b * 256
        n = min(256, DIN - c0)
        st, Tst = next_stage()
        sv = st[:, 0:8 * n].rearrange("p (a b) -> p a b", a=8)
        q_ = 'sp' if Tst is Tstage[0] else 'act'
        ld(sv, winv[:, :, c0:c0 + n], Tst, w=[Tst], q=q_)
        cp('dve' if q_ == 'sp' else 'act', win_b[:, :, c0:c0 + n], sv, [Tst], [Twin[b]])
    def load_wout():
        for b in range(4):
            st, Tst = next_stage()
            sv = v3(st, 8)
            ld(sv, woutv[:, :, b * 256:(b + 1) * 256], Tst, w=[Tst])
            cp('act' if b % 2 == 0 else 'dve', wout_b[:, :, b * 256:(b + 1) * 256], sv, [Tst], [Twout])
    chk('mw')
    stg_only0[0] = True
    raw = sbat(oo, [128, 8, 131], F32); oo += 4192
    Sst = sbat(oo, [128, 4, 128], F32); oo += 2048
    Sbf = sbat(oo, [128, 4, 128], BF16); oo += 1024
    Hst = sbat(oo, [128, 512], F32); oo += 2048
    Hbf = sbat(oo, [128, 512], BF16); oo += 1024
    sm = sbat(oo, [128, 96], F32); oo += 384
    assert oo <= TOTAL, oo
    Traw = T("raw"); TS = T("S"); TSb = T("Sbf"); TH = T("H"); THb = T("Hbf"); Tsm = T("sm")
    UNIT = 512
    A_OFF = ARENA_OFF
    A_N = ARENA_BYTES // UNIT
    a_free = [True] * A_N
    a_last = [None] * A_N
    for u in range(8192 // UNIT):
        a_last[u] = Tstage[1]
    a_pend = {}
    peak = [0]

    first_flag = [False]

    class Buf:
        def __init__(self, nbytes=2048, fixed=False):
            if first_flag[0] and not fixed:
                nbytes = max(UNIT, nbytes // 4)
            k = (nbytes + UNIT - 1) // UNIT
            sid = id(P.stream) if P.stream is not None else None
            own = a_pend.get(sid, set()) if sid is not None else set()
            run = None
            u = 0
            while u + k <= A_N:
                ok = True
                for j in range(k):
                    if not (a_free[u + j] or (u + j) in own):
                        ok = False
                        u = u + j + 1
                        break
                if ok:
                    run = list(range(u, u + k))
                    break
            assert run is not None, "arena full"
            self.t = T("buf")
            par = []
            for x in run:
                if a_last[x] is not None and a_last[x] not in par:
                    par.append(a_last[x])
                a_last[x] = self.t
                a_free[x] = False
                own.discard(x)
            self.t.parents = par
            self.run = run
            peak[0] = max(peak[0], run[-1] + 1)
            off = A_OFF + run[0] * UNIT
            self.f = sbat(off, [128, k * UNIT // 4], F32)
            self.b = sbat(off, [128, k * UNIT // 2], BF16)

        def free(self):
            sid = id(P.stream) if P.stream is not None else None
            if sid is None:
                for x in self.run:
                    a_free[x] = True
            else:
                a_pend.setdefault(sid, set()).update(self.run)

    def end_section():
        for s in a_pend.values():
            for x in s:
                a_free[x] = True
        a_pend.clear()

    P.add('pool', lambda e: e.memset(raw, 0.0), (), [Traw])
    P.add('pool', lambda e: e.memset(Sst, 0.0), (), [TS])
    P.add('pool', lambda e: e.memset(Hst, 0.0), (), [TH])
    P.add('pool', lambda e: e.memset(Sbf, 0.0), (), [TSb])
    P.add('pool', lambda e: e.memset(Hbf, 0.0), (), [THb])

    chk('m0a')
    pbk = [0]
    bank_list = [list(range(8))]

    def bank():
        bl = bank_list[0]
        b = bl[pbk[0] % len(bl)]
        pbk[0] += 1
        return ps[b], Tps[b]

    def rms_pow(src_ps, Tsrc, n_feat, width, dst_f, Tdst):
        ts('dve', dst_f, src_ps, 1.0 / n_feat, EPS, ALU.mult, ALU.add, [Tsrc], [Tdst])
        tt('pool', dst_f, dst_f, mhalf[:, 0:1].broadcast_to([128, width]) if width != 128 else mhalf, ALU.pow, [Tdst, Tconst], [Tdst])

    prenorm = {}

    def pre_norm(ci):
        c0, NT = (0, 32) if ci == 0 else (32 + 128 * (ci - 1), 128)
        hh = [Th[ci]]
        Bxn = Buf(); Bsq = Buf(); Brs = Buf()
        xn_m = v3(Bxn.b[:, 0:8 * NT], 8)
        sqh = v3(Bsq.b[:, 0:8 * NT], 8)
        act(sqh, hT[:, :, c0:c0 + NT], AF.Square, hh, [Bsq.t])
        pb, Tpb = bank()
        for c in range(8):
            mm(pb[:, 0:NT], ones_b, sqh[:, c, :], c == 0, c == 7, [Bsq.t, Tconst], [Tpb])
        rstd = Brs.f[:, 0:NT]
        act(rstd, pb[:, 0:NT], AF.Ln, [Tpb], [Brs.t], scale=1.0 / D, bias=EPS)
        act(rstd, rstd, AF.Exp, [Brs.t], [Brs.t], scale=-0.5)
        for c in range(8):
            stt('dve', xn_m[:, c, :], hT[:, c, c0:c0 + NT], nw(1)[:, c:c + 1], rstd, ALU.mult, ALU.mult,
                hh + [Brs.t, Tconst], [Bxn.t])
        Bsq.free(); Brs.free()
        return Bxn, xn_m

    def mixer_chunk(ci):
        first = ci == 0
        first_flag[0] = first
        c0, NT = (0, 32) if first else (32 + 128 * (ci - 1), 128)
        TS_ = 16 if first else 128
        hh = [Th[ci]]
        Bxn, xn_m = prenorm.pop(ci) if ci in prenorm else pre_norm(ci)

        if first:
            chk('m0b')
        def proj(col0, nfc, M=128):
            pb, Tpb = bank()
            wts = [Twin[b] for b in range(col0 // 256, (col0 + (nfc - 1) * 128 + M - 1) // 256 + 1)]
            for k in range(nfc):
                for c in range(8):
                    mm(pb[0:M, k * NT:(k + 1) * NT], win_b[:, c, col0 + k * 128:col0 + k * 128 + M], xn_m[:, c, :],
                       c == 0, c == 7, wts + [Bxn.t], [Tpb])
            return pb, Tpb
        BF_ = Buf(); Bq = Buf(); Bg = Buf(); Bz = Buf()
        tmpF = v3(BF_.f[:, 0:4 * NT], 4); sq = v3(Bq.f[:, 0:4 * NT], 4)
        sg = v3(Bg.f[:, 0:4 * NT], 4); sz = v3(Bz.f[:, 0:4 * NT], 4)
        pb, Tpb = proj(512, 4)
        act(tmpF, v3(pb[:, 0:4 * NT], 4), AF.Tanh, [Tpb], [BF_.t], scale=0.5)
        for half in range(2):
            pb, Tpb = proj(2560 + 512 * half, 4)
            cp('act', raw[:, 4 * half:4 * half + 4, 3:3 + NT], v3(pb[:, 0:4 * NT], 4), [Tpb], [Traw])
        pb, Tpb = proj(0, 4)
        act(sq, v3(pb[:, 0:4 * NT], 4), AF.Silu, [Tpb], [Bq.t])
        pb, Tpb = proj(1536, 4)
        act(sg, v3(pb[:, 0:4 * NT], 4), AF.Silu, [Tpb], [Bg.t])
        pb, Tpb = proj(2048, 4)
        act(sz, v3(pb[:, 0:4 * NT], 4), AF.Silu, [Tpb], [Bz.t])
        if first:
            chk('m0c')
        if first:
            chk('m0d')
        if first:
            chk('m0e')
        Bv = Buf(1024, True)
        pb, Tpb = bank()
        for c in range(8):
            mm(pb[0:NT, :], xn_m[:, c, :], win_b[:, c, 1024:1536], c == 0, c == 7, [Twin[4], Twin[5], Bxn.t], [Tpb])
        v_tok = Bv.b[0:NT, 0:512]
        cp('act', v_tok, pb[0:NT, :], [Tpb], [Bv.t])
        if first:
            Bvf = Buf(2048, True)
            cp('dve', Bvf.f[0:32, :], pb[0:32, :], [Tpb], [Bvf.t])

        Bdt = Buf(1536, True)
        dtT = Bdt.f[0:8, 0:NT]
        pb, Tpb = proj(3584, 1, M=8)
        act(dtT, pb[0:8, 0:NT], AF.Exp, [Tpb, Tconst], [Bdt.t], bias=dtb)
        act(dtT, dtT, AF.Ln, [Bdt.t], [Bdt.t], bias=1.0)
        Bxn.free()
        BmA = Buf(1024); BmB = Buf(1024)
        mixA = v3(BmA.b[:, 0:4 * NT], 4); mixB = v3(BmB.b[:, 0:4 * NT], 4)
        S1 = []
        P.stream = S1
        bank_list[0] = [0, 1, 2, 3]
        BK = Buf(); BB = Buf(); Be = Buf(); Bqk = Buf()
        tmpK = v3(BK.f[:, 0:4 * NT], 4); tmpB = v3(BB.f[:, 0:4 * NT], 4); eb = v3(Be.f[:, 0:4 * NT], 4)
        qt = v3(Bqk.b[:, 0:4 * NT], 4); kt = v3(Bqk.b[:, 4 * NT:8 * NT], 4)
        bc4 = lambda col: col.unsqueeze(2).broadcast_to([128, 4, NT])
        tt('dve', tmpF, tmpF, bc4(c1), ALU.mult, [BF_.t, Tconst], [BF_.t])
        tt('dve', tmpF, tmpF, bc4(c0_), ALU.add, [BF_.t, Tconst], [BF_.t])
        ts('dve', tmpK, tmpF, -1.0, 1.0, ALU.mult, ALU.add, [BF_.t], [BK.t])
        tt('dve', sg, sg, bc4(hgn), ALU.mult, [Bg.t, Tconst], [Bg.t])
        if first:
            Bsg = Buf(512, True)
            fs = v3(Bsg.f[:, 0:64], 4); ks = v3(Bsg.f[:, 64:128], 4)
            cp('pool', fs, tmpF[:, :, 16:32], [BF_.t], [Bsg.t])
            cp('pool', ks, tmpK[:, :, 16:32], [BK.t], [Bsg.t])
        act(tmpF, tmpF, AF.Ln, [BF_.t], [BF_.t])
        rmask = resetA[:, 0:4 * NT] if first else reset
        P.add('dve', lambda e: e.tensor_tensor_scan(out=BB.f[:, 0:4 * NT], data0=rmask, data1=BF_.f[:, 0:4 * NT],
                                                    initial=0.0, op0=ALU.mult, op1=ALU.add), [BF_.t, Tconst], [BB.t])
        act(eb, tmpB, AF.Exp, [BB.t], [Be.t])
        act(tmpF, tmpB, AF.Exp, [BB.t], [BF_.t], scale=-1.0)
        tt('dve', kt, tmpK, tmpF, ALU.mult, [BK.t, BF_.t], [Bqk.t])
        tt('dve', qt, sq, eb, ALU.mult, [Bq.t, Be.t], [Bqk.t])
        BB.free()

        if first:
            po_s, Tpo_s = ps[7], Tps[7]
            BstA = Buf(2048, True)
            for b in range(16):
                if b % 2 == 0:
                    st, Tst = next_stage()
                    st = st[:, 0:512]
                else:
                    st, Tst = BstA.f[:, 0:512], BstA.t
                sv = v3(st, 4)
                ld(sv, s_hg[b].rearrange("h k v -> k h v"), Tst, w=[Tst], q='act')
                pV, TpV = bank()
                mm(pV[:, 0:512], ident_f[0:32, 16 + b:17 + b].broadcast_to([32, 128]), Bvf.f[0:32, :], True, True,
                   [Tconst, Bvf.t], [TpV])
                Bt = Buf(2048, True); Bt2 = Buf(2048, True)
                tt('dve', v3(Bt.f, 4), sv, fs[:, :, b:b + 1].broadcast_to([128, 4, 128]), ALU.mult, [Tst, Bsg.t], [Bt.t])
                tt('dve', v3(Bt2.f, 4), v3(pV, 4), ks[:, :, b:b + 1].broadcast_to([128, 4, 128]), ALU.mult, [TpV, Bsg.t], [Bt2.t])
                tt('dve', st, Bt2.f, Bt.f, ALU.add, [Bt.t, Bt2.t], [Tst])
                for h in range(4):
                    mm(po_s[:, h * 16 + b:h * 16 + b + 1], sv[:, h, :], sq[:, h, 16 + b:17 + b], True, True, [Tst, Bq.t], [Tpo_s])
                stores.append(P.dma('act', (lambda sv=sv, b=b: lambda e: e.dma_start(out=hgs_d[b].rearrange("h k v -> k h v"), in_=sv))(),
                                    Tst, r=[Tst]))
                Bt.free(); Bt2.free()
            Bsg.free(); Bvf.free(); BstA.free()

        subs = [(0, 16, 0)] if first else [(0, 64, 0), (64, 64, 64)]
        for (t0, n, pb0) in subs:
            PR = slice(pb0, pb0 + n)
            pS, TpS = bank()
            for h in range(4):
                mm(pS[PR, h * n:(h + 1) * n], kt[:, h, t0:t0 + n], qt[:, h, t0:t0 + n], True, True, [Bqk.t], [TpS])
            Bp = Buf(512)
            PT = v3(Bp.b[PR, 0:4 * n], 4)
            tt('dve', PT, v3(pS[PR, 0:4 * n], 4), tri_f[PR, pb0:pb0 + n].unsqueeze(1).broadcast_to([n, 4, n]), ALU.mult,
               [TpS, Tconst], [Bp.t])
            pK, TpK = bank()
            pKb = pK.bitcast(BF16)
            for h in range(4):
                tr(pKb[PR, h * 128:(h + 1) * 128], kt[:, h, t0:t0 + n], ident_b, [Bqk.t, Tconst], [TpK])
            Bkt = Buf(1024, True)
            kt_tok = Bkt.b[PR, 0:512]
            cp('act', kt_tok, pKb[PR, 0:512], [TpK], [Bkt.t])
            pO, TpO = bank()
            for h in range(4):
                mm(pO[:, h * n:(h + 1) * n], v_tok[PR, h * 128:(h + 1) * 128], PT[:, h, :], True, False, [Bv.t, Bp.t], [TpO])
                mm(pO[:, h * n:(h + 1) * n], Sbf[:, h, :], qt[:, h, t0:t0 + n], False, True, [TSb, Bqk.t], [TpO])
            pKV, TpKV = bank()
            for h in range(4):
                mm(pKV[:, h * 128:(h + 1) * 128], kt_tok[:, h * 128:(h + 1) * 128], v_tok[PR, h * 128:(h + 1) * 128], True, True,
                   [Bkt.t, Bv.t], [TpKV])
            Bts = Buf(2048, True)
            tmpS = v3(Bts.f, 4)
            tt('dve', tmpS, Sst, v3(pKV, 4), ALU.add, [TS, TpKV], [Bts.t])
            tt('dve', Sst, tmpS, eb[:, :, t0 + n - 1:t0 + n].broadcast_to([128, 4, 128]), ALU.mult, [Bts.t, Be.t], [TS])
            cp('act', Sbf, Sst, [TS], [TSb])
            Bts.free(); Bp.free(); Bkt.free()
            hg_post(pO, TpO, n, lambda h0, h1, _t0=t0, _n=n: sg[:, h0:h1, _t0:_t0 + _n],
                    Bg.t, lambda _t0=t0, _n=n: mixA[:, :, _t0:_t0 + _n], BmA.t)
        if first:
            hg_post(po_s, Tpo_s, 16, lambda h0, h1: sg[:, h0:h1, 16:32], Bg.t, lambda: mixA[:, :, 16:32], BmA.t)
        Bq.free(); Bg.free(); Be.free(); Bqk.free(); Bv.free(); BK.free(); BF_.free()
        S2 = []
        P.stream = S2
        bank_list[0] = [4, 5, 6] if first else [4, 5, 6, 7]
        Bacc = [Buf(), Buf()]
        Bxs = Buf(); Bxb = Buf()
        xs = v3(Bxs.f[:, 0:4 * NT], 4)
        xsb = v3(Bxb.b[:, 0:4 * NT], 4); BC = v3(Bxb.b[:, 4 * NT:8 * NT], 4)
        for half in range(2):
            eng = 'dve'
            acc = v3(Bacc[half].f[:, 0:4 * NT], 4)
            for k in range(4):
                fc = 4 * half + k
                ts(eng, acc[:, k, :], raw[:, fc, 0:NT], cw(0)[:, fc:fc + 1], cbias[:, fc:fc + 1], ALU.mult, ALU.add,
                   [Traw, Tconst], [Bacc[half].t])
                for j in range(1, 4):
                    stt(eng, acc[:, k, :], raw[:, fc, j:j + NT], cw(j)[:, fc:fc + 1], acc[:, k, :], ALU.mult, ALU.add,
                        [Traw, Tconst, Bacc[half].t], [Bacc[half].t])
        if first:
            Bst2 = Buf(4096, True)
            st, Tst = Bst2.f, Bst2.t
            ld(st[0:48, 0:1024], s_conv.rearrange("b j c -> (b j) c"), Tst, w=[Tst])
            stores.append(P.dma('sp', lambda e: e.dma_start(out=convs_d[:, 0:2, :], in_=s_conv[:, 1:3, :]), Tst, r=[Tst]))
            pSC, TpSC = bank()
            for fc in range(8):
                tr(pSC[:, fc * 48:(fc + 1) * 48], st[0:48, fc * 128:(fc + 1) * 128], ident_f[0:48, 0:48], [Tst, Tconst], [TpSC])
            Bsc = Buf(1536, True)
            cp('dve', Bsc.f[:, 0:384], pSC[:, 0:384], [TpSC], [Bsc.t])
            scv = Bsc.f[:, 0:384].rearrange("p (f b j) -> p f b j", f=8, b=16)
            for fc in range(8):
                half, k = fc // 4, fc % 4
                accs = v3(Bacc[half].f[:, 0:4 * NT], 4)[:, k, 16:32]
                ts('dve', accs, raw[:, fc, 3 + 16:3 + 32], cw(3)[:, fc:fc + 1], cbias[:, fc:fc + 1], ALU.mult, ALU.add,
                   [Traw, Tconst, Bacc[half].t], [Bacc[half].t])
                for j in range(3):
                    stt('dve', accs, scv[:, fc, :, j], cw(j)[:, fc:fc + 1], accs, ALU.mult, ALU.add,
                        [Bsc.t, Tconst, Bacc[half].t], [Bacc[half].t])
            Bsc.free()
        act(xs, v3(Bacc[0].f[:, 0:4 * NT], 4), AF.Silu, [Bacc[0].t], [Bxs.t])
        act(BC, v3(Bacc[1].f[:, 0:4 * NT], 4), AF.Silu, [Bacc[1].t], [Bxb.t])
        act(xsb, v3(Bacc[0].f[:, 0:4 * NT], 4), AF.Silu, [Bacc[0].t], [Bxb.t])
        Bacc[0].free(); Bacc[1].free()
        cp('pool', raw[:, :, 0:3], raw[:, :, TS_:TS_ + 3], [Traw], [Traw])

        if first:
            Bd = Buf(512, True); Bys = Buf(512, True); Bxss = Buf(512, True); BstB = Buf(2048, True)
            dtaT = Bdt.f[0:8, 256:272]
            ts('dve', dtaT, dtT[:, 16:32], a_col, None, ALU.mult, ALU.bypass, [Bdt.t, Tconst], [Bdt.t])
            pE, TpE = bank()
            for fc in range(4):
                mm(pE[:, fc * 16:(fc + 1) * 16], E8[:, fc * 128:(fc + 1) * 128], dtT[:, 16:32], True, True, [Tconst, Bdt.t], [TpE])
            for fc in range(4):
                mm(pE[:, 64 + fc * 16:64 + (fc + 1) * 16], E8[:, fc * 128:(fc + 1) * 128], dtaT, True, True, [Tconst, Bdt.t], [TpE])
            dec = v3(Bd.f[:, 0:64], 4); xdts = v3(Bd.f[:, 64:128], 4)
            act(dec, v3(pE[:, 64:128], 4), AF.Exp, [TpE], [Bd.t])
            tt('dve', xdts, xs[:, :, 16:32], v3(pE[:, 0:64], 4), ALU.mult, [Bxs.t, TpE], [Bd.t])
            ysb = v3(Bys.f[:, 0:64], 4)
            xs_s = v3(Bxss.f[:, 0:64], 4)
            cp('pool', xs_s, xs[:, :, 16:32], [Bxs.t], [Bxss.t])
            for b in range(16):
                if b % 2 == 0:
                    st, Tst = Bst2.f[:, 0:512], Bst2.t
                else:
                    st, Tst = BstB.f[:, 0:512], BstB.t
                sv = v3(st, 4)
                ld(sv, s_ssm[b].rearrange("(f hh) q n -> (hh q) f n", hh=2), Tst, w=[Tst], q='act')
                pBC, TpBC = bank()
                for i in range(4):
                    mm(pBC[:, i * 128:(i + 1) * 128], BC[:, i, 16 + b:17 + b].broadcast_to([128, 128]), ident_b, True, True,
                       [Bxb.t, Tconst], [TpBC])
                Bt = Buf(2048, True); Bt2 = Buf(2048, True)
                tt('dve', v3(Bt.f, 4), sv, dec[:, :, b:b + 1].broadcast_to([128, 4, 128]), ALU.mult, [Tst, Bd.t], [Bt.t])
                b4 = lambda ap2: ap2.rearrange("p (g n) -> p g n", g=2).unsqueeze(2).broadcast_to([128, 2, 2, 128])
                f4 = lambda ap: ap.rearrange("p (g i n) -> p g i n", g=2, i=2)
                xcol = xdts[:, :, b:b + 1].rearrange("p (g i) o -> p g i o", g=2).broadcast_to([128, 2, 2, 128])
                tt('dve', f4(Bt2.f), b4(pBC[:, 0:256]), xcol, ALU.mult, [TpBC, Bd.t], [Bt2.t])
                tt('dve', st, Bt2.f, Bt.f, ALU.add, [Bt.t, Bt2.t], [Tst])
                tt('dve', f4(Bt2.f), f4(st), b4(pBC[:, 256:512]), ALU.mult, [Tst, TpBC], [Bt2.t])
                P.add('dve', (lambda b=b, src=v3(Bt2.f, 4): lambda e: e.tensor_reduce(out=ysb[:, :, b], in_=src, axis=mybir.AxisListType.X, op=ALU.add))(),
                      [Bt2.t], [Bys.t])
                stores.append(P.dma('act', (lambda sv=sv, b=b: lambda e: e.dma_start(
                    out=ssms_d[b].rearrange("(f hh) q n -> (hh q) f n", hh=2), in_=sv))(), Tst, r=[Tst]))
                Bt.free(); Bt2.free()
            Bd.free()
            pa, Tpa = bank(); pb2, Tpb2 = bank()
            for fc in range(8):
                pp, Tpp = (pa, Tpa) if fc < 4 else (pb2, Tpb2)
                tr(pp[0:16, (fc % 4) * 128:(fc % 4 + 1) * 128], raw[:, fc, 3 + 16:3 + 32], ident_f, [Traw, Tconst], [Tpp])
            st, Tst = Bst2.f, Bst2.t
            cp('dve', st[0:16, 0:512], pa[0:16, :], [Tpa], [Tst])
            cp('dve', st[0:16, 512:1024], pb2[0:16, :], [Tpb2], [Tst])
            stores.append(P.dma('sp', lambda e: e.dma_start(out=convs_d[:, 2, :], in_=st[0:16, 0:1024]), Tst, r=[Tst]))

        n = TS_
        pD, TpD = bank()
        tr(pD[0:n, 0:8], dtT[:, 0:n], ident_f[0:8, 0:8], [Bdt.t, Tconst], [TpD])
        dt_tok = sm[0:n, 0:8]; dta = sm[0:n, 8:16]; cums = sm[0:n, 16:32]; wdec = sm[0:n, 32:40]; wgt = sm[0:n, 40:48]
        ecl = sm[:, 48:56]
        cp('dve', dt_tok, pD[0:n, 0:8], [TpD], [Tsm])
        tt('dve', dta, dt_tok, a_bc[0:n, :], ALU.mult, [Tsm, Tconst], [Tsm])
        pC, TpC = bank()
        mm(pC[0:n, 0:8], tri_f[0:n, 0:n], dta, True, True, [Tsm, Tconst], [TpC])
        mm(pC[:, 8:16], ones_f[0:n, :], dta, True, True, [Tsm, Tconst], [TpC])
        cp('dve', sm[:, 56:64], pC[:, 8:16], [TpC], [Tsm])
        cp('dve', cums[:, 0:8], pC[0:n, 0:8], [TpC], [Tsm])
        tt('dve', wdec, sm[0:n, 56:64], cums[:, 0:8], ALU.subtract, [Tsm], [Tsm])
        act(wdec, wdec, AF.Exp, [Tsm], [Tsm])
        tt('dve', wgt, wdec, dt_tok, ALU.mult, [Tsm], [Tsm])
        act(ecl, sm[:, 56:64], AF.Exp, [Tsm], [Tsm])
        pR = [bank(), bank()]
        for h in range(8):
            pp, Tpp = pR[h // 4]
            mm(pp[:, (h % 4) * n:(h % 4 + 1) * n], dta[:, h:h + 1].broadcast_to([n, 128]), tri_f[0:n, 0:n], True, True,
               [Tsm, Tconst], [Tpp])
        BE = Buf(); Bec = Buf(); BD = [Buf(), Buf()]
        E = v3(BE.b[0:n, 0:8 * n], 8); ecum = v3(Bec.b[:, 0:8 * n], 8)
        for g in range(2):
            pp, Tpp = pR[g]
            Dm = v3(BD[g].f[0:n, 0:4 * n], 4)
            tt('dve', Dm, v3(pp[0:n, 0:4 * n], 4), cums[:, 4 * g:4 * g + 4].unsqueeze(2).broadcast_to([n, 4, n]), ALU.subtract,
               [Tpp, Tsm], [BD[g].t])
            ts('dve', Dm, Dm, 0.0, None, ALU.min, ALU.bypass, [BD[g].t], [BD[g].t])
            act(E[:, 4 * g:4 * g + 4, :], Dm, AF.Exp, [BD[g].t], [BE.t])
            act(ecum[:, 4 * g:4 * g + 4, :], v3(pp[:, 0:4 * n], 4), AF.Exp, [Tpp], [Bec.t])
        BD[0].free(); BD[1].free()
        pCB, TpCB = bank()
        for g in range(2):
            mm(pCB[0:n, g * n:(g + 1) * n], BC[:, g, 0:n], BC[:, 2 + g, 0:n], True, True, [Bxb.t], [TpCB])
        Bcb = Buf(1024)
        CBm = v3(Bcb.f[0:n, 0:2 * n], 2)
        tt('dve', CBm, v3(pCB[0:n, 0:2 * n], 2), tri_f[0:n, 0:n].unsqueeze(1).broadcast_to([n, 2, n]), ALU.mult, [TpCB, Tconst], [Bcb.t])
        BW = Buf(); BCs = Buf()
        Wm = v3(BW.b[0:n, 0:8 * n], 8); Cs = v3(BCs.b[:, 0:8 * n], 8)
        for g in range(2):
            tt('dve', Wm[:, 4 * g:4 * g + 4, :], E[:, 4 * g:4 * g + 4, :], CBm[:, g:g + 1, :].broadcast_to([n, 4, n]),
               ALU.mult, [BE.t, Bcb.t], [BW.t])
            tt('dve', Cs[:, 4 * g:4 * g + 4, :], ecum[:, 4 * g:4 * g + 4, :], BC[:, 2 + g:3 + g, 0:n].broadcast_to([128, 4, n]),
               ALU.mult, [Bec.t, Bxb.t], [BCs.t])
        BE.free(); Bec.free(); Bcb.free()
        pX, TpX = bank()
        pXb = pX.bitcast(BF16)
        for k in range(4):
            tr(pXb[0:n, k * 128:(k + 1) * 128], xsb[:, k, 0:n], ident_b, [Bxb.t, Tconst], [TpX])
        for g in range(2):
            tr(pXb[0:n, 512 + g * 128:512 + (g + 1) * 128], BC[:, g, 0:n], ident_b, [Bxb.t, Tconst], [TpX])
        Bx = Buf(2048, True)
        xdt = Bx.b[0:n, 0:512]; xw = Bx.b[0:n, 512:1024]
        Bbt = Buf(512, True)
        Btok = Bbt.b[0:n, 0:256]
        tt('dve', xdt.rearrange("p (h q) -> p h q", h=8), pXb[0:n, 0:512].rearrange("p (h q) -> p h q", h=8),
           dt_tok.unsqueeze(2).broadcast_to([n, 8, 64]), ALU.mult, [TpX, Tsm], [Bx.t])
        tt('dve', xw.rearrange("p (h q) -> p h q", h=8), pXb[0:n, 0:512].rearrange("p (h q) -> p h q", h=8),
           wgt.unsqueeze(2).broadcast_to([n, 8, 64]), ALU.mult, [TpX, Tsm], [Bx.t])
        cp('act', Btok, pXb[0:n, 512:768], [TpX], [Bbt.t])
        pY, TpY = bank()
        for h in range(8):
            outp = pY[(h % 2) * 64:(h % 2) * 64 + 64, (h // 2) * n:(h // 2 + 1) * n]
            mm(outp, xdt[:, h * 64:(h + 1) * 64], Wm[:, h, :], True, False, [Bx.t, BW.t], [TpY])
            mm(outp, Hbf[:, h * 64:(h + 1) * 64], Cs[:, h, :], False, True, [THb, BCs.t], [TpY])
        pH, TpH = bank()
        for g in range(2):
            mm(pH[:, g * 256:(g + 1) * 256], Btok[:, g * 128:(g + 1) * 128], xw[:, g * 256:(g + 1) * 256], True, True,
               [Bbt.t, Bx.t], [TpH])
        Bth = Buf(2048, True)
        tt('dve', Bth.f.rearrange("p (h q) -> p h q", h=8), Hst.rearrange("p (h q) -> p h q", h=8),
           ecl.unsqueeze(2).broadcast_to([128, 8, 64]), ALU.mult, [TH, Tsm], [Bth.t])
        tt('dve', Hst, Bth.f, pH, ALU.add, [Bth.t, TpH], [TH])
        cp('act', Hbf, Hst, [TH], [THb])
        Bth.free(); BW.free(); BCs.free(); Bx.free(); Bbt.free()
        ssd_post(pY, TpY, n, lambda: xs[:, :, 0:n], Bxs.t, lambda: sz[:, :, 0:n], Bz.t, lambda: mixB[:, :, 0:n], BmB.t)
        if first:
            ssd_post(ysb.rearrange("p a b -> p (a b)"), Bys.t, 16, lambda: xs_s, Bxss.t, lambda: sz[:, :, 16:32], Bz.t,
                     lambda: mixB[:, :, 16:32], BmB.t, in_sbuf=True)
            Bys.free(); Bxss.free(); Bst2.free(); BstB.free()
        Bxs.free(); Bxb.free(); Bdt.free(); Bz.free()

        P.stream = None
        P.flush([S1, S2])
        end_section()
        bank_list[0] = list(range(8))
        if first:
            load_wout()
        if ci + 1 < 17:
            first_flag[0] = False
            prenorm[ci + 1] = pre_norm(ci + 1)
        for half in range(2):
            pb, Tpb = bank()
            for k in range(4):
                m = 4 * half + k
                for fc in range(8):
                    mm(pb[:, k * NT:(k + 1) * NT], wout_b[:, fc, m * 128:(m + 1) * 128], (mixA if fc < 4 else mixB)[:, fc % 4, :], fc == 0, fc == 7,
                       [Twout, BmA.t, BmB.t], [Tpb])
            tt('dve', hT[:, 4 * half:4 * half + 4, c0:c0 + NT], hT[:, 4 * half:4 * half + 4, c0:c0 + NT],
               v3(pb[:, 0:4 * NT], 4), ALU.add, hh + [Tpb], hh)
        BmA.free(); BmB.free()

    def hg_post(pO, TpO, n, sg_f, Tsg, out_f, Tout):
        B1 = Buf(1024); B2 = Buf(1024)
        sqo = B2.b[:, 0:4 * n]
        act(sqo, pO[:, 0:4 * n], AF.Square, [TpO], [B2.t])
        pss, Tpss = bank()
        mm(pss[:, 0:4 * n], ones_b, sqo, True, True, [B2.t, Tconst], [Tpss])
        rs = B2.f[:, 0:4 * n]
        act(rs, pss[:, 0:4 * n], AF.Ln, [Tpss], [B2.t], scale=1.0 / 128, bias=EPS)
        act(rs, rs, AF.Exp, [B2.t], [B2.t], scale=-0.5)
        t1 = v3(B1.f[:, 0:4 * n], 4)
        tt('dve', t1, v3(pO[:, 0:4 * n], 4), sg_f(0, 4), ALU.mult, [TpO, Tsg, B1.t], [B1.t])
        tt('dve', out_f(), t1, v3(rs, 4), ALU.mult, [B1.t, B2.t], [Tout])
        B1.free(); B2.free()

    def ssd_post(pY, TpY, n, xs_f, Txs, sz_f, Tsz, out_f, Tout, in_sbuf=False):
        B1 = Buf(); B2 = Buf(); B3 = Buf(1024)
        yv = v3(B1.f[:, 0:4 * n], 4)
        tt('dve', yv, xs_f(), Dexp.unsqueeze(2).broadcast_to([128, 4, n]), ALU.mult, [Txs, Tconst], [B1.t])
        tt('dve', yv, yv, v3(pY[:, 0:4 * n], 4), ALU.add, [B1.t, TpY], [B1.t])
        tt('dve', yv, yv, sz_f(), ALU.mult, [B1.t, Tsz], [B1.t])
        sqy = v3(B2.b[:, 0:4 * n], 4)
        act(sqy, yv, AF.Square, [B1.t], [B2.t])
        pss, Tpss = bank()
        for g in range(2):
            for i in range(2):
                mm(pss[:, g * n:(g + 1) * n], ones_b, sqy[:, 2 * g + i, :], i == 0, i == 1, [B2.t, Tconst], [Tpss])
        rs = B3.f[:, 0:2 * n]
        act(rs, pss[:, 0:2 * n], AF.Ln, [Tpss], [B3.t], scale=1.0 / 256, bias=EPS)
        act(rs, rs, AF.Exp, [B3.t], [B3.t], scale=-0.5)
        tt('dve', yv, yv, snrm.unsqueeze(2).broadcast_to([128, 4, n]), ALU.mult, [B1.t, Tconst], [B1.t])
        for g in range(2):
            tt('dve', out_f()[:, 2 * g:2 * g + 2, :], yv[:, 2 * g:2 * g + 2, :],
               rs[:, g * n:(g + 1) * n].unsqueeze(1).broadcast_to([128, 2, n]), ALU.mult, [B1.t, B3.t], [Tout])
        B1.free(); B2.free(); B3.free()

    S_CTX = {}

    for ci in range(17):
        mixer_chunk(ci)
        chk('mix%d' % ci)
    stores.append(P.dma('sp', lambda e: e.dma_start(out=hgp_d.rearrange("h k v -> k h v"), in_=Sst), TS, r=[TS]))
    pb, Tpb = bank()
    for fc in range(4):
        tr(pb[:, fc * 128:(fc + 1) * 128], Hst[:, fc * 128:(fc + 1) * 128], ident_f, [TH, Tconst], [Tpb])
    st, Tst = next_stage()
    cp('dve', st[:, 0:512], pb, [Tpb], [Tst])
    stores.append(P.dma('sp', (lambda st=st: lambda e: e.dma_start(out=ssmp_d.rearrange("(f hh) q n -> (hh q) f n", hh=2),
                                                               in_=v3(st[:, 0:512], 4)))(), Tst, r=[Tst]))
    pa, Tpa = bank(); pb2, Tpb2 = bank()
    for fc in range(8):
        pp, Tpp = (pa, Tpa) if fc < 4 else (pb2, Tpb2)
        tr(pp[0:3, (fc % 4) * 128:(fc % 4 + 1) * 128], raw[:, fc, 0:3], ident_f, [Traw, Tconst], [Tpp])
    st2, Tst2 = next_stage()
    cp('dve', st2[0:3, 0:512], pa[0:3, :], [Tpa], [Tst2])
    cp('dve', st2[0:3, 512:1024], pb2[0:3, :], [Tpb2], [Tst2])
    stores.append(P.dma('sp', (lambda st2=st2: lambda e: e.dma_start(out=convp_d, in_=st2[0:3, 0:1024]))(), Tst2, r=[Tst2]))
    P.barrier()
    stg_only0[0] = False
    chk('mixer')
    fstate = {}

    def final_post(ti, off):
        if 'b' not in fstate:
            fb = []
            for i in range(3):
                a = sbat(off, [128, 8, 128], BF16); off += 2048
                b_ = sbat(off, [128, 128], F32); off += 512
                c_ = sbat(off, [128, 8, 128], F32); off += 4096
                fb.append((a, b_, c_, T("fsq%d" % i), T("frs%d" % i), T("fy%d" % i)))
            assert off <= TOTAL, off
            fstate['b'] = fb
            fstate['k'] = 0
        bank_list[0] = [0, 1, 2, 3, 6, 7]

        def stage_a(ci):
            r0, n = xrows[ci]
            hh = [Th[ci]]
            fsq, frs, fy, Tfsq, Tfrs, Tfy = fstate['b'][ci % 3]
            act(fsq[:, :, 0:n], hT[:, :, r0:r0 + n], AF.Square, hh, [Tfsq])
            pb, Tpb = bank()
            for c in range(8):
                mm(pb[:, 0:n], ones_b, fsq[:, c, 0:n], c == 0, c == 7, [Tfsq, Tconst], [Tpb])
            act(frs[:, 0:n], pb[:, 0:n], AF.Ln, [Tpb], [Tfrs], scale=1.0 / D, bias=EPS)
            act(frs[:, 0:n], frs[:, 0:n], AF.Exp, [Tfrs], [Tfrs], scale=-0.5)
            for c in range(8):
                stt('dve', fy[:, c, 0:n], hT[:, c, r0:r0 + n], nw(3)[:, c:c + 1], frs[:, 0:n], ALU.mult, ALU.mult,
                    hh + [Tfrs, Tconst], [Tfy])

        def stage_b(ci):
            r0, n = xrows[ci]
            fsq, frs, fy, Tfsq, Tfrs, Tfy = fstate['b'][ci % 3]
            st, Tst = next_stage()
            for half in range(2):
                pb, Tpb = bank()
                for k in range(4):
                    tr(pb[0:n, k * 128:(k + 1) * 128], fy[:, 4 * half + k, 0:n], ident_f, [Tfy, Tconst], [Tpb])
                cp('act' if half == 0 else 'dve', st[0:n, 512 * half:512 * half + 512], pb[0:n, :], [Tpb], [Tst])
            stores.append(P.dma('sp', (lambda st=st, r0=r0, n=n: lambda e: e.dma_start(out=y_d[r0:r0 + n, :], in_=st[0:n, 0:1024]))(),
                                Tst, r=[Tst]))

        L = TILE_CHUNKS[ti]
        stage_a(L[0])
        for k in range(1, len(L)):
            stage_a(L[k])
            stage_b(L[k - 1])
        stage_b(L[-1])
        bank_list[0] = list(range(8))

    ffn(2, w2g, w2u, w2d, post=final_post)


_NC_CACHE = {}


def kernel(x_prompt, x_sample, state_hgrn, state_ssm, state_conv, meta_tokens, lb_logits,
           norm_ffn1, w_ffn1_gate, w_ffn1_up, w_ffn1_down, norm_mix, w_in, hg_norm, conv_w, conv_b,
           dt_bias, a_log, d_skip, ssm_norm, w_out, norm_ffn2, w_ffn2_gate, w_ffn2_up, w_ffn2_down,
           norm_final):
    f = lambda a: np.ascontiguousarray(np.asarray(a, dtype=np.float32))
    x_prompt = f(x_prompt); x_sample = f(x_sample)
    state_hgrn = f(state_hgrn)[0]; state_ssm = f(state_ssm)[0]; state_conv = f(state_conv)[0]
    meta = f(meta_tokens)
    vec = np.zeros((128, 104), np.float32)
    pc = lambda v: f(v).reshape(-1, 128).T
    vec[:, 0:8] = pc(norm_ffn1[0]); vec[:, 8:16] = pc(norm_mix[0]); vec[:, 16:24] = pc(norm_ffn2[0]); vec[:, 24:32] = pc(norm_final)
    vec[:, 32:36] = pc(hg_norm[0]); vec[:, 36:40] = pc(ssm_norm[0])
    cwv = f(conv_w)[0]
    for j in range(4):
        vec[:, 40 + 8 * j:48 + 8 * j] = pc(cwv[j])
    vec[:, 72:80] = pc(conv_b[0])
    lbl = f(lb_logits)
    vec[:, 80:84] = pc(lbl[0]); vec[:, 84:88] = pc(lbl[1])
    vec[:, 88:92] = pc(np.repeat(f(d_skip)[0], 64))
    vec[:, 92:100] = np.broadcast_to(f(a_log)[0][None, :], (128, 8))
    vec[0:8, 100] = f(dt_bias)[0]
    vec[0:8, 101] = f(a_log)[0]
    cst = np.zeros((128, 1664), np.float32)
    cst[:, 0:128] = np.eye(128, dtype=np.float32)
    cst[:, 128:256] = np.triu(np.ones((128, 128), np.float32))
    cst[:, 256:384] = 1.0
    r = np.ones(512, np.float32); r[::64] = 0.0
    cst[:, 384:896] = r[None, :]
    ra = np.ones(128, np.float32); ra[::32] = 0.0
    cst[:, 896:1024] = ra[None, :]
    for h in range(8):
        cst[h, 1024 + 64 * h:1024 + 64 * (h + 1)] = 1.0
    cst[:, 1536:1664] = -0.5
    if 'nc' not in _NC_CACHE:
        _NC_CACHE['nc'] = build_nc()
    nc = _NC_CACHE['nc']
    shared = dict(w1g=f(w_ffn1_gate)[0], w1u=f(w_ffn1_up)[0], w1d=f(w_ffn1_down)[0],
                  w2g=f(w_ffn2_gate)[0], w2u=f(w_ffn2_up)[0], w2d=f(w_ffn2_down)[0],
                  w_in=f(w_in)[0], w_out=f(w_out)[0], vecs=vec, consts=cst)
    in_maps = []
    for c in range(8):
        xs_ = x_sample[16 * c:16 * c + 16, 0, :]
        m = dict(shared)
        m['xin'] = np.ascontiguousarray(np.concatenate([meta, xs_, x_prompt[c]], axis=0))
        m['s_hg'] = np.ascontiguousarray(state_hgrn[16 * c:16 * c + 16])
        m['s_ssm'] = np.ascontiguousarray(state_ssm[16 * c:16 * c + 16])
        m['s_conv'] = np.ascontiguousarray(state_conv[16 * c:16 * c + 16])
        in_maps.append(m)
    res = run_bass_kernel_spmd(nc, in_maps, core_ids=list(range(8)))
    R = res.results
    y_prompt = np.stack([R[c]['y'][32:] for c in range(8)], 0)
    y_sample = np.concatenate([R[c]['y'][16:32] for c in range(8)], 0)[:, None, :]
    hgrn_prompt = np.stack([R[c]['hg_p'] for c in range(8)], 0)[None]
    ssm_prompt = np.stack([R[c]['ssm_p'] for c in range(8)], 0)[None]
    conv_prompt = np.stack([R[c]['conv_p'] for c in range(8)], 0)[None]
    hgrn_sample = np.concatenate([R[c]['hg_s'] for c in range(8)], 0)[None]
    ssm_sample = np.concatenate([R[c]['ssm_s'] for c in range(8)], 0)[None]
    conv_sample = np.concatenate([R[c]['conv_s'] for c in range(8)], 0)[None]
    return tuple(np.ascontiguousarray(a, dtype=np.float32) for a in
                 (y_prompt, y_sample, hgrn_prompt, ssm_prompt, conv_prompt, hgrn_sample, ssm_sample, conv_sample))
```
